# Optimizing a Trainium2 kernel written in Bass

```python
import math
import jax, jax.numpy as jnp
from jax import lax
import numpy as np

D_MODEL = 2048
BATCH = 4
SEQ = 2048
DEPTH = 1
DEC_BATCH = 128
DEC_SEQ = 8
PAST_LEN = 16384
PAGE_SIZE = 128

D_CONV = D_MODEL // 2
D_SSM = D_MODEL - D_CONV
CONV_WIDTH = 31
SSM_GROUP_CH = 16
N_SSM_GROUPS = D_SSM // SSM_GROUP_CH
SSM_STATE = 64
D_FF = ((8 * D_MODEL // 3 + 255) // 256) * 256
D_IN = 2 * D_CONV + D_SSM
EPS = 1e-6
DT_MIN = 1e-3
DT_MAX = 1e-1

kernel_name = "hymba_conformer_conv_s5_decode_step"


def rmsnorm(x, g):
    xf = x.astype(jnp.float32)
    y = xf * lax.rsqrt(jnp.mean(xf * xf, axis=-1, keepdims=True) + EPS) * g.astype(jnp.float32)
    return y.astype(x.dtype)


def conformer_conv(v, gate, conv_buf, conv_w, conv_b, ln_g, ln_b):
    u = v * jax.nn.sigmoid(gate)
    up = jnp.concatenate([conv_buf.astype(u.dtype), u], axis=1)
    y = lax.conv_general_dilated(
        up, conv_w.astype(u.dtype)[:, None, :], window_strides=(1,), padding='VALID',
        dimension_numbers=('NWC', 'WIO', 'NWC'), feature_group_count=D_CONV)
    y = y.astype(jnp.float32) + conv_b.astype(jnp.float32)
    mu = jnp.mean(y, axis=-1, keepdims=True)
    yc = y - mu
    var = jnp.mean(yc * yc, axis=-1, keepdims=True)
    y = yc * lax.rsqrt(var + EPS) * ln_g.astype(jnp.float32) + ln_b.astype(jnp.float32)
    y = jax.nn.silu(y)
    new_buf = up[:, -(CONV_WIDTH - 1):]
    return y.astype(v.dtype), new_buf


def _complex_scan_op(e1, e2):
    a1r, a1i, b1r, b1i = e1
    a2r, a2i, b2r, b2i = e2
    ar = a2r * a1r - a2i * a1i
    ai = a2r * a1i + a2i * a1r
    br = a2r * b1r - a2i * b1i + b2r
    bi = a2r * b1i + a2i * b1r + b2i
    return (ar, ai, br, bi)


def s5_mixer(u, h_re, h_im, A_re, A_im, log_dt, B_re, B_im, C_re, C_im, D, w_glu):
    b, l, _ = u.shape
    f32 = jnp.float32
    uf = u.astype(f32).reshape(b, l, N_SSM_GROUPS, SSM_GROUP_CH)
    Ar = A_re.astype(f32); Ai = A_im.astype(f32)
    dt = jnp.exp(log_dt.astype(f32))[:, None]
    mag = jnp.exp(dt * Ar)
    ab_re = mag * jnp.cos(dt * Ai)
    ab_im = mag * jnp.sin(dt * Ai)
    den = Ar * Ar + Ai * Ai
    nr = ab_re - 1.0
    ni = ab_im
    k_re = (nr * Ar + ni * Ai) / den
    k_im = (ni * Ar - nr * Ai) / den
    Br = B_re.astype(f32); Bi = B_im.astype(f32)
    Bb_re = k_re[..., None] * Br - k_im[..., None] * Bi
    Bb_im = k_re[..., None] * Bi + k_im[..., None] * Br
    bu_re = jnp.einsum('blgc,gpc->blgp', uf, Bb_re)
    bu_im = jnp.einsum('blgc,gpc->blgp', uf, Bb_im)
    hr = h_re.astype(f32); hi = h_im.astype(f32)
    bu_re = bu_re.at[:, 0].add(ab_re * hr - ab_im * hi)
    bu_im = bu_im.at[:, 0].add(ab_re * hi + ab_im * hr)
    a_re = jnp.broadcast_to(ab_re, bu_re.shape)
    a_im = jnp.broadcast_to(ab_im, bu_re.shape)
    _, _, hs_re, hs_im = lax.associative_scan(_complex_scan_op, (a_re, a_im, bu_re, bu_im), axis=1)
    y = (jnp.einsum('blgp,gcp->blgc', hs_re, C_re.astype(f32))
         - jnp.einsum('blgp,gcp->blgc', hs_im, C_im.astype(f32)))
    y = y + D.astype(f32).reshape(N_SSM_GROUPS, SSM_GROUP_CH) * uf
    y = jax.nn.gelu(y.reshape(b, l, D_SSM))
    y = y * jax.nn.sigmoid(y @ w_glu.astype(f32))
    return y.astype(u.dtype), hs_re[:, -1], hs_im[:, -1]


def hybrid_layer(x, conv_buf, h_re, h_im, norm_mix, w_in, conv_w, conv_b, conv_ln_g, conv_ln_b,
                 ssm_A_re, ssm_A_im, ssm_log_dt, ssm_B_re, ssm_B_im, ssm_C_re, ssm_C_im, ssm_D,
                 w_glu, gnorm_conv, gnorm_ssm, w_out, norm_ffn, w_ffn_gate, w_ffn_up, w_ffn_down):
    xn = rmsnorm(x, norm_mix)
    proj = xn @ w_in
    c_val = proj[..., :D_CONV]
    c_gate = proj[..., D_CONV:2 * D_CONV]
    s_in = proj[..., 2 * D_CONV:]
    yc, new_buf = conformer_conv(c_val, c_gate, conv_buf, conv_w, conv_b, conv_ln_g, conv_ln_b)
    ys, nh_re, nh_im = s5_mixer(s_in, h_re, h_im, ssm_A_re, ssm_A_im, ssm_log_dt,
                                ssm_B_re, ssm_B_im, ssm_C_re, ssm_C_im, ssm_D, w_glu)
    mix = jnp.concatenate([rmsnorm(yc, gnorm_conv), rmsnorm(ys, gnorm_ssm)], axis=-1) @ w_out
    h = x + mix.astype(x.dtype)
    hn = rmsnorm(h, norm_ffn)
    ff = (jax.nn.silu(hn @ w_ffn_gate) * (hn @ w_ffn_up)) @ w_ffn_down
    h = h + ff.astype(x.dtype)
    return h, new_buf, nh_re, nh_im


def setup_inputs(seed: int = 0) -> dict:
    key = jax.random.key(seed)
    ks = jax.random.split(key, 32)
    f32 = jnp.float32
    nrm = lambda k, shape, s: jax.random.normal(k, shape, f32) * s
    L = DEPTH
    x_prompt = nrm(ks[0], (BATCH, SEQ, D_MODEL), 1.0)
    x_sample = nrm(ks[1], (DEC_BATCH, DEC_SEQ, D_MODEL), 1.0)
    state_conv = nrm(ks[2], (L, DEC_BATCH, CONV_WIDTH - 1, D_CONV), 0.5)
    state_ssm_re = nrm(ks[3], (L, DEC_BATCH, N_SSM_GROUPS, SSM_STATE), 0.5)
    state_ssm_im = nrm(ks[4], (L, DEC_BATCH, N_SSM_GROUPS, SSM_STATE), 0.5)
    norm_mix = 1.0 + nrm(ks[5], (L, D_MODEL), 0.02)
    w_in = nrm(ks[6], (L, D_MODEL, D_IN), D_MODEL ** -0.5)
    conv_w = nrm(ks[7], (L, CONV_WIDTH, D_CONV), CONV_WIDTH ** -0.5)
    conv_b = nrm(ks[8], (L, D_CONV), 0.02)
    conv_ln_g = 1.0 + nrm(ks[9], (L, D_CONV), 0.02)
    conv_ln_b = nrm(ks[10], (L, D_CONV), 0.02)
    n = jnp.arange(SSM_STATE, dtype=f32)
    ssm_A_re = -0.5 * jnp.exp(nrm(ks[11], (L, N_SSM_GROUPS, SSM_STATE), 0.01))
    ssm_A_im = math.pi * n + nrm(ks[12], (L, N_SSM_GROUPS, SSM_STATE), 0.01)
    ssm_log_dt = jax.random.uniform(ks[13], (L, N_SSM_GROUPS), f32,
                                    math.log(DT_MIN), math.log(DT_MAX))
    ssm_B_re = nrm(ks[14], (L, N_SSM_GROUPS, SSM_STATE, SSM_GROUP_CH), (2 * SSM_GROUP_CH) ** -0.5)
    ssm_B_im = nrm(ks[15], (L, N_SSM_GROUPS, SSM_STATE, SSM_GROUP_CH), (2 * SSM_GROUP_CH) ** -0.5)
    ssm_C_re = nrm(ks[16], (L, N_SSM_GROUPS, SSM_GROUP_CH, SSM_STATE), (2 * SSM_STATE) ** -0.5)
    ssm_C_im = nrm(ks[17], (L, N_SSM_GROUPS, SSM_GROUP_CH, SSM_STATE), (2 * SSM_STATE) ** -0.5)
    ssm_D = nrm(ks[18], (L, D_SSM), 1.0)
    w_glu = nrm(ks[19], (L, D_SSM, D_SSM), D_SSM ** -0.5)
    gnorm_conv = 1.0 + nrm(ks[20], (L, D_CONV), 0.02)
    gnorm_ssm = 1.0 + nrm(ks[21], (L, D_SSM), 0.02)
    w_out = nrm(ks[22], (L, D_MODEL, D_MODEL), D_MODEL ** -0.5)
    norm_ffn = 1.0 + nrm(ks[23], (L, D_MODEL), 0.02)
    w_ffn_gate = nrm(ks[24], (L, D_MODEL, D_FF), D_MODEL ** -0.5)
    w_ffn_up = nrm(ks[25], (L, D_MODEL, D_FF), D_MODEL ** -0.5)
    w_ffn_down = nrm(ks[26], (L, D_FF, D_MODEL), D_FF ** -0.5)
    norm_final = 1.0 + nrm(ks[27], (D_MODEL,), 0.02)
    return {"x_prompt": x_prompt, "x_sample": x_sample, "state_conv": state_conv,
            "state_ssm_re": state_ssm_re, "state_ssm_im": state_ssm_im,
            "norm_mix": norm_mix, "w_in": w_in, "conv_w": conv_w, "conv_b": conv_b,
            "conv_ln_g": conv_ln_g, "conv_ln_b": conv_ln_b, "ssm_A_re": ssm_A_re,
            "ssm_A_im": ssm_A_im, "ssm_log_dt": ssm_log_dt, "ssm_B_re": ssm_B_re,
            "ssm_B_im": ssm_B_im, "ssm_C_re": ssm_C_re, "ssm_C_im": ssm_C_im, "ssm_D": ssm_D,
            "w_glu": w_glu, "gnorm_conv": gnorm_conv, "gnorm_ssm": gnorm_ssm, "w_out": w_out,
            "norm_ffn": norm_ffn, "w_ffn_gate": w_ffn_gate, "w_ffn_up": w_ffn_up,
            "w_ffn_down": w_ffn_down, "norm_final": norm_final}


def reference(x_prompt, x_sample, state_conv, state_ssm_re, state_ssm_im,
              norm_mix, w_in, conv_w, conv_b, conv_ln_g, conv_ln_b,
              ssm_A_re, ssm_A_im, ssm_log_dt, ssm_B_re, ssm_B_im, ssm_C_re, ssm_C_im, ssm_D,
              w_glu, gnorm_conv, gnorm_ssm, w_out, norm_ffn, w_ffn_gate, w_ffn_up, w_ffn_down,
              norm_final):
    hp = x_prompt
    hs = x_sample
    pc, pr, pi_, sc, sr, si = [], [], [], [], [], []
    for d in range(DEPTH):
        params = (norm_mix[d], w_in[d], conv_w[d], conv_b[d], conv_ln_g[d], conv_ln_b[d],
                  ssm_A_re[d], ssm_A_im[d], ssm_log_dt[d], ssm_B_re[d], ssm_B_im[d],
                  ssm_C_re[d], ssm_C_im[d], ssm_D[d], w_glu[d], gnorm_conv[d], gnorm_ssm[d],
                  w_out[d], norm_ffn[d], w_ffn_gate[d], w_ffn_up[d], w_ffn_down[d])
        b = hp.shape[0]
        buf0 = jnp.zeros((b, CONV_WIDTH - 1, D_CONV), hp.dtype)
        h0 = jnp.zeros((b, N_SSM_GROUPS, SSM_STATE), jnp.float32)
        hp, nbuf, nr, ni = hybrid_layer(hp, buf0, h0, h0, *params)
        pc.append(nbuf); pr.append(nr); pi_.append(ni)
        hs, nbuf, nr, ni = hybrid_layer(hs, state_conv[d], state_ssm_re[d], state_ssm_im[d], *params)
        sc.append(nbuf); sr.append(nr); si.append(ni)
    y_prompt = rmsnorm(hp, norm_final)
    y_sample = rmsnorm(hs, norm_final)
    new_conv_prompt = jnp.stack(pc, axis=0)
    new_ssm_re_prompt = jnp.stack(pr, axis=0)
    new_ssm_im_prompt = jnp.stack(pi_, axis=0)
    new_conv_sample = jnp.stack(sc, axis=0)
    new_ssm_re_sample = jnp.stack(sr, axis=0)
    new_ssm_im_sample = jnp.stack(si, axis=0)
    return (y_prompt, y_sample, new_conv_prompt, new_ssm_re_prompt, new_ssm_im_prompt,
            new_conv_sample, new_ssm_re_sample, new_ssm_im_sample)
```

```python
import numpy as np
from contextlib import ExitStack
import concourse.bass as bass
import concourse.mybir as mybir
from concourse.bass_utils import run_bass_kernel_spmd

F32 = mybir.dt.float32
BF16 = mybir.dt.bfloat16
AF = mybir.ActivationFunctionType
ALU = mybir.AluOpType

D = 2048
DC = 1024
DFF = 5632
EPS = 1e-6
NCH = 44
SEM_CH = 12000
TWO_PI = 6.283185307179586
GC1 = 1.5957691216057308
GC2 = GC1 * 0.044715


class KB:
    def __init__(self, nc, es):
        self.nc = nc
        self.es = es
        self.eng = {}
        for name, h in (("pe", nc.tensor), ("act", nc.scalar), ("dve", nc.vector), ("pool", nc.gpsimd), ("sp", nc.sync)):
            self.eng[name] = dict(h=h, n=0, seen={}, sems=[])
        self.sems = {}
        self.lastw = {}
        self.readers = {}
        self.dmas = {}
        self.psn = 0
        self.dead = False
        self.reserved = set()
        self.fence = None

    def sem(self, name):
        if name not in self.sems:
            self.sems[name] = self.es.enter_context(self.nc.semaphore(name))
        return self.sems[name]

    def _wait(self, en, tok):
        e = self.eng[en]
        sname, val, pen, pidx = tok
        if pen == en:
            if en in ("pe", "sp"):
                return
            if e["n"] - pidx > 3:
                return
        if pen == "dma":
            val = self.dmas[sname]
        if e["seen"].get(sname, 0) >= val:
            return
        e["h"].wait_ge(self.sems[sname], val)
        e["seen"][sname] = val

    def _deps(self, en, R, W):
        toks = {}
        for k in R:
            t = self.lastw.get(k)
            if t is not None:
                toks[(t[0], t[2])] = max(toks.get((t[0], t[2]), (0, 0)), (t[1], t[3] if t[3] is not None else 0))
        for k in W:
            t = self.lastw.get(k)
            if t is not None:
                toks[(t[0], t[2])] = max(toks.get((t[0], t[2]), (0, 0)), (t[1], t[3] if t[3] is not None else 0))
            for t in self.readers.get(k, {}).values():
                toks[(t[0], t[2])] = max(toks.get((t[0], t[2]), (0, 0)), (t[1], t[3] if t[3] is not None else 0))
        for (sname, pen), (val, pidx) in toks.items():
            self._wait(en, (sname, val, pen, pidx))

    def _record(self, tok, R, W):
        for k in W:
            self.lastw[k] = tok
            self.readers[k] = {}
        for k in R:
            d = self.readers.setdefault(k, {})
            old = d.get(tok[0])
            if old is None or old[1] < tok[1]:
                d[tok[0]] = tok

    def op(self, en, fn, R=(), W=()):
        if self.dead:
            return None
        e = self.eng[en]
        self._deps(en, R, W)
        ins = fn(e["h"])
        si = e["n"] // SEM_CH
        sname = f"s_{en}{si}"
        ins.then_inc(self.sem(sname), 1)
        val = e["n"] % SEM_CH + 1
        tok = (sname, val, en, e["n"])
        e["n"] += 1
        self._record(tok, R, W)
        return tok

    def dma(self, qn, out, in_, R, W, key):
        if self.dead:
            return None
        e = self.eng[qn]
        fk = []
        for k in R:
            t = self.lastw.get(k)
            if t is not None and t[2] in ("act", "dve", "pool") and self.fence is not None:
                en_ = t[2]
                f = self.fence
                if en_ == "act":
                    self.op(en_, lambda h: h.copy(out=f[:, 0:1], in_=f[:, 1:2]), R=[k], W=["fence_" + en_])
                else:
                    self.op(en_, lambda h: h.tensor_copy(out=f[:, 2:3] if en_ == "dve" else f[:, 4:5], in_=f[:, 3:4] if en_ == "dve" else f[:, 5:6]),
                            R=[k], W=["fence_" + en_])
                fk.append("fence_" + en_)
        R = list(R) + fk
        self._deps(qn, R, W)
        sname = "d_" + key
        s = self.sem(sname)
        ins = e["h"].dma_start(out=out, in_=in_)
        ins.then_inc(s, 16)
        self.dmas[sname] = self.dmas.get(sname, 0) + 16
        tok = (sname, self.dmas[sname], "dma", None)
        self._record(tok, R, W)
        return tok

    def barrier(self):
        if self.dead:
            return
        toks = []
        for en, e in self.eng.items():
            if e["n"] > 0:
                last = e["n"] - 1
                toks.append((f"s_{en}{last // SEM_CH}", last % SEM_CH + 1, en, last))
        for sname, val in self.dmas.items():
            toks.append((sname, val, "dma", None))
        for en, e in self.eng.items():
            for (sname, val, pen, pidx) in toks:
                if pen == en:
                    continue
                if e["seen"].get(sname, 0) >= val:
                    continue
                e["h"].wait_ge(self.sems[sname], val)
                e["seen"][sname] = val
        self.lastw.clear()
        self.readers.clear()

    def ps(self, reserve=False):
        while True:
            i = self.psn
            self.psn = (self.psn + 1) % 8
            if i not in self.reserved:
                break
        if reserve:
            self.reserved.add(i)
        return i

    def release(self, i):
        self.reserved.discard(i)


class _Stop(Exception):
    pass


def build_program(stop=None):
    nc = bass.Bass("TRN2", target_bir_lowering=False)
    es = ExitStack()
    kb = KB(nc, es)

    def din(name, shape):
        return nc.dram_tensor(name, list(shape), F32, kind="ExternalInput").ap()

    def dout(name, shape):
        return nc.dram_tensor(name, list(shape), F32, kind="ExternalOutput").ap()

    xm = din("xm", (1024, D)); xp = din("xp", (1024, D)); xs = din("xs", (128, D))
    stc = din("stc", (16, 30, DC)); sre = din("sre", (16, 64, 64)); sim = din("sim", (16, 64, 64))
    norm_mix = din("norm_mix", (D,)); norm_ffn = din("norm_ffn", (D,)); norm_final = din("norm_final", (D,))
    w_in = din("w_in", (D, 3072)); conv_w = din("conv_w", (31, DC)); conv_b = din("conv_b", (DC,))
    ln_g = din("ln_g", (DC,)); ln_b = din("ln_b", (DC,))
    A_re = din("A_re", (64, 64)); A_im = din("A_im", (64, 64)); log_dt = din("log_dt", (64,))
    B_re = din("B_re", (64, 64, 16)); B_im = din("B_im", (64, 64, 16))
    C_re = din("C_re", (64, 16, 64)); C_im = din("C_im", (64, 16, 64)); ssm_D = din("ssm_D", (DC,))
    w_glu = din("w_glu", (DC, DC)); gn_c = din("gn_c", (DC,)); gn_s = din("gn_s", (DC,))
    w_out = din("w_out", (D, D)); w_g = din("w_g", (D, DFF)); w_u = din("w_u", (D, DFF)); w_d = din("w_d", (DFF, D))
    pmaskd = din("pmask", (128, 2))
    identd = din("ident", (128, 128)); maskd = din("mask2", (128, 128)); ident2d = din("ident2", (128, 128))

    o_ym = dout("o_ym", (1024, D)); o_ys = dout("o_ys", (128, D))
    o_cp = dout("o_cp", (30, DC)); o_rp = dout("o_rp", (64, 64)); o_ip = dout("o_ip", (64, 64))
    o_cs = dout("o_cs", (16, 30, DC)); o_rs = dout("o_rs", (16, 64, 64)); o_is = dout("o_is", (16, 64, 64))

    d_MS2r = nc.dram_tensor("d_MS2r", [128, 4096], BF16, kind="Internal").ap()
    d_MS2i = nc.dram_tensor("d_MS2i", [128, 4096], BF16, kind="Internal").ap()
    d_MY1r = nc.dram_tensor("d_MY1r", [128, 4096], BF16, kind="Internal").ap()
    d_MY1i = nc.dram_tensor("d_MY1i", [128, 4096], BF16, kind="Internal").ap()
    d_MY2 = nc.dram_tensor("d_MY2", [128, 8192], BF16, kind="Internal").ap()
    d_Cj = nc.dram_tensor("d_Cj", [128, 2048], F32, kind="Internal").ap()
    d_Sj = nc.dram_tensor("d_Sj", [128, 2048], F32, kind="Internal").ap()
    d_D0 = nc.dram_tensor("d_D0", [128, 2048], F32, kind="Internal").ap()

    uniq = {"n": 0}

    def sbL(stack, name, shape, dt=F32):
        uniq["n"] += 1
        return stack.enter_context(nc.sbuf_tensor(f"{name}_{uniq['n']}", list(shape), dt))

    def sb(name, shape, dt=F32):
        return sbL(es, name, shape, dt)

    psum = [es.enter_context(nc.psum_tensor(f"ps{i}", [128, 512], F32)) for i in range(8)]

    def psf(i):
        return psum[i][:]

    def psb(i):
        return psum[i][:].bitcast(BF16)

    PK = [f"ps{i}" for i in range(8)]

    def barrier():
        kb.barrier()

    dbg_names = []

    def stage(name, dumps=()):
        if stop != name or kb.dead:
            return
        for (dn, t_, keys) in dumps:
            shp = list(t_.shape)
            d_ = nc.dram_tensor("dbg_" + dn, shp, t_.dtype, kind="ExternalOutput").ap()
            idx = tuple(slice(None) for _ in shp)
            kb.dma("sp", d_[idx], t_[idx] if not isinstance(t_, bass.AP) else t_, R=keys, W=["dbg_" + dn], key="o_dbg")
            dbg_names.append("dbg_" + dn)
        kb.dead = True

    def cp(eng, out, in_, R, W):
        if eng == "act":
            return kb.op("act", lambda h: h.copy(out=out, in_=in_), R=R, W=W)
        return kb.op(eng, lambda h: h.tensor_copy(out=out, in_=in_), R=R, W=W)

    def tt(out, a, b, op, R, W, eng="dve"):
        return kb.op(eng, lambda h: h.tensor_tensor(out=out, in0=a, in1=b, op=op), R=R, W=W)

    def ts(out, a, s1, s2, op0, op1, R, W):
        if op1 is None:
            return kb.op("dve", lambda h: h.tensor_scalar(out=out, in0=a, scalar1=s1, scalar2=None, op0=op0), R=R, W=W)
        return kb.op("dve", lambda h: h.tensor_scalar(out=out, in0=a, scalar1=s1, scalar2=s2, op0=op0, op1=op1), R=R, W=W)

    def stt(out, a, sc, b, op0, op1, R, W):
        return kb.op("dve", lambda h: h.scalar_tensor_tensor(out=out, in0=a, scalar=sc, in1=b, op0=op0, op1=op1), R=R, W=W)

    def actf(out, in_, func, R, W, scale=None, bias=None, accum=None):
        kw = {}
        if scale is not None:
            kw["scale"] = scale
        if bias is not None:
            kw["bias"] = bias
        if accum is not None:
            kw["accum_out"] = accum
        return kb.op("act", lambda h: h.activation(out=out, in_=in_, func=func, **kw), R=R, W=W)

    def mm(out, lhsT, rhs, start, stop, R, W):
        return kb.op("pe", lambda h: h.matmul(out, lhsT=lhsT, rhs=rhs, start=start, stop=stop), R=R, W=W)

    def tr(out, in_, ident, R, W):
        return kb.op("pe", lambda h: h.transpose(out=out, in_=in_, identity=ident), R=R, W=W)

    def ld(dst, src, key, W, q="sp", R=()):
        if key in ("c0", "c_aa", "c_bb", "c_mk"):
            key = "k_" + W[0]
        return kb.dma(q, dst, src, R=R, W=W, key=key)

    ident_f = sb("ident_f", (128, 128)); ident_b = sb("ident_b", (128, 128), BF16)
    ones_b = sb("ones_b", (128, 128), BF16)
    neghalf = sb("neghalf", (128, 640))
    pm = sb("pm", (128, 2))
    fence_t = sb("fence_t", (128, 8))
    kb.op("pool", lambda h: h.memset(fence_t[:], 0.0), W=["fence_act", "fence_dve", "fence_pool"])
    kb.fence = fence_t
    cb_t = sb("cb_t", (128, 8)); lng_t = sb("lng_t", (128, 8)); lnb_t = sb("lnb_t", (128, 8))
    gnc_t = sb("gnc_t", (128, 8)); gns_t = sb("gns_t", (128, 8))
    cwT = sb("cwT", (128, 8, 31))
    A8r = sb("A8r", (128, 32)); A8i = sb("A8i", (128, 32))

    ld(ident_f[:], identd[:, :], "c0", ["ident_f"])
    ld(pm[:], pmaskd[:, :], "c0", ["pm"])
    cp("dve", ident_b[:], ident_f[:], ["ident_f"], ["ident_b"])
    kb.op("pool", lambda h: h.memset(ones_b[:], 1.0), W=["ones_b"])
    kb.op("pool", lambda h: h.memset(neghalf[:], -0.5), W=["neghalf"])
    with nc.allow_non_contiguous_dma(reason="small param vectors"):
        for t, v, nm in ((cb_t, conv_b, "cb"), (lng_t, ln_g, "lng"), (lnb_t, ln_b, "lnb"), (gnc_t, gn_c, "gnc"), (gns_t, gn_s, "gns")):
            ld(t[:], v.rearrange("(q p) -> p q", p=128), "c0", [nm])

    stage("s0", dumps=[("ident_b", ident_b, ["ident_b"]), ("cb", cb_t, ["cb"]), ("ones", ones_b, ["ones_b"])])

    nwt = sb("nwt", (128, 640))

    def rsqrt(dst, src, n, Rk, Wk):
        if n <= 8:
            kb.op("pool", lambda h: h.tensor_tensor(out=dst, in0=src, in1=neghalf[:, 0:n], op=ALU.pow), R=Rk + ["neghalf"], W=Wk)
            return
        actf(dst, src, AF.Sqrt, Rk + Wk, Wk)
        kb.op("dve", lambda h: h.reciprocal(out=dst, in_=dst), R=Wk, W=Wk)
        t = nwt[:, 0:n]
        for _ in range(2):
            tt(t, dst, dst, ALU.mult, Wk + ["nwt"], ["nwt"])
            tt(t, t, src, ALU.mult, Rk + ["nwt"], ["nwt"])
            ts(t, t, -0.5, 1.5, ALU.mult, ALU.add, ["nwt"], ["nwt"])
            tt(dst, dst, t, ALU.mult, Wk + ["nwt"], Wk)

    gstate = {"g": None}

    def load_g(vec, name):
        if gstate["g"] != name:
            ld(gbc[:], vec.partition_broadcast(128), "gbc", ["gbc"])
            gstate["g"] = name

    def norm_T(src_ap, src_keys, dst_fn, dst_keys):
        i_ = nt_cnt["n"] % 2; nt_cnt["n"] += 1
        xbn = xbs[i_]; xk_ = f"xb{i_}"
        c0_ = 3 * i_
        ck = [f"col{c0_}", f"col{c0_ + 1}", f"col{c0_ + 2}"]
        actf(junk[:], src_ap, AF.Square, src_keys + ["junk"], ["junk", ck[0]], accum=col[:, c0_:c0_ + 1])
        ts(col[:, c0_ + 1:c0_ + 2], col[:, c0_:c0_ + 1], 1.0 / D, EPS, ALU.mult, ALU.add, [ck[0]], [ck[1]])
        rsqrt(col[:, c0_ + 2:c0_ + 3], col[:, c0_ + 1:c0_ + 2], 1, [ck[1]], [ck[2]])
        stt(xbn[:], src_ap, col[:, c0_ + 2:c0_ + 3], gbc[:], ALU.mult, ALU.mult, src_keys + [ck[2], "gbc", xk_], [xk_])
        for hf in range(2):
            b = kb.ps()
            for k8 in range(8):
                kc = hf * 8 + k8
                tr(psb(b)[:, k8 * 128:(k8 + 1) * 128], xbn[:, kc * 128:(kc + 1) * 128], ident_b[:], [xk_, "ident_b"], [PK[b]])
            src = psb(b)[:, 0:1024].rearrange("p (a r) -> p a r", a=8)
            cp("act" if hf == 0 else "dve", dst_fn(hf), src, [PK[b]], dst_keys)

    xcount = {"n": 0}

    def load_x(rows_ap, r=128, c=D):
        s = xcount["n"] % 2
        xcount["n"] += 1
        ld(xt[s][0:r, 0:c], rows_ap, f"xt{s}", [f"xt{s}"])
        return xt[s], f"xt{s}"

    class Stream:
        def __init__(self, stack, name, nslots, shape, loads):
            self.name = name; self.n = nslots
            self.slots = [sbL(stack, f"{name}{i}", shape, BF16) for i in range(nslots)]
            self.loads = loads; self.issued = 0

        def ensure(self, upto):
            while self.issued <= min(upto, len(self.loads) - 1):
                i = self.issued; s = i % self.n
                for (dst_fn, src) in self.loads[i]:
                    kb.dma("pool", dst_fn(self.slots[s]), src, R=(), W=[f"{self.name}{s}"], key=f"{self.name}{s}")
                self.issued += 1

        def get(self, i):
            self.ensure(i + self.n - 1)
            s = i % self.n
            return self.slots[s], f"{self.name}{s}"

    w_in_v = w_in.rearrange("(kc p) n -> p kc n", p=128)
    w_out_v = w_out.rearrange("(kc p) n -> p kc n", p=128)
    w_glu_v = w_glu.rearrange("(kc p) n -> p kc n", p=128)
    w_g_v = w_g.rearrange("(kc p) n -> p kc n", p=128)
    w_u_v = w_u.rearrange("(kc p) n -> p kc n", p=128)
    w_d_v = w_d.rearrange("(kc p) n -> p kc n", p=128)


    with ExitStack() as ses:
        def sbt(name, shape, dt=F32):
            return sbL(ses, name, shape, dt)
        tA = sbt("tA", (128, 128)); tB = sbt("tB", (128, 128))
        twopi = sbt("twopi", (128, 32))
        ArT = sbt("ArT", (128, 32)); AiT = sbt("AiT", (128, 32)); LdT = sbt("LdT", (128, 32))
        s1 = sbt("s1", (128, 32)); s2_ = sbt("s2_", (128, 32)); s3 = sbt("s3", (128, 32)); s4 = sbt("s4", (128, 32))
        abr = sbt("abr", (128, 32)); abi = sbt("abi", (128, 32)); kr = sbt("kr", (128, 32)); ki = sbt("ki", (128, 32))
        ivr = sbt("ivr", (128, 32)); ivi = sbt("ivi", (128, 32))
        pwr = sbt("pwr", (128, 32, 8)); pwi = sbt("pwi", (128, 32, 8)); nwr = sbt("nwr", (128, 32, 8)); nwi = sbt("nwi", (128, 32, 8))
        CrT = sbt("CrT", (128, 32, 16)); CiT = sbt("CiT", (128, 32, 16))
        Br = sbt("Br", (128, 32, 16)); Bi = sbt("Bi", (128, 32, 16)); Bbr = sbt("Bbr", (128, 32, 16)); Bbi = sbt("Bbi", (128, 32, 16))
        t16a = sbt("t16a", (128, 32, 16)); t16b = sbt("t16b", (128, 32, 16))
        Qr = sbt("Qr", (128, 32, 128)); Qi = sbt("Qi", (128, 32, 128))
        Mr = sbt("Mr", (128, 32, 128)); Mi = sbt("Mi", (128, 32, 128))
        Y1r = sbt("Y1r", (128, 32, 128)); Y1i = sbt("Y1i", (128, 32, 128))
        big = sbt("big", (128, 32, 128))
        mask2 = sbt("mask2", (128, 128)); ident2 = sbt("ident2", (128, 128)); Dcol = sbt("Dcol", (128, 64))
        cwl = sbt("cwl", (32, DC))
        tmy = sbt("tmy", (128, 2, 128))
        MS2r = sbt("MS2r", (128, 4096), BF16); MS2i = sbt("MS2i", (128, 4096), BF16)
        MY1rb = sbt("MY1rb", (128, 4096), BF16); MY1ib = sbt("MY1ib", (128, 4096), BF16)
        MY2 = sbt("MY2", (128, 8192), BF16)

        ld(mask2[:], maskd[:, :], "c_mk", ["mask2"])
        ld(ident2[:], ident2d[:, :], "c_mk", ["ident2"])
        with nc.allow_non_contiguous_dma(reason="small param vectors"):
            for m in range(8):
                ld(Dcol[16 * m:16 * m + 16, :], ssm_D.rearrange("(g c) -> c g", c=16), "c_dc", ["Dcol"])
            ld(tmy[:, 0, 0:64], log_dt.partition_broadcast(128), "c_ld", ["tmy"])
            for ge in range(2):
                cp("dve", LdT[64 * ge:64 * ge + 64, :], tmy[64 * ge:64 * ge + 64, 0, ge:64:2], ["tmy"], ["LdT"])
            for ge in range(2):
                ld(Br[64 * ge:64 * ge + 64, :, :], B_re.rearrange("(gp ge) p c -> ge p gp c", ge=2)[ge], "c_bb", ["Br"])
                ld(Bi[64 * ge:64 * ge + 64, :, :], B_im.rearrange("(gp ge) p c -> ge p gp c", ge=2)[ge], "c_bb", ["Bi"])
        with nc.allow_non_contiguous_dma(reason="one-time conv weight transpose load"):
            for q in range(8):
                ld(cwT[:, q, :], conv_w[:, q * 128:(q + 1) * 128].rearrange("k p -> p k"), "c_cw", ["cwT"])
        stage("s1", dumps=[("cwT", cwT, ["cwT"]), ("Br", Br, ["Br"]), ("LdT", LdT, ["LdT"]), ("Dcol", Dcol, ["Dcol"])])
        for (src, dst, nm, tAx, tk) in ((A_re, ArT, "ArT", tA, "tA"), (A_im, AiT, "AiT", tB, "tBA")):
            kb.op("pool", lambda h, tAx=tAx: h.memset(tAx[:], 0.0), W=[tk])
            ld(tAx[0:32, :], src.rearrange("(gp ge) p -> gp (ge p)", ge=2), "c_aa", [tk])
            b = kb.ps()
            tr(psf(b)[:, 0:128], tAx[:, :], ident_f[:], [tk, "ident_f"], [PK[b]])
            cp("act", dst[:], psf(b)[:, 0:32], [PK[b]], [nm])
        tBs = [sbt(f"tB{i}", (128, 128)) for i in range(8)]
        ti = 0
        for (src, dst, nm) in ((C_re, CrT, "CrT"), (C_im, CiT, "CiT")):
            for blk in range(4):
                tBx = tBs[ti]; tk = f"tB{ti}"; ti += 1
                for gl in range(8):
                    gp = blk * 8 + gl
                    ld(tBx[16 * gl:16 * gl + 16, :].rearrange("c (ge p) -> c ge p", ge=2),
                       src[2 * gp:2 * gp + 2].rearrange("ge c p -> c ge p"), "c_" + tk, [tk])
                b = kb.ps()
                tr(psf(b)[:, 0:128], tBx[:, :], ident_f[:], [tk, "ident_f"], [PK[b]])
                cp("act", dst[:, blk * 8:(blk + 1) * 8, :], psf(b)[:, 0:128].rearrange("p (g c) -> p g c", c=16), [PK[b]], [nm])
        PI = float(np.pi)

        def range_reduce(shift):
            ts(s4[:], s3[:], shift, None, ALU.add, None, ["ang", "abi", "abr", "s4"], ["s4"])
            ts(twopi[:], s4[:], -PI, None, ALU.add, None, ["s4", "twopi"], ["twopi"])
            for m in range(1, 6):
                ts(kr[:], s4[:], TWO_PI * m, -TWO_PI, ALU.is_ge, ALU.mult, ["s4", "kr"], ["kr"])
                tt(twopi[:], twopi[:], kr[:], ALU.add, ["twopi", "kr"], ["twopi"])
            cp("dve", s4[:], twopi[:], ["twopi"], ["s4"])
        stage("s2", dumps=[("cwT", cwT, ["cwT"]), ("ArT", ArT, ["ArT"]), ("AiT", AiT, ["AiT"]), ("CrT", CrT, ["CrT"]), ("CiT", CiT, ["CiT"])])
        import math as _m

        def horner(dst, zz, cs, R, W):
            ts(dst, zz, float(cs[-1]), None, ALU.mult, None, R + W, W)
            for c in reversed(cs[1:-1]):
                stt(dst, dst, float(c), zz, ALU.add, ALU.mult, R + W, W)
            ts(dst, dst, float(cs[0]), None, ALU.add, None, W, W)

        ecs = [1.0 / _m.factorial(k) for k in range(11)]
        ts(kr[:], LdT[:], 0.125, None, ALU.mult, None, ["LdT", "kr"], ["kr"])
        horner(s1[:], kr[:], ecs, ["kr"], ["dt"])
        for _ in range(3):
            tt(s1[:], s1[:], s1[:], ALU.mult, ["dt"], ["dt"])
        tt(kr[:], s1[:], ArT[:], ALU.mult, ["dt", "ArT", "kr"], ["kr"])
        horner(s2_[:], kr[:], ecs[:8], ["kr"], ["mag"])
        tt(s3[:], s1[:], AiT[:], ALU.mult, ["dt", "AiT"], ["ang"])
        range_reduce(PI)
        ts(s4[:], s4[:], 0.25, None, ALU.mult, None, ["s4"], ["s4"])
        tt(ki[:], s4[:], s4[:], ALU.mult, ["s4", "ki"], ["ki"])
        horner(abi[:], ki[:], [1.0, -1.0 / 6, 1.0 / 120, -1.0 / 5040, 1.0 / 362880], ["ki"], ["abi"])
        tt(abi[:], abi[:], s4[:], ALU.mult, ["abi", "s4"], ["abi"])
        horner(abr[:], ki[:], [1.0, -0.5, 1.0 / 24, -1.0 / 720, 1.0 / 40320, -1.0 / 3628800], ["ki"], ["abr"])
        for _ in range(2):
            tt(kr[:], abi[:], abr[:], ALU.mult, ["abi", "abr", "kr"], ["kr"])
            tt(ki[:], abi[:], abi[:], ALU.mult, ["abi", "ki"], ["ki"])
            ts(abi[:], kr[:], 2.0, None, ALU.mult, None, ["kr", "abi"], ["abi"])
            ts(abr[:], ki[:], -2.0, 1.0, ALU.mult, ALU.add, ["ki", "abr"], ["abr"])
        tt(abr[:], abr[:], s2_[:], ALU.mult, ["abr", "mag"], ["abr"])
        tt(abi[:], abi[:], s2_[:], ALU.mult, ["abi", "mag"], ["abi"])
        tt(s1[:], s2_[:], s2_[:], ALU.mult, ["mag", "dt"], ["m2"])
        kb.op("dve", lambda h: h.reciprocal(out=s1[:], in_=s1[:]), R=["m2"], W=["m2"])
        tt(ivr[:], abr[:], s1[:], ALU.mult, ["abr", "m2"], ["ivr"])
        tt(ivi[:], abi[:], s1[:], ALU.mult, ["abi", "m2"], ["ivi"])
        ts(ivi[:], ivi[:], -1.0, None, ALU.mult, None, ["ivi"], ["ivi"])
        tt(s1[:], ArT[:], ArT[:], ALU.mult, ["ArT", "ivr", "ivi", "m2"], ["den"])
        tt(s4[:], AiT[:], AiT[:], ALU.mult, ["AiT", "abr"], ["s4"])
        tt(s1[:], s1[:], s4[:], ALU.add, ["den", "s4"], ["den"])
        kb.op("dve", lambda h: h.reciprocal(out=s1[:], in_=s1[:]), R=["den"], W=["den"])
        ts(s3[:], abr[:], -1.0, None, ALU.add, None, ["abr", "ang", "s4"], ["nr"])
        tt(kr[:], s3[:], ArT[:], ALU.mult, ["nr", "ArT"], ["kr"])
        tt(s4[:], abi[:], AiT[:], ALU.mult, ["abi", "AiT", "den"], ["s4"])
        tt(kr[:], kr[:], s4[:], ALU.add, ["kr", "s4"], ["kr"])
        tt(kr[:], kr[:], s1[:], ALU.mult, ["kr", "den"], ["kr"])
        tt(ki[:], abi[:], ArT[:], ALU.mult, ["abi", "ArT"], ["ki"])
        tt(s4[:], s3[:], AiT[:], ALU.mult, ["nr", "AiT", "kr"], ["s4"])
        tt(ki[:], ki[:], s4[:], ALU.subtract, ["ki", "s4"], ["ki"])
        tt(ki[:], ki[:], s1[:], ALU.mult, ["ki", "den"], ["ki"])

        stage("s3", dumps=[("cwT", cwT, ["cwT"]), ("abr", abr, ["abr"]), ("abi", abi, ["abi"]), ("kr", kr, ["kr"]), ("ki", ki, ["ki"]), ("ivr", ivr, ["ivr"]), ("ivi", ivi, ["ivi"])])

        def cmul_into(dr, di, ar_, ai_, br_, bi_, R, W):
            tt(s2_[:], ar_, br_, ALU.mult, R + ["tmpa", "mag", "s2"], ["tmpa"])
            tt(s4[:], ai_, bi_, ALU.mult, R + ["tmpb", "s4", "ki"], ["tmpb"])
            tt(dr, s2_[:], s4[:], ALU.subtract, ["tmpa", "tmpb"], W)
            tt(s2_[:], ar_, bi_, ALU.mult, R + ["tmpa"] + W, ["tmpa"])
            tt(s4[:], ai_, br_, ALU.mult, R + ["tmpb"] + W, ["tmpb"])
            tt(di, s2_[:], s4[:], ALU.add, ["tmpa", "tmpb"], W)

        cp("dve", pwr[:, :, 0], abr[:], ["abr", "mag"], ["pw"])
        cp("dve", pwi[:, :, 0], abi[:], ["abi"], ["pw"])
        cp("dve", nwr[:, :, 0], ivr[:], ["ivr"], ["nw"])
        cp("dve", nwi[:, :, 0], ivi[:], ["ivi"], ["nw"])
        for k in range(1, 8):
            cmul_into(pwr[:, :, k], pwi[:, :, k], pwr[:, :, k - 1], pwi[:, :, k - 1], abr[:], abi[:], ["pw", "abr", "abi"], ["pw"])
            cmul_into(nwr[:, :, k], nwi[:, :, k], nwr[:, :, k - 1], nwi[:, :, k - 1], ivr[:], ivi[:], ["nw", "ivr", "ivi"], ["nw"])
        cp("dve", A8r[:], pwr[:, :, 7], ["pw"], ["A8"])
        cp("dve", A8i[:], pwi[:, :, 7], ["pw"], ["A8"])
        Tc = big[:, :, 0:64]; Ts = big[:, :, 64:128]; D0 = Mi[:, :, 0:64]; Tt = Mi[:, :, 64:96]
        rho = sbt("rho", (128, 32)); ur = sbt("ur", (128, 32)); ui = sbt("ui", (128, 32)); u2r = sbt("u2r", (128, 32)); u2i = sbt("u2i", (128, 32))
        tt(rho[:], A8r[:], A8r[:], ALU.mult, ["A8", "rho"], ["rho"])
        tt(u2r[:], A8i[:], A8i[:], ALU.mult, ["A8"], ["u2r"])
        tt(rho[:], rho[:], u2r[:], ALU.add, ["rho", "u2r"], ["rho"])
        kb.op("dve", lambda h: h.reciprocal(out=u2i[:], in_=rho[:]), R=["rho"], W=["u2i"])
        kb.op("pool", lambda h: h.memset(ur[:], 1.0), W=["ur"])
        for _ in range(7):
            tt(ui[:], ur[:], ur[:], ALU.mult, ["ur", "ui"], ["ui"])
            tt(ui[:], ui[:], rho[:], ALU.mult, ["ui", "rho"], ["ui"])
            ts(ui[:], ui[:], -0.5, 1.5, ALU.mult, ALU.add, ["ui"], ["ui"])
            tt(ur[:], ur[:], ui[:], ALU.mult, ["ur", "ui"], ["ur"])
        tt(rho[:], rho[:], ur[:], ALU.mult, ["rho", "ur"], ["rho"])
        tt(ui[:], A8i[:], ur[:], ALU.mult, ["A8", "ur", "ui"], ["ui"])
        tt(ur[:], A8r[:], ur[:], ALU.mult, ["A8", "ur"], ["ur"])
        kb.op("pool", lambda h: h.memset(Tc[:, :, 0:1], 1.0), W=["big"])
        kb.op("pool", lambda h: h.memset(Ts[:, :, 0:1], 0.0), W=["big"])
        cp("dve", u2r[:], ur[:], ["ur", "u2r", "rho"], ["u2"]); cp("dve", u2i[:], ui[:], ["ui", "u2i"], ["u2"])
        n_ = 1
        while n_ < 64:
            def bj(a):
                return a.unsqueeze(2).to_broadcast([128, 32, n_])
            tt(Tt[:, :, 0:n_], Tc[:, :, 0:n_], bj(u2r[:]), ALU.mult, ["big", "u2", "Mi"], ["Mi"])
            tt(Tc[:, :, n_:2 * n_], Ts[:, :, 0:n_], bj(u2i[:]), ALU.mult, ["big", "u2", "big"], ["big"])
            tt(Tc[:, :, n_:2 * n_], Tt[:, :, 0:n_], Tc[:, :, n_:2 * n_], ALU.subtract, ["Mi", "big"], ["big"])
            tt(Tt[:, :, 0:n_], Tc[:, :, 0:n_], bj(u2i[:]), ALU.mult, ["big", "u2", "Mi"], ["Mi"])
            tt(Ts[:, :, n_:2 * n_], Ts[:, :, 0:n_], bj(u2r[:]), ALU.mult, ["big", "u2"], ["big"])
            tt(Ts[:, :, n_:2 * n_], Tt[:, :, 0:n_], Ts[:, :, n_:2 * n_], ALU.add, ["Mi", "big"], ["big"])
            tt(Tt[:, :, 0], u2r[:], u2r[:], ALU.mult, ["u2", "Mi"], ["Mi"])
            tt(Tt[:, :, 1], u2i[:], u2i[:], ALU.mult, ["u2", "Mi"], ["Mi"])
            tt(Tt[:, :, 2], u2r[:], u2i[:], ALU.mult, ["u2", "Mi"], ["Mi"])
            tt(u2r[:], Tt[:, :, 0], Tt[:, :, 1], ALU.subtract, ["Mi", "u2"], ["u2"])
            ts(u2i[:], Tt[:, :, 2], 2.0, None, ALU.mult, None, ["Mi", "u2"], ["u2"])
            n_ *= 2
        cp("dve", D0, rho[:].unsqueeze(2).to_broadcast([128, 32, 64]), ["rho"], ["Mi"])
        kb.op("pool", lambda h: h.memset(D0[:, :, 0:1], 0.0), R=["Mi"], W=["Mi"])
        stage("s4b", dumps=[("big", big, ["big"]), ("Mi", Mi, ["Mi"]), ("rho", rho, ["rho"]), ("ur", ur, ["ur"]), ("ui", ui, ["ui"])])
        for (t_, d_, nm) in ((Tc, d_Cj, "big"), (Ts, d_Sj, "big"), (D0, d_D0, "Mi")):
            kb.dma("sp", d_.rearrange("p (g j) -> p g j", j=64), t_, R=[nm], W=["dstash"], key="stash")

        stage("s4", dumps=[("cwT", cwT, ["cwT"]), ("pwr", pwr, ["pw"]), ("pwi", pwi, ["pw"]), ("nwr", nwr, ["nw"]), ("nwi", nwi, ["nw"])])

        def bc16(a):
            return a.unsqueeze(2).to_broadcast([128, 32, 16])
        tt(t16a[:], Br[:], bc16(kr[:]), ALU.mult, ["Br", "kr"], ["t16a"])
        tt(t16b[:], Bi[:], bc16(ki[:]), ALU.mult, ["Bi", "ki"], ["t16b"])
        tt(Bbr[:], t16a[:], t16b[:], ALU.subtract, ["t16a", "t16b"], ["Bbr"])
        tt(t16a[:], Bi[:], bc16(kr[:]), ALU.mult, ["Bi", "kr", "Bbr"], ["t16a"])
        tt(t16b[:], Br[:], bc16(ki[:]), ALU.mult, ["Br", "ki", "Bbr"], ["t16b"])
        tt(Bbi[:], t16a[:], t16b[:], ALU.add, ["t16a", "t16b"], ["Bbi"])

        def v4(t):
            return t[:].rearrange("p g (s c) -> p g s c", c=16)

        def bs(t):
            return t[:].unsqueeze(3).to_broadcast([128, 32, 8, 16])

        def bcc(t):
            return t[:].unsqueeze(2).to_broadcast([128, 32, 8, 16])
        tt(v4(Qr), bs(nwr), bcc(Bbr), ALU.mult, ["nw", "Bbr"], ["Qr"])
        tt(v4(big), bs(nwi), bcc(Bbi), ALU.mult, ["nw", "Bbi"], ["big"])
        tt(Qr[:], Qr[:], big[:], ALU.subtract, ["Qr", "big"], ["Qr"])
        tt(v4(Qi), bs(nwr), bcc(Bbi), ALU.mult, ["nw", "Bbi"], ["Qi"])
        tt(v4(big), bs(nwi), bcc(Bbr), ALU.mult, ["nw", "Bbr", "Qr"], ["big"])
        tt(Qi[:], Qi[:], big[:], ALU.add, ["Qi", "big"], ["Qi"])

        def b128(a):
            return a.unsqueeze(2).to_broadcast([128, 32, 128])
        tt(Mr[:], Qr[:], b128(A8r[:]), ALU.mult, ["Qr", "A8"], ["Mr"])
        tt(big[:], Qi[:], b128(A8i[:]), ALU.mult, ["Qi", "A8"], ["big"])
        tt(Mr[:], Mr[:], big[:], ALU.subtract, ["Mr", "big"], ["Mr"])
        tt(Mi[:], Qi[:], b128(A8r[:]), ALU.mult, ["Qi", "A8"], ["Mi"])
        tt(big[:], Qr[:], b128(A8i[:]), ALU.mult, ["Qr", "A8", "Mr"], ["big"])
        tt(Mi[:], Mi[:], big[:], ALU.add, ["Mi", "big"], ["Mi"])
        tt(v4(Y1r), bs(pwr), bcc(CrT), ALU.mult, ["pw", "CrT"], ["Y1r"])
        tt(v4(big), bs(pwi), bcc(CiT), ALU.mult, ["pw", "CiT", "Mi"], ["big"])
        tt(Y1r[:], Y1r[:], big[:], ALU.subtract, ["Y1r", "big"], ["Y1r"])
        tt(v4(Y1i), bs(pwi), bcc(CrT), ALU.mult, ["pw", "CrT"], ["Y1i"])
        tt(v4(big), bs(pwr), bcc(CiT), ALU.mult, ["pw", "CiT", "Y1r"], ["big"])
        tt(Y1i[:], Y1i[:], big[:], ALU.add, ["Y1i", "big"], ["Y1i"])
        ts(Y1i[:], Y1i[:], -1.0, None, ALU.mult, None, ["Y1i"], ["Y1i"])
        cp("act", MY1rb[:].rearrange("p (g n) -> p g n", n=128), Y1r[:], ["Y1r"], ["MY1rb"])
        cp("act", MY1ib[:].rearrange("p (g n) -> p g n", n=128), Y1i[:], ["Y1i"], ["MY1ib"])
        stage("s5", dumps=[("cwT", cwT, ["cwT"]), ("Qr", Qr, ["Qr"]), ("Qi", Qi, ["Qi"]), ("Mr", Mr, ["Mr"]), ("Y1r", Y1r, ["Y1r"]), ("Y1i", Y1i, ["Y1i"])])
        for gp in range(32):
            b = kb.ps()
            for ri, M_ in enumerate((Mr, Mi)):
                tr(psf(b)[:, ri * 128:(ri + 1) * 128], M_[:, gp, :], ident_f[:], ["Mr", "Mi", "ident_f"], [PK[b]])
            cp("act", MS2r[:, gp * 128:(gp + 1) * 128], psf(b)[:, 0:128], [PK[b]], ["MS2r"])
            cp("act", MS2i[:, gp * 128:(gp + 1) * 128], psf(b)[:, 128:256], [PK[b]], ["MS2i"])
        stage("s6", dumps=[("cwT", cwT, ["cwT"]), ("MS2r", MS2r, ["MS2r"]), ("MS2i", MS2i, ["MS2i"])])
        MrB = Mr[:].bitcast(BF16); MiB = Mi[:].bitcast(BF16); bigB = big[:].bitcast(BF16)
        cp("act", MrB[:, :, 0:128], Qr[:], ["Qr", "Mr", "MS2r", "MS2i"], ["Mr"])
        cp("act", MrB[:, :, 128:256], Qi[:], ["Qi", "Mr"], ["Mr"])
        for ge in range(2):
            ts(MiB[:, :, ge * 128:(ge + 1) * 128], Y1r[:], pm[:, ge:ge + 1], None, ALU.mult, None, ["Y1r", "pm", "Mi", "MS2r", "MS2i"], ["Mi"])
            ts(bigB[:, :, ge * 128:(ge + 1) * 128], Y1i[:], pm[:, ge:ge + 1], None, ALU.mult, None, ["Y1i", "pm", "big", "MY1ib", "MY1rb"], ["big"])
        for gp in range(32):
            bge = (kb.ps(), kb.ps())
            for ge in range(2):
                mm(psf(bge[ge])[:, 0:128], MrB[:, gp, 0:128], MiB[:, gp, ge * 128:(ge + 1) * 128], True, False, ["Mr", "Mi"], [PK[bge[ge]]])
                mm(psf(bge[ge])[:, 0:128], MrB[:, gp, 128:256], bigB[:, gp, ge * 128:(ge + 1) * 128], False, True, ["Mr", "big"], [PK[bge[ge]]])
                tt(tmy[:, ge, :], psf(bge[ge])[:, 0:128], mask2[:], ALU.mult, [PK[bge[ge]], "mask2", "tmy"], ["tmy"])
            for ge in range(2):
                g = 2 * gp + ge
                stt(MY2[:, g * 128:(g + 1) * 128], ident2[:], Dcol[:, g:g + 1], tmy[:, ge, :], ALU.mult, ALU.add,
                    ["tmy", "ident2", "Dcol"], ["MY2"])
        stage("setup", dumps=[("A8r", A8r, ["A8"]), ("A8i", A8i, ["A8"]), ("MS2r", MS2r, ["MS2r"]), ("MS2i", MS2i, ["MS2i"]), ("MY2", MY2, ["MY2"]),
                              ("MY1rb", MY1rb, ["MY1rb"]), ("cwT", cwT, ["cwT"]), ("kr", kr, ["kr"]), ("abr", abr, ["abr"]), ("abi", abi, ["abi"])])
        for (t_, d_, nm) in ((MS2r, d_MS2r, "MS2r"), (MS2i, d_MS2i, "MS2i"), (MY1rb, d_MY1r, "MY1rb"), (MY1ib, d_MY1i, "MY1ib"), (MY2, d_MY2, "MY2")):
            kb.dma("sp", d_[:, :], t_[:], R=[nm], W=["dstash"], key="stash")
        if not kb.dead:
            kb.eng["act"]["h"].wait_ge(kb.sems["d_stash"], kb.dmas["d_stash"])
        barrier()

    gbc = sb("gbc", (128, D))
    xt = [sb(f"xt{i}", (128, D)) for i in range(2)]
    xbs = [sb(f"xb{i}", (128, D), BF16) for i in range(2)]
    junk = sb("junk", (128, D), BF16)
    xb = xbs[0]
    nt_cnt = {"n": 0}
    col = sb("col", (128, 8))
    HRc = sb("HRc", (128, 2, 32)); HIc = sb("HIc", (128, 2, 32))
    HRs = sb("HRs", (128, 16, 32)); HIs = sb("HIs", (128, 16, 32))
    concat = sb("concat", (128, 16, 640), BF16)
    utail = sb("utail", (128, 8, 32), BF16)
    rt = [sb(f"rt{i}", (128, 16, 32)) for i in range(4)]
    def P0_tiles(stack_x, tiles, xnT, prefix=False):
        load_g(norm_mix, "mix")
        for (rows, c0) in tiles:
            xtile, xk = load_x(rows)
            norm_T(xtile[:], [xk], lambda hf, c0=c0: xnT[:, hf * 8:(hf + 1) * 8, c0:c0 + 128], ["xnT"])

    for grp in range(2):
        has_s = (grp == 0)
        NTOK = 640 if has_s else 512
        NT = NTOK // 128
        NCOL = 32 + NTOK
        tiles_main = [(xm[grp * 512 + 128 * t: grp * 512 + 128 * (t + 1), :], 32 + 128 * t) for t in range(4)]
        if has_s:
            tiles_main.append((xs[:, :], 32 + 512))
        blocks = [(0, 512)] + ([(512, 128)] if has_s else [])

        with ExitStack() as x1:
            xnT = sbL(x1, "xnT", (128, 16, 672), BF16)
            u = sbL(x1, "u", (128, 8, 672), BF16)
            ups = sbL(x1, "ups", (128, 8, 16, 38), BF16)
            y32 = sbL(x1, "y32", (128, 8, 640))
            uf = sbL(x1, "uf", (128, 512)); sg = sbL(x1, "sg", (128, 512))
            dgs = [sbL(x1, f"dg{i}", (128, 31, 128), BF16) for i in range(2)]
            ybf = sbL(x1, "ybf", (128, 640), BF16); ysq = sbL(x1, "ysq", (128, 640), BF16)
            mu = sbL(x1, "mu", (128, 640)); rs = sbL(x1, "rs", (128, 640)); vt = sbL(x1, "vt", (128, 640))
            uout = sbL(x1, "uout", (128, 8, 160))
            kb.op("dve", lambda h: h.memset(uout[:], 0.0), W=["uout"])
            if grp == 1:
                kb.op("dve", lambda h: h.memset(xnT[:, :, 0:32], 0.0), W=["xnT"])
            if grp == 0:
                pass
            if grp == 0:
                P0_tiles(x1, [(xp[896:1024, :], 32)], xnT)
                cp("dve", xnT[:, :, 0:32], xnT[:, :, 128:160], ["xnT"], ["xnT"])
            P0_tiles(x1, tiles_main, xnT)
            if has_s:
                for i in range(4):
                    xtile, xk = load_x(stc[4 * i:4 * i + 4].rearrange("b i c -> (b i) c"), 120, DC)
                    for bl in range(4):
                        kb.dma("sp", o_cs[4 * i + bl, 0:22, :], xtile[bl * 30 + 8:bl * 30 + 30, 0:DC], R=[xk], W=["o_cs_a"], key="ocs")
                    for q in range(8):
                        b = kb.ps()
                        tr(psf(b)[:, 0:128], xtile[:, q * 128:(q + 1) * 128], ident_f[:], [xk, "ident_f"], [PK[b]])
                        cp("act" if q % 2 == 0 else "dve", ups[:, q, 4 * i:4 * i + 4, 0:30],
                           psf(b)[:, 0:120].rearrange("p (b i) -> p b i", i=30), [PK[b]], ["ups"])
            loads = [[(lambda s: s[:, 0, :, :], w_in_v[:, :, q * 128:(q + 1) * 128]),
                      (lambda s: s[:, 1, :, :], w_in_v[:, :, 1024 + q * 128:1024 + (q + 1) * 128])] for q in range(8)]
            with ExitStack() as wsx:
                wvg = Stream(wsx, "wvg", 2, (128, 2, 16, 128), loads)
                mblocks = [(0, 512), (512, NCOL - 512)]
                for q in range(8):
                    wslot, wk = wvg.get(q)
                    for (c0, n) in mblocks:
                        bv = kb.ps(); bg = kb.ps()
                        for vg, bb in ((0, bv), (1, bg)):
                            for kc in range(16):
                                mm(psf(bb)[:, 0:n], wslot[:, vg, kc, :], xnT[:, kc, c0:c0 + n], kc == 0, kc == 15, [wk, "xnT"], [PK[bb]])
                        actf(sg[:, 0:n], psf(bg)[:, 0:n], AF.Sigmoid, [PK[bg]], ["sg"])
                        tt(uf[:, 0:n], psf(bv)[:, 0:n], sg[:, 0:n], ALU.mult, [PK[bv], "sg"], ["uf"])
                        lo = 0 if (grp == 0 or c0 > 0) else 32
                        pe_ = min(c0 + n, 32 + 512)
                        if pe_ > c0 + lo:
                            cp("pool", u[:, q, c0 + lo:pe_], uf[:, lo:pe_ - c0], ["uf"], ["u"])
                        if has_s and c0 + n > 544:
                            s0 = 544 - c0
                            cp("act", ups[:, q, :, 30:38], uf[:, s0:s0 + 128].rearrange("p (b i) -> p b i", i=8), ["uf"], ["ups"])
                            cp("dve", uout[:, q, 32:160], uf[:, s0:s0 + 128], ["uf"], ["uout"])
                        if grp == 1 and c0 + n >= 544:
                            s0 = 514 - c0
                            cp("dve", uout[:, q, 0:30], uf[:, s0:s0 + 30], ["uf"], ["uout"])
                if grp == 1:
                    cp("dve", u[:, :, 2:32], utail[:, :, 2:32], ["utail"], ["u"])
                else:
                    pass
                if grp == 0:
                    cp("dve", utail[:, :, 2:32], u[:, :, 514:544], ["u"], ["utail"])
            if has_s:
                for tch in range(2):
                    xo = xt[xcount["n"] % 2]; xok = f"xt{xcount['n'] % 2}"; xcount["n"] += 1
                    for q4 in range(4):
                        q = tch * 4 + q4
                        b = kb.ps()
                        tr(psf(b)[:, 0:128], uout[:, q, 32:160], ident_f[:], ["uout", "ident_f"], [PK[b]])
                        cp("act" if q % 2 == 0 else "dve", xo[:, q * 128:(q + 1) * 128], psf(b)[:, 0:128], [PK[b]], [xok])
                for bq in range(16):
                    for tch in range(2):
                        pass
                s_a = (xcount["n"] - 2) % 2; s_b = (xcount["n"] - 1) % 2
                for bq in range(16):
                    kb.dma("sp", o_cs[bq, 22:30, 0:512], xt[s_a][bq * 8:(bq + 1) * 8, 0:512], R=[f"xt{s_a}"], W=["o_cs_b"], key="ocs")
                    kb.dma("sp", o_cs[bq, 22:30, 512:1024], xt[s_b][bq * 8:(bq + 1) * 8, 512:1024], R=[f"xt{s_b}"], W=["o_cs_b"], key="ocs")
            if grp == 1:
                xo = xt[xcount["n"] % 2]; xok = f"xt{xcount['n'] % 2}"; xcount["n"] += 1
                for q in range(8):
                    b = kb.ps()
                    tr(psf(b)[:, 0:128], uout[:, q, 0:128], ident_f[:], ["uout", "ident_f"], [PK[b]])
                    cp("act" if q % 2 == 0 else "dve", xo[0:30, q * 128:(q + 1) * 128], psf(b)[0:30, 0:128], [PK[b]], [xok])
                kb.dma("sp", o_cp[:, :], xo[0:30, 0:1024], R=[xok], W=["o_cp"], key="ocp")
            sblocks = [(0, 512, False)] + ([(512, 128, True)] if has_s else [])
            st_b = {}
            for (t0, n, is_s) in sblocks:
                st_b[t0] = (kb.ps(reserve=True), kb.ps(reserve=True))
            for q in range(8):
                dg = dgs[q % 2]; dgk = f"dg{q % 2}"
                for k in range(31):
                    ts(dg[:, k, :], ident_b[:], cwT[:, q, k:k + 1], None, ALU.mult, None, ["ident_b", "cwT", dgk], [dgk])
                for (t0, n, is_s) in sblocks:
                    b = kb.ps()
                    for k in range(31):
                        if is_s:
                            rhs = ups[:, q, :, k:k + 8]
                        else:
                            rhs = u[:, q, 2 + k:2 + k + 512]
                        mm(psf(b)[:, 0:n], dg[:, k, :], rhs, k == 0, k == 30, [dgk, "u", "ups"], [PK[b]])
                    actf(y32[:, q, t0:t0 + n], psf(b)[:, 0:n], AF.Identity, [PK[b], "cb"], ["y32"], bias=cb_t[:, q:q + 1])
                    actf(ybf[:, t0:t0 + n], psf(b)[:, 0:n], AF.Identity, [PK[b], "cb", "ybf"], ["ybf"], bias=cb_t[:, q:q + 1])
                    actf(ysq[:, t0:t0 + n], psf(b)[:, 0:n], AF.Square, [PK[b], "cb", "ysq"], ["ysq"], bias=cb_t[:, q:q + 1])
                    b1, b2 = st_b[t0]
                    mm(psf(b1)[:, 0:n], ones_b[:], ybf[:, t0:t0 + n], q == 0, q == 7, ["ones_b", "ybf"], [PK[b1]])
                    mm(psf(b2)[:, 0:n], ones_b[:], ysq[:, t0:t0 + n], q == 0, q == 7, ["ones_b", "ysq"], [PK[b2]])
            for (t0, n, is_s) in sblocks:
                b1, b2 = st_b[t0]
                ts(mu[:, t0:t0 + n], psf(b1)[:, 0:n], 1.0 / DC, None, ALU.mult, None, [PK[b1]], ["mu"])
                tt(vt[:, t0:t0 + n], mu[:, t0:t0 + n], mu[:, t0:t0 + n], ALU.mult, ["mu"], ["vt"])
                stt(vt[:, t0:t0 + n], psf(b2)[:, 0:n], 1.0 / DC, vt[:, t0:t0 + n], ALU.mult, ALU.subtract, [PK[b2], "vt"], ["vt"])
                ts(vt[:, t0:t0 + n], vt[:, t0:t0 + n], EPS, None, ALU.add, None, ["vt"], ["vt"])
                rsqrt(rs[:, t0:t0 + n], vt[:, t0:t0 + n], n, ["vt"], ["rs"])
                kb.release(b1); kb.release(b2)

            def rms_finish(src32, key32, gvec, gk, cbase):
                fb = {}
                for (t0, n, is_s) in sblocks:
                    fb[t0] = kb.ps()
                for q in range(8):
                    for (t0, n, is_s) in sblocks:
                        actf(ysq[:, t0:t0 + n], src32[:, q, t0:t0 + n], AF.Square, [key32, "ysq"], ["ysq"])
                        mm(psf(fb[t0])[:, 0:n], ones_b[:], ysq[:, t0:t0 + n], q == 0, q == 7, ["ones_b", "ysq"], [PK[fb[t0]]])
                for (t0, n, is_s) in sblocks:
                    ts(vt[:, t0:t0 + n], psf(fb[t0])[:, 0:n], 1.0 / DC, EPS, ALU.mult, ALU.add, [PK[fb[t0]], "vt"], ["vt"])
                    rsqrt(rs[:, t0:t0 + n], vt[:, t0:t0 + n], n, ["vt"], ["rs"])
                for q in range(8):
                    stt(concat[:, cbase + q, 0:NTOK], src32[:, q, 0:NTOK], gvec[:, q:q + 1], rs[:, 0:NTOK], ALU.mult, ALU.mult,
                        [key32, gk, "rs"], ["concat"])

            for q in range(8):
                tt(y32[:, q, 0:NTOK], y32[:, q, 0:NTOK], mu[:, 0:NTOK], ALU.subtract, ["y32", "mu"], ["y32"])
                tt(y32[:, q, 0:NTOK], y32[:, q, 0:NTOK], rs[:, 0:NTOK], ALU.mult, ["y32", "rs"], ["y32"])
                for (t0, n, is_s) in sblocks:
                    actf(uf[:, 0:n], y32[:, q, t0:t0 + n], AF.Identity, ["y32", "lng", "lnb", "uf"], ["uf"], scale=lng_t[:, q:q + 1], bias=lnb_t[:, q:q + 1])
                    actf(sg[:, 0:n], uf[:, 0:n], AF.Sigmoid, ["uf", "sg"], ["sg"])
                    tt(y32[:, q, t0:t0 + n], uf[:, 0:n], sg[:, 0:n], ALU.mult, ["uf", "sg", "y32"], ["y32"])
            rms_finish(y32, "y32", gnc_t, "gnc", 0)
            if grp == 0:
                stage("x1c", dumps=[("concat", concat, ["concat"])])
                stage("x1", dumps=[("concat", concat, ["concat"]), ("u", u, ["u"]), ("y32", y32, ["y32"]), ("xnT", xnT, ["xnT"]), ("ups", ups, ["ups"]), ("mu", mu, ["mu"]), ("rs", rs, ["rs"]), ("vt", vt, ["vt"]), ("ybf", ybf, ["ybf"]), ("cwT", cwT, ["cwT"])])
        if grp == 0:
            stage("x1d", dumps=[("concat", concat, ["concat"])])
        barrier()

        if grp == 0:
            stage("x1b", dumps=[("concat", concat, ["concat"])])
        with ExitStack() as e_:
            NBT = 80
            U = sbL(e_, "U", (128, 64, NBT), BF16)
            HRb = [sbL(e_, f"HRb{i}", (128, 32, NBT), BF16) for i in range(2)]
            HIb = [sbL(e_, f"HIb{i}", (128, 32, NBT), BF16) for i in range(2)]
            ygT = sbL(e_, "ygT", (128, 8, 640), BF16)

            def run_A1(xnT, ZT, wst_stack, NB, cbase, uo):
                R2 = 2 * NB
                loads = [[(lambda s: s[:, :, :], w_in_v[:, :, 2048 + i * 256:2048 + (i + 1) * 256])] for i in range(4)]
                with ExitStack() as ws_:
                    wS = Stream(ws_, "wS", 2, (128, 16, 256), loads)
                    for nbk in range(4):
                        wslot, wk = wS.get(nbk)
                        for m in range(4):
                            b = kb.ps()
                            for kc in range(16):
                                lhs = xnT[:, kc, cbase + m:cbase + m + 8 * NB - 3:4]
                                mm(psf(b)[0:R2, 0:256], lhs, wslot[:, kc, :], kc == 0, kc == 15, ["xnT", wk], [PK[b]])
                            srcv = psf(b)[0:R2, 0:256].rearrange("r (g c) -> r g c", c=16)
                            cp("act", ZT[0:R2, nbk * 16:(nbk + 1) * 16, 0, m * 16:(m + 1) * 16], srcv, [PK[b]], ["ZT"])
                            cp("act", ZT[0:R2, nbk * 16:(nbk + 1) * 16, 1, m * 16:(m + 1) * 16], srcv, [PK[b]], ["ZT"])
                    for g8 in range(8):
                        b = kb.ps()
                        for gl in range(8):
                            g = g8 * 8 + gl
                            tr(psb(b)[:, gl * R2:(gl + 1) * R2], ZT[0:R2, g, :, :], ident_b[0:R2, 0:R2], ["ZT", "ident_b"], [PK[b]])
                        v = psb(b)[:, 0:16 * NB].rearrange("k (g j s) -> k g j s", g=8, s=2)
                        cp("act", U[0:64, g8 * 8:(g8 + 1) * 8, uo:uo + NB], v[0:64, :, :, 0], [PK[b]], ["U"])
                        cp("act", U[64:128, g8 * 8:(g8 + 1) * 8, uo:uo + NB], v[64:128, :, :, 1], [PK[b]], ["U"])

            def run_A2(a2, SRs, SIs, NB, uo, chain):
                if chain:
                    Cj = sbL(a2, "Cj", (128, 32, 64)); Sj = sbL(a2, "Sj", (128, 32, 64)); D0 = sbL(a2, "D0_", (128, 32, 64))
                    ld(Cj[:].rearrange("p g j -> p (g j)"), d_Cj[:, :], "cst2", ["Cj"]); ld(Sj[:].rearrange("p g j -> p (g j)"), d_Sj[:, :], "cst2", ["Sj"])
                    ld(D0[:].rearrange("p g j -> p (g j)"), d_D0[:, :], "cst2", ["D0"])
                with ExitStack() as ms_:
                    MS2r = sbL(ms_, "MS2r_", (128, 4096), BF16); MS2i = sbL(ms_, "MS2i_", (128, 4096), BF16)
                    ld(MS2r[:], d_MS2r[:, :], "cst", ["MS2r"]); ld(MS2i[:], d_MS2i[:, :], "cst", ["MS2i"])
                    for gq in range(4):
                        bR = kb.ps(); bI = kb.ps()
                        for gl in range(8):
                            gp = gq * 8 + gl
                            for ge in range(2):
                                g = 2 * gp + ge
                                for (bb, MS_, mk) in ((bR, MS2r, "MS2r"), (bI, MS2i, "MS2i")):
                                    mm(psf(bb)[64 * ge:64 * ge + 64, gl * NB:(gl + 1) * NB], MS_[:, gp * 128 + ge * 64:gp * 128 + ge * 64 + 64],
                                       U[:, g, uo:uo + NB], True, True, [mk, "U"], [PK[bb]])
                        for (bb, dst, nm, eng) in ((bR, SRs, "SRs", "act"), (bI, SIs, "SIs", "dve")):
                            src = psf(bb)[:, 0:8 * NB].rearrange("p (g j) -> p g j", g=8)
                            cp(eng, dst[:, gq * 8:(gq + 1) * 8, 0:NB], src, [PK[bb]], [nm])
                if chain:
                    wr = sbL(a2, "wr", (128, 32, 64)); wi = sbL(a2, "wi", (128, 32, 64))
                    t1 = sbL(a2, "t1", (128, 32, 64)); t2 = sbL(a2, "t2", (128, 32, 64))
                    hr = HRc[:, 0, :]; hi = HIc[:, 0, :]
                    tt(rt[0][:, 0, :], A8r[:], hr, ALU.mult, ["H0c", "A8", "rt0"], ["rt0"])
                    tt(rt[1][:, 0, :], A8i[:], hi, ALU.mult, ["H0c", "A8", "rt1"], ["rt1"])
                    tt(rt[2][:, 0, :], A8r[:], hi, ALU.mult, ["H0c", "A8", "rt2"], ["rt2"])
                    tt(rt[3][:, 0, :], A8i[:], hr, ALU.mult, ["H0c", "A8", "rt3"], ["rt3"])
                    tt(rt[0][:, 0, :], rt[0][:, 0, :], rt[1][:, 0, :], ALU.subtract, ["rt0", "rt1"], ["rt0"])
                    tt(rt[2][:, 0, :], rt[2][:, 0, :], rt[3][:, 0, :], ALU.add, ["rt2", "rt3"], ["rt2"])
                    tt(SRs[:, :, 0], SRs[:, :, 0], rt[0][:, 0, :], ALU.add, ["rt0", "SRs"], ["SRs"])
                    tt(SIs[:, :, 0], SIs[:, :, 0], rt[2][:, 0, :], ALU.add, ["rt2", "SIs"], ["SIs"])
                    for ge in range(2):
                        ts(HRb[ge][:, :, uo], hr, pm[:, ge:ge + 1], None, ALU.mult, None, ["H0c", "pm"], ["HRb"])
                        ts(HIb[ge][:, :, uo], hi, pm[:, ge:ge + 1], None, ALU.mult, None, ["H0c", "pm"], ["HIb"])
                    tt(t1[:], Cj[:], SRs[:], ALU.mult, ["Cj", "SRs", "t1"], ["t1"])
                    tt(t2[:], Sj[:], SIs[:], ALU.mult, ["Sj", "SIs", "t2"], ["t2"])
                    tt(wr[:], t1[:], t2[:], ALU.add, ["t1", "t2", "wr"], ["wr"])
                    tt(t1[:], Cj[:], SIs[:], ALU.mult, ["Cj", "SIs", "t1"], ["t1"])
                    tt(t2[:], Sj[:], SRs[:], ALU.mult, ["Sj", "SRs", "t2"], ["t2"])
                    tt(wi[:], t1[:], t2[:], ALU.subtract, ["t1", "t2", "wi"], ["wi"])
                    fl = lambda a: a[:].rearrange("p g j -> p (g j)")
                    kb.op("dve", lambda h: h.tensor_tensor_scan(out=fl(t1), data0=fl(D0), data1=fl(wr), initial=0.0, op0=ALU.mult, op1=ALU.add),
                          R=["D0", "wr", "t1"], W=["t1"])
                    kb.op("dve", lambda h: h.tensor_tensor_scan(out=fl(t2), data0=fl(D0), data1=fl(wi), initial=0.0, op0=ALU.mult, op1=ALU.add),
                          R=["D0", "wi", "t2"], W=["t2"])
                    tt(SRs[:], Cj[:], t1[:], ALU.mult, ["Cj", "t1", "SRs"], ["SRs"])
                    tt(SIs[:], Sj[:], t2[:], ALU.mult, ["Sj", "t2", "SIs"], ["SIs"])
                    tt(wr[:], SRs[:], SIs[:], ALU.subtract, ["SRs", "SIs", "wr"], ["wr"])
                    tt(SRs[:], Cj[:], t2[:], ALU.mult, ["Cj", "t2", "SRs"], ["SRs"])
                    tt(SIs[:], Sj[:], t1[:], ALU.mult, ["Sj", "t1", "SIs"], ["SIs"])
                    tt(wi[:], SRs[:], SIs[:], ALU.add, ["SRs", "SIs", "wi"], ["wi"])
                    for ge in range(2):
                        ts(HRb[ge][:, :, uo + 1:uo + NB], wr[:, :, 0:NB - 1], pm[:, ge:ge + 1], None, ALU.mult, None, ["wr", "pm"], ["HRb"])
                        ts(HIb[ge][:, :, uo + 1:uo + NB], wi[:, :, 0:NB - 1], pm[:, ge:ge + 1], None, ALU.mult, None, ["wi", "pm"], ["HIb"])
                    stage("a2x", dumps=[("Cj", Cj, ["Cj"]), ("Sj", Sj, ["Sj"]), ("D0", D0, ["D0"]), ("wr", wr, ["wr"]), ("t1", t1, ["t1"]), ("t2", t2, ["t2"]), ("SRs", SRs, ["SRs"])])
                    cp("dve", HRc[:, 1, :], wr[:, :, NB - 1], ["wr"], ["H64"])
                    cp("dve", HIc[:, 1, :], wi[:, :, NB - 1], ["wi"], ["H64"])
                else:
                    a8r = A8r[:].unsqueeze(1).to_broadcast([128, NB, 32]); a8i = A8i[:].unsqueeze(1).to_broadcast([128, NB, 32])
                    for ge in range(2):
                        ts(HRb[ge][:, :, uo:uo + NB], HRs[:].rearrange("p j g -> p g j"), pm[:, ge:ge + 1], None, ALU.mult, None, ["HRs", "pm"], ["HRb"])
                        ts(HIb[ge][:, :, uo:uo + NB], HIs[:].rearrange("p j g -> p g j"), pm[:, ge:ge + 1], None, ALU.mult, None, ["HIs", "pm"], ["HIb"])
                    tt(rt[0][:], a8r, HRs[:], ALU.mult, ["HRs", "A8", "rt0"], ["rt0"])
                    tt(rt[1][:], a8i, HIs[:], ALU.mult, ["HIs", "A8", "rt1"], ["rt1"])
                    tt(rt[2][:], a8r, HIs[:], ALU.mult, ["HIs", "A8", "rt2", "HIb"], ["rt2"])
                    tt(rt[3][:], a8i, HRs[:], ALU.mult, ["HRs", "A8", "rt3", "HRb"], ["rt3"])
                    tt(rt[0][:], rt[0][:], rt[1][:], ALU.subtract, ["rt0", "rt1"], ["rt0"])
                    tt(rt[2][:], rt[2][:], rt[3][:], ALU.add, ["rt2", "rt3"], ["rt2"])
                    tt(rt[0][:], rt[0][:], SRs[:, :, 0:NB].rearrange("p g j -> p j g"), ALU.add, ["rt0", "SRs"], ["rt0"])
                    tt(rt[2][:], rt[2][:], SIs[:, :, 0:NB].rearrange("p g j -> p j g"), ALU.add, ["rt2", "SIs"], ["rt2"])
                    for (src_t, sk, dst_o, ok) in ((rt[0], "rt0", o_rs, "ors"), (rt[2], "rt2", o_is, "ois")):
                        xo = xt[xcount["n"] % 2]; xok = f"xt{xcount['n'] % 2}"; xcount["n"] += 1
                        for i in range(4):
                            b = kb.ps()
                            tr(psf(b)[:, 0:128], src_t[:, 4 * i:4 * i + 4, :], ident_f[:], [sk, "ident_f"], [PK[b]])
                            cp("act" if i % 2 == 0 else "dve", xo[:, i * 128:(i + 1) * 128], psf(b)[:, 0:128], [PK[b]], [xok])
                        for i in range(4):
                            kb.dma("sp", dst_o.rearrange("b (gp ge) p -> (b gp) (ge p)", ge=2)[128 * i:128 * (i + 1), :],
                                   xo[:, i * 128:(i + 1) * 128], R=[xok], W=[ok], key=ok)

            passes = []
            if grp == 0:
                passes += [("pre", 0), ("pre", 1)]
            passes += [("main", grp)]
            if has_s:
                passes += [("samp", 0)]
            if grp == 0:
                kb.op("pool", lambda h: h.memset(HRc[:, 0, :], 0.0), W=["H0c"])
                kb.op("pool", lambda h: h.memset(HIc[:, 0, :], 0.0), W=["H0c"])
                for (src_d, dst_t, nm) in ((sre, HRs, "HRs"), (sim, HIs, "HIs")):
                    for i in range(4):
                        xtile, xk = load_x(src_d.rearrange("b (gp ge) p -> (b gp) (ge p)", ge=2)[128 * i:128 * (i + 1), :], 128, 128)
                        b = kb.ps()
                        tr(psf(b)[:, 0:128], xtile[:, 0:128], ident_f[:], [xk, "ident_f"], [PK[b]])
                        cp("act", dst_t[:, 4 * i:4 * i + 4, :], psf(b)[:, 0:128].rearrange("p (b g) -> p b g", b=4), [PK[b]], [nm])
            for (kind, idx) in passes:
                NB = 16 if kind == "samp" else 64
                uo = 64 if kind == "samp" else 0
                with ExitStack() as a1:
                    xnT = sbL(a1, "xnT2", (128, 16, 672), BF16)
                    ZT = sbL(a1, "ZT", (128, 64, 2, 64), BF16)
                    if kind == "pre":
                        tl = [(xp[idx * 512 + 128 * t: idx * 512 + 128 * (t + 1), :], 32 + 128 * t) for t in range(4)]
                        cbase = 32
                    elif kind == "main":
                        tl = [(xm[idx * 512 + 128 * t: idx * 512 + 128 * (t + 1), :], 32 + 128 * t) for t in range(4)]
                        cbase = 32
                    else:
                        tl = [(xs[:, :], 32 + 512)]
                        cbase = 32 + 512
                    P0_tiles(a1, tl, xnT)
                    run_A1(xnT, ZT, a1, NB, cbase, uo)
                    if grp == 0 and kind == "pre" and idx == 0:
                        stage("a1", dumps=[("concat", concat, ["concat"]), ("U", U, ["U"]), ("ZT", ZT, ["ZT"]), ("xnT", xnT, ["xnT"])])
                    if grp == 0 and kind == "main":
                        stage("a1m", dumps=[("U", U, ["U"]), ("ZT", ZT, ["ZT"]), ("xnT", xnT, ["xnT"])])
                barrier()
                with ExitStack() as a2:
                    SRs = sbL(a2, "SRs", (128, 32, 64)); SIs = sbL(a2, "SIs", (128, 32, 64))
                    run_A2(a2, SRs, SIs, NB, uo, chain=(kind != "samp"))
                    if grp == 0 and kind == "pre" and idx == 0:
                        stage("a2", dumps=[("concat", concat, ["concat"]), ("HRc", HRc, ["H64"]), ("HIc", HIc, ["H64"]), ("SRs", SRs, ["SRs"])])
                    if grp == 0 and kind == "main":
                        stage("a2m", dumps=[("concat", concat, ["concat"]), ("HRc", HRc, ["H64"]), ("HIc", HIc, ["H64"]), ("SRs", SRs, ["SRs"]), ("HRb0", HRb[0], ["HRb"])])
                    if kind != "samp":
                        cp("dve", HRc[:, 0, :], HRc[:, 1, :], ["H64", "HRb", "HIb"], ["H0c"])
                        cp("dve", HIc[:, 0, :], HIc[:, 1, :], ["H64", "HRb", "HIb"], ["H0c"])
                    if kind == "main" and grp == 1:
                        for (src_t, dst_o, ok) in ((HRc, o_rp, "orp"), (HIc, o_ip, "oip")):
                            xo = xt[xcount["n"] % 2]; xok = f"xt{xcount['n'] % 2}"; xcount["n"] += 1
                            b = kb.ps()
                            cp("dve", rt[0][:, 0, :], src_t[:, 1, :], ["H64", "rt0"], ["rt0"])
                            tr(psf(b)[:, 0:128], rt[0][:, 0:4, :], ident_f[:], ["rt0", "ident_f"], [PK[b]])
                            cp("act", xo[0:32, 0:128], psf(b)[0:32, 0:128], [PK[b]], [xok])
                            kb.dma("sp", dst_o.rearrange("(gp ge) p -> gp (ge p)", ge=2), xo[0:32, 0:128], R=[xok], W=[ok], key=ok)
                barrier()
                if kind == "pre":
                    continue
            a3 = ExitStack()
            if True:
                MY1rb = sbL(a3, "MY1rb_", (128, 4096), BF16); MY1ib = sbL(a3, "MY1ib_", (128, 4096), BF16)
                MY2 = sbL(a3, "MY2_", (128, 8192), BF16)
                YT = sbL(a3, "YT", (64, 8, 1024), BF16)
                gq_ = sbL(a3, "gq_", (128, 512)); gw_ = sbL(a3, "gw_", (128, 512))
                ld(MY1rb[:], d_MY1r[:, :], "cst", ["MY1rb"]); ld(MY1ib[:], d_MY1i[:, :], "cst", ["MY1ib"]); ld(MY2[:], d_MY2[:, :], "cst", ["MY2"])
                ypasses = [(64, 0, 0)] + ([(16, 64, 512)] if has_s else [])
                for (NB, uo, tbase) in ypasses:
                    for g4 in range(16):
                        b = kb.ps()
                        for gl in range(4):
                            g = g4 * 4 + gl
                            gp, ge = g // 2, g % 2
                            o = gl * 128
                            P0_, P1_ = 64 * ge, 64 * ge + 64
                            mm(psf(b)[0:NB, o:o + 128], U[:, g, uo:uo + NB], MY2[:, g * 128:(g + 1) * 128], True, False, ["U", "MY2"], [PK[b]])
                            mm(psf(b)[0:NB, o:o + 128], HRb[ge][:, gp, uo:uo + NB], MY1rb[:, gp * 128:(gp + 1) * 128], False, False, ["HRb", "MY1rb"], [PK[b]])
                            mm(psf(b)[0:NB, o:o + 128], HIb[ge][:, gp, uo:uo + NB], MY1ib[:, gp * 128:(gp + 1) * 128], False, True, ["HIb", "MY1ib"], [PK[b]])
                        src = psf(b)[0:NB, :].rearrange("j (g r c) -> j r g c", g=4, r=8)
                        dst = YT[0:NB, :, g4 * 64:(g4 + 1) * 64].rearrange("j r (g c) -> j r g c", g=4)
                        cp("act" if g4 % 2 == 0 else "dve", dst, src, [PK[b]], ["YT"])
                    for r in range(8):
                        b = kb.ps()
                        for q in range(8):
                            tr(psb(b)[:, q * NB:(q + 1) * NB], YT[0:NB, r, q * 128:(q + 1) * 128], ident_b[0:NB, 0:NB], ["YT", "ident_b"], [PK[b]])
                        yv = psb(b)[:, 0:8 * NB]
                        n8 = 8 * NB
                        actf(gq_[:, 0:n8], yv, AF.Square, [PK[b], "gq"], ["gq"])
                        ts(gq_[:, 0:n8], gq_[:, 0:n8], GC2, GC1, ALU.mult, ALU.add, ["gq"], ["gq"])
                        tt(gq_[:, 0:n8], gq_[:, 0:n8], yv, ALU.mult, ["gq", PK[b]], ["gq"])
                        actf(gw_[:, 0:n8], gq_[:, 0:n8], AF.Sigmoid, ["gq", "gw"], ["gw"])
                        dstv = ygT[:, :, tbase:tbase + 8 * NB].rearrange("p q (j e) -> p q e j", e=8)[:, :, r, :]
                        tt(dstv, gw_[:, 0:n8].rearrange("p (q j) -> p q j", q=8), yv.rearrange("p (q j) -> p q j", q=8), ALU.mult, ["gw", PK[b]], ["ygT"])
                if grp == 0:
                    stage("a3", dumps=[("concat", concat, ["concat"]), ("ygT", ygT, ["ygT"]), ("YT", YT, ["YT"])])
            with ExitStack() as g_:
                ys32 = sbL(g_, "ys32", (128, 8, 640))
                sg = sbL(g_, "sg2", (128, 512))
                ysq = sbL(g_, "ysq2", (128, 640), BF16)
                vt = sbL(g_, "vt2", (128, 640)); rs = sbL(g_, "rs2", (128, 640))
                loads = [[(lambda s: s[:, :, :], w_glu_v[:, :, q * 128:(q + 1) * 128])] for q in range(8)]
                wgl = Stream(g_, "wgl", 2, (128, 8, 128), loads)
                sblocks = [(0, 512, False)] + ([(512, 128, True)] if has_s else [])
                for q in range(8):
                    wslot, wk = wgl.get(q)
                    for (t0, n, is_s) in sblocks:
                        b = kb.ps()
                        for kc in range(8):
                            mm(psf(b)[:, 0:n], wslot[:, kc, :], ygT[:, kc, t0:t0 + n], kc == 0, kc == 7, [wk, "ygT"], [PK[b]])
                        actf(sg[:, 0:n], psf(b)[:, 0:n], AF.Sigmoid, [PK[b], "sg"], ["sg"])
                        tt(ys32[:, q, t0:t0 + n], ygT[:, q, t0:t0 + n], sg[:, 0:n], ALU.mult, ["ygT", "sg"], ["ys32"])
                fb = {}
                for (t0, n, is_s) in sblocks:
                    fb[t0] = kb.ps()
                for q in range(8):
                    for (t0, n, is_s) in sblocks:
                        actf(ysq[:, t0:t0 + n], ys32[:, q, t0:t0 + n], AF.Square, ["ys32", "ysq"], ["ysq"])
                        mm(psf(fb[t0])[:, 0:n], ones_b[:], ysq[:, t0:t0 + n], q == 0, q == 7, ["ones_b", "ysq"], [PK[fb[t0]]])
                for (t0, n, is_s) in sblocks:
                    ts(vt[:, t0:t0 + n], psf(fb[t0])[:, 0:n], 1.0 / DC, EPS, ALU.mult, ALU.add, [PK[fb[t0]], "vt"], ["vt"])
                    rsqrt(rs[:, t0:t0 + n], vt[:, t0:t0 + n], n, ["vt"], ["rs"])
                for q in range(8):
                    stt(concat[:, 8 + q, 0:NTOK], ys32[:, q, 0:NTOK], gns_t[:, q:q + 1], rs[:, 0:NTOK], ALU.mult, ALU.mult,
                        ["ys32", "gns", "rs"], ["concat"])
                if grp == 0:
                    stage("g", dumps=[("concat", concat, ["concat"]), ("ys32", ys32, ["ys32"])])
            barrier()
            a3.close()
        with ExitStack() as f_:
            hm = [sbL(f_, f"hm{t}", (128, D)) for t in range(NT)]
            hnT = sbL(f_, "hnT", (128, 16, 640), BF16)
            act = sbL(f_, "act", (128, 11, 640), BF16)
            sg = sbL(f_, "sg3", (128, 512)); tg = sbL(f_, "tg3", (128, 512))
            rows = []
            for t in range(NT):
                if t < 4:
                    rows.append((xm[grp * 512 + 128 * t: grp * 512 + 128 * (t + 1), :], o_ym[grp * 512 + 128 * t: grp * 512 + 128 * (t + 1), :]))
                else:
                    rows.append((xs[:, :], o_ys[:, :]))
            for t in range(NT):
                ld(hm[t][:], rows[t][0], f"hm{t}", [f"hm{t}"])
            with ExitStack() as wo_:
                loads = [[(lambda s: s[:, :, :], w_out_v[:, :, nb * 256:(nb + 1) * 256])] for nb in range(8)]
                wo = Stream(wo_, "wo", 2, (128, 16, 256), loads)
                for nb in range(8):
                    wslot, wk = wo.get(nb)
                    for t in range(NT):
                        b = kb.ps()
                        for kc in range(16):
                            mm(psf(b)[:, 0:256], concat[:, kc, t * 128:(t + 1) * 128], wslot[:, kc, :], kc == 0, kc == 15, ["concat", wk], [PK[b]])
                        tt(hm[t][:, nb * 256:(nb + 1) * 256], psf(b)[:, 0:256], hm[t][:, nb * 256:(nb + 1) * 256], ALU.add, [PK[b], f"hm{t}"], [f"hm{t}"])
            if grp == 0:
                stage("f1", dumps=[("concat", concat, ["concat"]), ("hm0", hm[0], ["hm0"]), ("hm4", hm[4], ["hm4"])])
            load_g(norm_ffn, "ffn")
            for t in range(NT):
                norm_T(hm[t][:], [f"hm{t}"], lambda hf, t=t: hnT[:, hf * 8:(hf + 1) * 8, t * 128:(t + 1) * 128], ["hnT"])
            if NTOK == 640:
                mblocks = [(0, 320), (320, 320)]
            else:
                mblocks = [(0, 512)]
            with ExitStack() as wf_:
                gl_loads = []
                for c2 in range(NCH // 2):
                    gl_loads.append([(lambda s: s[:, 0, :, :], w_g_v[:, :, c2 * 256:(c2 + 1) * 256]),
                                     (lambda s: s[:, 1, :, :], w_u_v[:, :, c2 * 256:(c2 + 1) * 256])])
                wgu = Stream(wf_, "wgu", 2, (128, 2, 16, 256), gl_loads)
                d_loads = []
                for qd in range(4):
                    for nb in range(4):
                        for (k0, kn) in ((0, 6), (6, 5)):
                            kc = qd * 11 + k0
                            d_loads.append([(lambda s, kn=kn: s[:, 0:kn, :], w_d_v[:, kc:kc + kn, nb * 512:(nb + 1) * 512])])
                wdn = Stream(wf_, "wdn", 2, (128, 6, 512), d_loads)
                di = 0
                for qd in range(4):
                    for c11 in range(11):
                        c = qd * 11 + c11
                        wslot, wk = wgu.get(c // 2)
                        co = (c % 2) * 128
                        for (c0, n) in mblocks:
                            bg = kb.ps(); bu = kb.ps()
                            for vg, bb in ((0, bg), (1, bu)):
                                for kc in range(16):
                                    mm(psf(bb)[:, 0:n], wslot[:, vg, kc, co:co + 128], hnT[:, kc, c0:c0 + n], kc == 0, kc == 15, [wk, "hnT"], [PK[bb]])
                            actf(sg[:, 0:n], psf(bg)[:, 0:n], AF.Sigmoid, [PK[bg], "sg"], ["sg"])
                            tt(tg[:, 0:n], psf(bg)[:, 0:n], sg[:, 0:n], ALU.mult, [PK[bg], "sg", "tg"], ["tg"])
                            tt(act[:, c11, c0:c0 + n], tg[:, 0:n], psf(bu)[:, 0:n], ALU.mult, ["tg", PK[bu]], ["act"])
                    for nb in range(4):
                        bs_ = [kb.ps() for _ in range(NT)]
                        for (k0, kn) in ((0, 6), (6, 5)):
                            wslot, wk = wdn.get(di); di += 1
                            for kk in range(kn):
                                k2 = k0 + kk
                                for t in range(NT):
                                    mm(psf(bs_[t])[:, :], act[:, k2, t * 128:(t + 1) * 128], wslot[:, kk, :], k2 == 0, k2 == 10, ["act", wk], [PK[bs_[t]]])
                        for t in range(NT):
                            tt(hm[t][:, nb * 512:(nb + 1) * 512], psf(bs_[t])[:, :], hm[t][:, nb * 512:(nb + 1) * 512], ALU.add,
                               [PK[bs_[t]], f"hm{t}"], [f"hm{t}"])
            load_g(norm_final, "final")
            for t in range(NT):
                actf(junk[:], hm[t][:], AF.Square, [f"hm{t}", "junk"], ["junk", "col0"], accum=col[:, 0:1])
                ts(col[:, 1:2], col[:, 0:1], 1.0 / D, EPS, ALU.mult, ALU.add, ["col0"], ["col1"])
                rsqrt(col[:, 2:3], col[:, 1:2], 1, ["col1"], ["col2"])
                stt(hm[t][:], hm[t][:], col[:, 2:3], gbc[:], ALU.mult, ALU.mult, [f"hm{t}", "col2", "gbc"], [f"hm{t}"])
                kb.dma("sp", rows[t][1], hm[t][:], R=[f"hm{t}"], W=[f"oy{grp}{t}"], key="oy")
            if not kb.dead:
                kb.eng["sp"]["h"].wait_ge(kb.sems["d_oy"], kb.dmas["d_oy"])
            if grp == 0:
                stage("f", dumps=[("hnT", hnT, ["hnT"])])
        barrier()
    for sname, val in kb.dmas.items():
        if sname.startswith("d_o"):
            kb.eng["sp"]["h"].wait_ge(kb.sems[sname], val)
    kb.barrier()
    nc._dbg_names = dbg_names
    return nc, es


_CACHE = {}


def kernel(**inputs):
    inp = {k: np.ascontiguousarray(np.asarray(v), dtype=np.float32) for k, v in inputs.items()}
    if "nc" not in _CACHE:
        _CACHE["nc"] = build_program()
    nc, _es = _CACHE["nc"]
    ident = np.eye(128, dtype=np.float32)
    rho = np.arange(128); s_of = rho // 16
    colr = np.arange(128) // 16
    mask2 = (colr[None, :] >= s_of[:, None]).astype(np.float32)
    ident2 = np.eye(128, dtype=np.float32)
    shared = dict(
        norm_mix=inp["norm_mix"][0], norm_ffn=inp["norm_ffn"][0], norm_final=inp["norm_final"],
        w_in=inp["w_in"][0], conv_w=inp["conv_w"][0], conv_b=inp["conv_b"][0], ln_g=inp["conv_ln_g"][0], ln_b=inp["conv_ln_b"][0],
        A_re=inp["ssm_A_re"][0], A_im=inp["ssm_A_im"][0], log_dt=inp["ssm_log_dt"][0],
        B_re=inp["ssm_B_re"][0], B_im=inp["ssm_B_im"][0], C_re=inp["ssm_C_re"][0], C_im=inp["ssm_C_im"][0], ssm_D=inp["ssm_D"][0],
        w_glu=inp["w_glu"][0], gn_c=inp["gnorm_conv"][0], gn_s=inp["gnorm_ssm"][0], w_out=inp["w_out"][0],
        w_g=inp["w_ffn_gate"][0], w_u=inp["w_ffn_up"][0], w_d=inp["w_ffn_down"][0],
        ident=ident, mask2=mask2, ident2=ident2, pmask=np.stack([(np.arange(128) < 64), (np.arange(128) >= 64)], 1).astype(np.float32))
    shared = {k: np.ascontiguousarray(v) for k, v in shared.items()}
    in_maps = []
    for c in range(8):
        b, half = c // 2, c % 2
        m = dict(shared)
        m["xm"] = np.ascontiguousarray(inp["x_prompt"][b, half * 1024:(half + 1) * 1024])
        m["xp"] = np.ascontiguousarray(inp["x_prompt"][b, 0:1024]) if half == 1 else np.zeros((1024, D), np.float32)
        m["xs"] = np.ascontiguousarray(inp["x_sample"][16 * c:16 * c + 16].reshape(128, D))
        m["stc"] = np.ascontiguousarray(inp["state_conv"][0, 16 * c:16 * c + 16])
        m["sre"] = np.ascontiguousarray(inp["state_ssm_re"][0, 16 * c:16 * c + 16])
        m["sim"] = np.ascontiguousarray(inp["state_ssm_im"][0, 16 * c:16 * c + 16])
        in_maps.append(m)
    res = run_bass_kernel_spmd(nc, in_maps, core_ids=list(range(8)))
    R = res.results
    y_prompt = np.zeros((4, 2048, D), np.float32); y_sample = np.zeros((128, 8, D), np.float32)
    ncp = np.zeros((1, 4, 30, DC), np.float32); nrp = np.zeros((1, 4, 64, 64), np.float32); nip = np.zeros((1, 4, 64, 64), np.float32)
    ncs = np.zeros((1, 128, 30, DC), np.float32); nrs = np.zeros((1, 128, 64, 64), np.float32); nis = np.zeros((1, 128, 64, 64), np.float32)
    for c in range(8):
        b, half = c // 2, c % 2
        y_prompt[b, half * 1024:(half + 1) * 1024] = R[c]["o_ym"]
        y_sample[16 * c:16 * c + 16] = R[c]["o_ys"].reshape(16, 8, D)
        ncs[0, 16 * c:16 * c + 16] = R[c]["o_cs"]; nrs[0, 16 * c:16 * c + 16] = R[c]["o_rs"]; nis[0, 16 * c:16 * c + 16] = R[c]["o_is"]
        if half == 1:
            ncp[0, b] = R[c]["o_cp"]; nrp[0, b] = R[c]["o_rp"]; nip[0, b] = R[c]["o_ip"]
    return (y_prompt, y_sample, ncp, nrp, nip, ncs, nrs, nis)
```

```python
import numpy as np
from contextlib import ExitStack
import concourse.bass as bass
import concourse.mybir as mybir
from concourse.bass_utils import run_bass_kernel_spmd

F32 = mybir.dt.float32
BF16 = mybir.dt.bfloat16
AF = mybir.ActivationFunctionType
ALU = mybir.AluOpType

D = 2048
DC = 1024
DFF = 5632
EPS = 1e-6
NCH = 44
SEM_CH = 12000
TWO_PI = 6.283185307179586
GC1 = 1.5957691216057308
GC2 = GC1 * 0.044715


class KB:
    def __init__(self, nc, es):
        self.nc = nc
        self.es = es
        self.eng = {}
        for name, h in (("pe", nc.tensor), ("act", nc.scalar), ("dve", nc.vector), ("pool", nc.gpsimd), ("sp", nc.sync)):
            self.eng[name] = dict(h=h, n=0, seen={}, sems=[])
        self.sems = {}
        self.lastw = {}
        self.readers = {}
        self.dmas = {}
        self.psn = 0
        self.dead = False
        self.reserved = set()
        self.fence = None

    def sem(self, name):
        if name not in self.sems:
            self.sems[name] = self.es.enter_context(self.nc.semaphore(name))
        return self.sems[name]

    def _wait(self, en, tok):
        e = self.eng[en]
        sname, val, pen, pidx = tok
        if pen == en:
            if en in ("pe", "sp"):
                return
            if e["n"] - pidx > 3:
                return
        if pen == "dma":
            val = self.dmas[sname]
        if e["seen"].get(sname, 0) >= val:
            return
        e["h"].wait_ge(self.sems[sname], val)
        e["seen"][sname] = val

    def _deps(self, en, R, W):
        toks = {}
        for k in R:
            t = self.lastw.get(k)
            if t is not None:
                toks[(t[0], t[2])] = max(toks.get((t[0], t[2]), (0, 0)), (t[1], t[3] if t[3] is not None else 0))
        for k in W:
            t = self.lastw.get(k)
            if t is not None:
                toks[(t[0], t[2])] = max(toks.get((t[0], t[2]), (0, 0)), (t[1], t[3] if t[3] is not None else 0))
            for t in self.readers.get(k, {}).values():
                toks[(t[0], t[2])] = max(toks.get((t[0], t[2]), (0, 0)), (t[1], t[3] if t[3] is not None else 0))
        for (sname, pen), (val, pidx) in toks.items():
            self._wait(en, (sname, val, pen, pidx))

    def _record(self, tok, R, W):
        for k in W:
            self.lastw[k] = tok
            self.readers[k] = {}
        for k in R:
            d = self.readers.setdefault(k, {})
            old = d.get(tok[0])
            if old is None or old[1] < tok[1]:
                d[tok[0]] = tok

    def op(self, en, fn, R=(), W=()):
        if self.dead:
            return None
        e = self.eng[en]
        self._deps(en, R, W)
        ins = fn(e["h"])
        si = e["n"] // SEM_CH
        sname = f"s_{en}{si}"
        ins.then_inc(self.sem(sname), 1)
        val = e["n"] % SEM_CH + 1
        tok = (sname, val, en, e["n"])
        e["n"] += 1
        self._record(tok, R, W)
        return tok

    def dma(self, qn, out, in_, R, W, key):
        if self.dead:
            return None
        e = self.eng[qn]
        fk = []
        for k in R:
            t = self.lastw.get(k)
            if t is not None and t[2] in ("act", "dve", "pool") and self.fence is not None:
                en_ = t[2]
                f = self.fence
                if en_ == "act":
                    self.op(en_, lambda h: h.copy(out=f[:, 0:1], in_=f[:, 1:2]), R=[k], W=["fence_" + en_])
                else:
                    self.op(en_, lambda h: h.tensor_copy(out=f[:, 2:3] if en_ == "dve" else f[:, 4:5], in_=f[:, 3:4] if en_ == "dve" else f[:, 5:6]),
                            R=[k], W=["fence_" + en_])
                fk.append("fence_" + en_)
        R = list(R) + fk
        self._deps(qn, R, W)
        sname = "d_" + key
        s = self.sem(sname)
        ins = e["h"].dma_start(out=out, in_=in_)
        ins.then_inc(s, 16)
        self.dmas[sname] = self.dmas.get(sname, 0) + 16
        tok = (sname, self.dmas[sname], "dma", None)
        self._record(tok, R, W)
        return tok

    def barrier(self):
        if self.dead:
            return
        toks = []
        for en, e in self.eng.items():
            if e["n"] > 0:
                last = e["n"] - 1
                toks.append((f"s_{en}{last // SEM_CH}", last % SEM_CH + 1, en, last))
        for sname, val in self.dmas.items():
            toks.append((sname, val, "dma", None))
        for en, e in self.eng.items():
            for (sname, val, pen, pidx) in toks:
                if pen == en:
                    continue
                if e["seen"].get(sname, 0) >= val:
                    continue
                e["h"].wait_ge(self.sems[sname], val)
                e["seen"][sname] = val
        self.lastw.clear()
        self.readers.clear()

    def ps(self, reserve=False):
        while True:
            i = self.psn
            self.psn = (self.psn + 1) % 8
            if i not in self.reserved:
                break
        if reserve:
            self.reserved.add(i)
        return i

    def release(self, i):
        self.reserved.discard(i)


class _Stop(Exception):
    pass


def build_program(stop=None):
    nc = bass.Bass("TRN2", target_bir_lowering=False)
    es = ExitStack()
    kb = KB(nc, es)

    def din(name, shape):
        return nc.dram_tensor(name, list(shape), F32, kind="ExternalInput").ap()

    def dout(name, shape):
        return nc.dram_tensor(name, list(shape), F32, kind="ExternalOutput").ap()

    xm = din("xm", (1024, D)); xp = din("xp", (1024, D)); xs = din("xs", (128, D))
    stc = din("stc", (16, 30, DC)); sre = din("sre", (16, 64, 64)); sim = din("sim", (16, 64, 64))
    norm_mix = din("norm_mix", (D,)); norm_ffn = din("norm_ffn", (D,)); norm_final = din("norm_final", (D,))
    w_in = din("w_in", (D, 3072)); conv_w = din("conv_w", (31, DC)); conv_b = din("conv_b", (DC,))
    ln_g = din("ln_g", (DC,)); ln_b = din("ln_b", (DC,))
    A_re = din("A_re", (64, 64)); A_im = din("A_im", (64, 64)); log_dt = din("log_dt", (64,))
    B_re = din("B_re", (64, 64, 16)); B_im = din("B_im", (64, 64, 16))
    C_re = din("C_re", (64, 16, 64)); C_im = din("C_im", (64, 16, 64)); ssm_D = din("ssm_D", (DC,))
    w_glu = din("w_glu", (DC, DC)); gn_c = din("gn_c", (DC,)); gn_s = din("gn_s", (DC,))
    w_out = din("w_out", (D, D)); w_g = din("w_g", (D, DFF)); w_u = din("w_u", (D, DFF)); w_d = din("w_d", (DFF, D))
    pmaskd = din("pmask", (128, 2))
    identd = din("ident", (128, 128)); maskd = din("mask2", (128, 128)); ident2d = din("ident2", (128, 128))

    o_ym = dout("o_ym", (1024, D)); o_ys = dout("o_ys", (128, D))
    o_cp = dout("o_cp", (30, DC)); o_rp = dout("o_rp", (64, 64)); o_ip = dout("o_ip", (64, 64))
    o_cs = dout("o_cs", (16, 30, DC)); o_rs = dout("o_rs", (16, 64, 64)); o_is = dout("o_is", (16, 64, 64))

    d_MS2r = nc.dram_tensor("d_MS2r", [128, 4096], BF16, kind="Internal").ap()
    d_MS2i = nc.dram_tensor("d_MS2i", [128, 4096], BF16, kind="Internal").ap()
    d_MY1r = nc.dram_tensor("d_MY1r", [128, 4096], BF16, kind="Internal").ap()
    d_MY1i = nc.dram_tensor("d_MY1i", [128, 4096], BF16, kind="Internal").ap()
    d_MY2 = nc.dram_tensor("d_MY2", [128, 8192], BF16, kind="Internal").ap()
    d_Cj = nc.dram_tensor("d_Cj", [128, 2048], F32, kind="Internal").ap()
    d_Sj = nc.dram_tensor("d_Sj", [128, 2048], F32, kind="Internal").ap()
    d_D0 = nc.dram_tensor("d_D0", [128, 2048], F32, kind="Internal").ap()

    uniq = {"n": 0}

    def sbL(stack, name, shape, dt=F32):
        uniq["n"] += 1
        return stack.enter_context(nc.sbuf_tensor(f"{name}_{uniq['n']}", list(shape), dt))

    def sb(name, shape, dt=F32):
        return sbL(es, name, shape, dt)

    psum = [es.enter_context(nc.psum_tensor(f"ps{i}", [128, 512], F32)) for i in range(8)]

    def psf(i):
        return psum[i][:]

    def psb(i):
        return psum[i][:].bitcast(BF16)

    PK = [f"ps{i}" for i in range(8)]

    def barrier():
        kb.barrier()

    dbg_names = []

    def stage(name, dumps=()):
        if stop != name or kb.dead:
            return
        for (dn, t_, keys) in dumps:
            shp = list(t_.shape)
            d_ = nc.dram_tensor("dbg_" + dn, shp, t_.dtype, kind="ExternalOutput").ap()
            idx = tuple(slice(None) for _ in shp)
            kb.dma("sp", d_[idx], t_[idx] if not isinstance(t_, bass.AP) else t_, R=keys, W=["dbg_" + dn], key="o_dbg")
            dbg_names.append("dbg_" + dn)
        kb.dead = True

    def cp(eng, out, in_, R, W):
        if eng == "act":
            return kb.op("act", lambda h: h.copy(out=out, in_=in_), R=R, W=W)
        return kb.op(eng, lambda h: h.tensor_copy(out=out, in_=in_), R=R, W=W)

    def tt(out, a, b, op, R, W, eng="dve"):
        return kb.op(eng, lambda h: h.tensor_tensor(out=out, in0=a, in1=b, op=op), R=R, W=W)

    def ts(out, a, s1, s2, op0, op1, R, W):
        if op1 is None:
            return kb.op("dve", lambda h: h.tensor_scalar(out=out, in0=a, scalar1=s1, scalar2=None, op0=op0), R=R, W=W)
        return kb.op("dve", lambda h: h.tensor_scalar(out=out, in0=a, scalar1=s1, scalar2=s2, op0=op0, op1=op1), R=R, W=W)

    def stt(out, a, sc, b, op0, op1, R, W):
        return kb.op("dve", lambda h: h.scalar_tensor_tensor(out=out, in0=a, scalar=sc, in1=b, op0=op0, op1=op1), R=R, W=W)

    def actf(out, in_, func, R, W, scale=None, bias=None, accum=None):
        kw = {}
        if scale is not None:
            kw["scale"] = scale
        if bias is not None:
            kw["bias"] = bias
        if accum is not None:
            kw["accum_out"] = accum
        return kb.op("act", lambda h: h.activation(out=out, in_=in_, func=func, **kw), R=R, W=W)

    def mm(out, lhsT, rhs, start, stop, R, W):
        return kb.op("pe", lambda h: h.matmul(out, lhsT=lhsT, rhs=rhs, start=start, stop=stop), R=R, W=W)

    def tr(out, in_, ident, R, W):
        return kb.op("pe", lambda h: h.transpose(out=out, in_=in_, identity=ident), R=R, W=W)

    def ld(dst, src, key, W, q="sp", R=()):
        if key in ("c0", "c_aa", "c_bb", "c_mk"):
            key = "k_" + W[0]
        return kb.dma(q, dst, src, R=R, W=W, key=key)

    ident_f = sb("ident_f", (128, 128)); ident_b = sb("ident_b", (128, 128), BF16)
    ones_b = sb("ones_b", (128, 128), BF16)
    neghalf = sb("neghalf", (128, 640))
    pm = sb("pm", (128, 2))
    fence_t = sb("fence_t", (128, 8))
    kb.op("pool", lambda h: h.memset(fence_t[:], 0.0), W=["fence_act", "fence_dve", "fence_pool"])
    kb.fence = fence_t
    cb_t = sb("cb_t", (128, 8)); lng_t = sb("lng_t", (128, 8)); lnb_t = sb("lnb_t", (128, 8))
    gnc_t = sb("gnc_t", (128, 8)); gns_t = sb("gns_t", (128, 8))
    cwT = sb("cwT", (128, 8, 31))
    A8r = sb("A8r", (128, 32)); A8i = sb("A8i", (128, 32))

    ld(ident_f[:], identd[:, :], "c0", ["ident_f"])
    ld(pm[:], pmaskd[:, :], "c0", ["pm"])
    cp("dve", ident_b[:], ident_f[:], ["ident_f"], ["ident_b"])
    kb.op("pool", lambda h: h.memset(ones_b[:], 1.0), W=["ones_b"])
    kb.op("pool", lambda h: h.memset(neghalf[:], -0.5), W=["neghalf"])
    with nc.allow_non_contiguous_dma(reason="small param vectors"):
        for t, v, nm in ((cb_t, conv_b, "cb"), (lng_t, ln_g, "lng"), (lnb_t, ln_b, "lnb"), (gnc_t, gn_c, "gnc"), (gns_t, gn_s, "gns")):
            ld(t[:], v.rearrange("(q p) -> p q", p=128), "c0", [nm])

    stage("s0", dumps=[("ident_b", ident_b, ["ident_b"]), ("cb", cb_t, ["cb"]), ("ones", ones_b, ["ones_b"])])

    nwt = sb("nwt", (128, 640))

    def rsqrt(dst, src, n, Rk, Wk):
        if n <= 8:
            kb.op("pool", lambda h: h.tensor_tensor(out=dst, in0=src, in1=neghalf[:, 0:n], op=ALU.pow), R=Rk + ["neghalf"], W=Wk)
            return
        actf(dst, src, AF.Sqrt, Rk + Wk, Wk)
        kb.op("dve", lambda h: h.reciprocal(out=dst, in_=dst), R=Wk, W=Wk)
        t = nwt[:, 0:n]
        for _ in range(2):
            tt(t, dst, dst, ALU.mult, Wk + ["nwt"], ["nwt"])
            tt(t, t, src, ALU.mult, Rk + ["nwt"], ["nwt"])
            ts(t, t, -0.5, 1.5, ALU.mult, ALU.add, ["nwt"], ["nwt"])
            tt(dst, dst, t, ALU.mult, Wk + ["nwt"], Wk)

    gstate = {"g": None}

    def load_g(vec, name):
        if gstate["g"] != name:
            ld(gbc[:], vec.partition_broadcast(128), "gbc", ["gbc"])
            gstate["g"] = name

    def norm_T(src_ap, src_keys, dst_fn, dst_keys):
        i_ = nt_cnt["n"] % 2; nt_cnt["n"] += 1
        xbn = xbs[i_]; xk_ = f"xb{i_}"
        c0_ = 3 * i_
        ck = [f"col{c0_}", f"col{c0_ + 1}", f"col{c0_ + 2}"]
        actf(junk[:], src_ap, AF.Square, src_keys + ["junk"], ["junk", ck[0]], accum=col[:, c0_:c0_ + 1])
        ts(col[:, c0_ + 1:c0_ + 2], col[:, c0_:c0_ + 1], 1.0 / D, EPS, ALU.mult, ALU.add, [ck[0]], [ck[1]])
        rsqrt(col[:, c0_ + 2:c0_ + 3], col[:, c0_ + 1:c0_ + 2], 1, [ck[1]], [ck[2]])
        stt(xbn[:], src_ap, col[:, c0_ + 2:c0_ + 3], gbc[:], ALU.mult, ALU.mult, src_keys + [ck[2], "gbc", xk_], [xk_])
        for hf in range(2):
            b = kb.ps()
            for k8 in range(8):
                kc = hf * 8 + k8
                tr(psb(b)[:, k8 * 128:(k8 + 1) * 128], xbn[:, kc * 128:(kc + 1) * 128], ident_b[:], [xk_, "ident_b"], [PK[b]])
            src = psb(b)[:, 0:1024].rearrange("p (a r) -> p a r", a=8)
            cp("act" if hf == 0 else "dve", dst_fn(hf), src, [PK[b]], dst_keys)

    xcount = {"n": 0}

    def load_x(rows_ap, r=128, c=D):
        s = xcount["n"] % 2
        xcount["n"] += 1
        ld(xt[s][0:r, 0:c], rows_ap, f"xt{s}", [f"xt{s}"])
        return xt[s], f"xt{s}"

    class Stream:
        def __init__(self, stack, name, nslots, shape, loads):
            self.name = name; self.n = nslots
            self.slots = [sbL(stack, f"{name}{i}", shape, BF16) for i in range(nslots)]
            self.loads = loads; self.issued = 0

        def ensure(self, upto):
            while self.issued <= min(upto, len(self.loads) - 1):
                i = self.issued; s = i % self.n
                for (dst_fn, src) in self.loads[i]:
                    kb.dma("pool", dst_fn(self.slots[s]), src, R=(), W=[f"{self.name}{s}"], key=f"{self.name}{s}")
                self.issued += 1

        def get(self, i):
            self.ensure(i + self.n - 1)
            s = i % self.n
            return self.slots[s], f"{self.name}{s}"

    w_in_v = w_in.rearrange("(kc p) n -> p kc n", p=128)
    w_out_v = w_out.rearrange("(kc p) n -> p kc n", p=128)
    w_glu_v = w_glu.rearrange("(kc p) n -> p kc n", p=128)
    w_g_v = w_g.rearrange("(kc p) n -> p kc n", p=128)
    w_u_v = w_u.rearrange("(kc p) n -> p kc n", p=128)
    w_d_v = w_d.rearrange("(kc p) n -> p kc n", p=128)


    with ExitStack() as ses:
        def sbt(name, shape, dt=F32):
            return sbL(ses, name, shape, dt)
        tA = sbt("tA", (128, 128)); tB = sbt("tB", (128, 128))
        twopi = sbt("twopi", (128, 32))
        ArT = sbt("ArT", (128, 32)); AiT = sbt("AiT", (128, 32)); LdT = sbt("LdT", (128, 32))
        s1 = sbt("s1", (128, 32)); s2_ = sbt("s2_", (128, 32)); s3 = sbt("s3", (128, 32)); s4 = sbt("s4", (128, 32))
        abr = sbt("abr", (128, 32)); abi = sbt("abi", (128, 32)); kr = sbt("kr", (128, 32)); ki = sbt("ki", (128, 32))
        ivr = sbt("ivr", (128, 32)); ivi = sbt("ivi", (128, 32))
        pwr = sbt("pwr", (128, 32, 8)); pwi = sbt("pwi", (128, 32, 8)); nwr = sbt("nwr", (128, 32, 8)); nwi = sbt("nwi", (128, 32, 8))
        CrT = sbt("CrT", (128, 32, 16)); CiT = sbt("CiT", (128, 32, 16))
        Br = sbt("Br", (128, 32, 16)); Bi = sbt("Bi", (128, 32, 16)); Bbr = sbt("Bbr", (128, 32, 16)); Bbi = sbt("Bbi", (128, 32, 16))
        t16a = sbt("t16a", (128, 32, 16)); t16b = sbt("t16b", (128, 32, 16))
        Qr = sbt("Qr", (128, 32, 128)); Qi = sbt("Qi", (128, 32, 128))
        Mr = sbt("Mr", (128, 32, 128)); Mi = sbt("Mi", (128, 32, 128))
        Y1r = sbt("Y1r", (128, 32, 128)); Y1i = sbt("Y1i", (128, 32, 128))
        big = sbt("big", (128, 32, 128))
        mask2 = sbt("mask2", (128, 128)); ident2 = sbt("ident2", (128, 128)); Dcol = sbt("Dcol", (128, 64))
        cwl = sbt("cwl", (32, DC))
        tmy = sbt("tmy", (128, 2, 128))
        MS2r = sbt("MS2r", (128, 4096), BF16); MS2i = sbt("MS2i", (128, 4096), BF16)
        MY1rb = sbt("MY1rb", (128, 4096), BF16); MY1ib = sbt("MY1ib", (128, 4096), BF16)
        MY2 = sbt("MY2", (128, 8192), BF16)

        ld(mask2[:], maskd[:, :], "c_mk", ["mask2"])
        ld(ident2[:], ident2d[:, :], "c_mk", ["ident2"])
        with nc.allow_non_contiguous_dma(reason="small param vectors"):
            for m in range(8):
                ld(Dcol[16 * m:16 * m + 16, :], ssm_D.rearrange("(g c) -> c g", c=16), "c_dc", ["Dcol"])
            ld(tmy[:, 0, 0:64], log_dt.partition_broadcast(128), "c_ld", ["tmy"])
            for ge in range(2):
                cp("dve", LdT[64 * ge:64 * ge + 64, :], tmy[64 * ge:64 * ge + 64, 0, ge:64:2], ["tmy"], ["LdT"])
            for ge in range(2):
                ld(Br[64 * ge:64 * ge + 64, :, :], B_re.rearrange("(gp ge) p c -> ge p gp c", ge=2)[ge], "c_bb", ["Br"])
                ld(Bi[64 * ge:64 * ge + 64, :, :], B_im.rearrange("(gp ge) p c -> ge p gp c", ge=2)[ge], "c_bb", ["Bi"])
        with nc.allow_non_contiguous_dma(reason="one-time conv weight transpose load"):
            for q in range(8):
                ld(cwT[:, q, :], conv_w[:, q * 128:(q + 1) * 128].rearrange("k p -> p k"), "c_cw", ["cwT"])
        stage("s1", dumps=[("cwT", cwT, ["cwT"]), ("Br", Br, ["Br"]), ("LdT", LdT, ["LdT"]), ("Dcol", Dcol, ["Dcol"])])
        for (src, dst, nm, tAx, tk) in ((A_re, ArT, "ArT", tA, "tA"), (A_im, AiT, "AiT", tB, "tBA")):
            kb.op("pool", lambda h, tAx=tAx: h.memset(tAx[:], 0.0), W=[tk])
            ld(tAx[0:32, :], src.rearrange("(gp ge) p -> gp (ge p)", ge=2), "c_aa", [tk])
            b = kb.ps()
            tr(psf(b)[:, 0:128], tAx[:, :], ident_f[:], [tk, "ident_f"], [PK[b]])
            cp("act", dst[:], psf(b)[:, 0:32], [PK[b]], [nm])
        tBs = [sbt(f"tB{i}", (128, 128)) for i in range(8)]
        ti = 0
        for (src, dst, nm) in ((C_re, CrT, "CrT"), (C_im, CiT, "CiT")):
            for blk in range(4):
                tBx = tBs[ti]; tk = f"tB{ti}"; ti += 1
                for gl in range(8):
                    gp = blk * 8 + gl
                    ld(tBx[16 * gl:16 * gl + 16, :].rearrange("c (ge p) -> c ge p", ge=2),
                       src[2 * gp:2 * gp + 2].rearrange("ge c p -> c ge p"), "c_" + tk, [tk])
                b = kb.ps()
                tr(psf(b)[:, 0:128], tBx[:, :], ident_f[:], [tk, "ident_f"], [PK[b]])
                cp("act", dst[:, blk * 8:(blk + 1) * 8, :], psf(b)[:, 0:128].rearrange("p (g c) -> p g c", c=16), [PK[b]], [nm])
        PI = float(np.pi)

        def range_reduce(shift):
            ts(s4[:], s3[:], shift, None, ALU.add, None, ["ang", "abi", "abr", "s4"], ["s4"])
            ts(twopi[:], s4[:], -PI, None, ALU.add, None, ["s4", "twopi"], ["twopi"])
            for m in range(1, 6):
                ts(kr[:], s4[:], TWO_PI * m, -TWO_PI, ALU.is_ge, ALU.mult, ["s4", "kr"], ["kr"])
                tt(twopi[:], twopi[:], kr[:], ALU.add, ["twopi", "kr"], ["twopi"])
            cp("dve", s4[:], twopi[:], ["twopi"], ["s4"])
        stage("s2", dumps=[("cwT", cwT, ["cwT"]), ("ArT", ArT, ["ArT"]), ("AiT", AiT, ["AiT"]), ("CrT", CrT, ["CrT"]), ("CiT", CiT, ["CiT"])])
        import math as _m

        def horner(dst, zz, cs, R, W):
            ts(dst, zz, float(cs[-1]), None, ALU.mult, None, R + W, W)
            for c in reversed(cs[1:-1]):
                stt(dst, dst, float(c), zz, ALU.add, ALU.mult, R + W, W)
            ts(dst, dst, float(cs[0]), None, ALU.add, None, W, W)

        ecs = [1.0 / _m.factorial(k) for k in range(11)]
        ts(kr[:], LdT[:], 0.125, None, ALU.mult, None, ["LdT", "kr"], ["kr"])
        horner(s1[:], kr[:], ecs, ["kr"], ["dt"])
        for _ in range(3):
            tt(s1[:], s1[:], s1[:], ALU.mult, ["dt"], ["dt"])
        tt(kr[:], s1[:], ArT[:], ALU.mult, ["dt", "ArT", "kr"], ["kr"])
        horner(s2_[:], kr[:], ecs[:8], ["kr"], ["mag"])
        tt(s3[:], s1[:], AiT[:], ALU.mult, ["dt", "AiT"], ["ang"])
        range_reduce(PI)
        ts(s4[:], s4[:], 0.25, None, ALU.mult, None, ["s4"], ["s4"])
        tt(ki[:], s4[:], s4[:], ALU.mult, ["s4", "ki"], ["ki"])
        horner(abi[:], ki[:], [1.0, -1.0 / 6, 1.0 / 120, -1.0 / 5040, 1.0 / 362880], ["ki"], ["abi"])
        tt(abi[:], abi[:], s4[:], ALU.mult, ["abi", "s4"], ["abi"])
        horner(abr[:], ki[:], [1.0, -0.5, 1.0 / 24, -1.0 / 720, 1.0 / 40320, -1.0 / 3628800], ["ki"], ["abr"])
        for _ in range(2):
            tt(kr[:], abi[:], abr[:], ALU.mult, ["abi", "abr", "kr"], ["kr"])
            tt(ki[:], abi[:], abi[:], ALU.mult, ["abi", "ki"], ["ki"])
            ts(abi[:], kr[:], 2.0, None, ALU.mult, None, ["kr", "abi"], ["abi"])
            ts(abr[:], ki[:], -2.0, 1.0, ALU.mult, ALU.add, ["ki", "abr"], ["abr"])
        tt(abr[:], abr[:], s2_[:], ALU.mult, ["abr", "mag"], ["abr"])
        tt(abi[:], abi[:], s2_[:], ALU.mult, ["abi", "mag"], ["abi"])
        tt(s1[:], s2_[:], s2_[:], ALU.mult, ["mag", "dt"], ["m2"])
        kb.op("dve", lambda h: h.reciprocal(out=s1[:], in_=s1[:]), R=["m2"], W=["m2"])
        tt(ivr[:], abr[:], s1[:], ALU.mult, ["abr", "m2"], ["ivr"])
        tt(ivi[:], abi[:], s1[:], ALU.mult, ["abi", "m2"], ["ivi"])
        ts(ivi[:], ivi[:], -1.0, None, ALU.mult, None, ["ivi"], ["ivi"])
        tt(s1[:], ArT[:], ArT[:], ALU.mult, ["ArT", "ivr", "ivi", "m2"], ["den"])
        tt(s4[:], AiT[:], AiT[:], ALU.mult, ["AiT", "abr"], ["s4"])
        tt(s1[:], s1[:], s4[:], ALU.add, ["den", "s4"], ["den"])
        kb.op("dve", lambda h: h.reciprocal(out=s1[:], in_=s1[:]), R=["den"], W=["den"])
        ts(s3[:], abr[:], -1.0, None, ALU.add, None, ["abr", "ang", "s4"], ["nr"])
        tt(kr[:], s3[:], ArT[:], ALU.mult, ["nr", "ArT"], ["kr"])
        tt(s4[:], abi[:], AiT[:], ALU.mult, ["abi", "AiT", "den"], ["s4"])
        tt(kr[:], kr[:], s4[:], ALU.add, ["kr", "s4"], ["kr"])
        tt(kr[:], kr[:], s1[:], ALU.mult, ["kr", "den"], ["kr"])
        tt(ki[:], abi[:], ArT[:], ALU.mult, ["abi", "ArT"], ["ki"])
        tt(s4[:], s3[:], AiT[:], ALU.mult, ["nr", "AiT", "kr"], ["s4"])
        tt(ki[:], ki[:], s4[:], ALU.subtract, ["ki", "s4"], ["ki"])
        tt(ki[:], ki[:], s1[:], ALU.mult, ["ki", "den"], ["ki"])

        stage("s3", dumps=[("cwT", cwT, ["cwT"]), ("abr", abr, ["abr"]), ("abi", abi, ["abi"]), ("kr", kr, ["kr"]), ("ki", ki, ["ki"]), ("ivr", ivr, ["ivr"]), ("ivi", ivi, ["ivi"])])

        def cmul_into(dr, di, ar_, ai_, br_, bi_, R, W):
            tt(s2_[:], ar_, br_, ALU.mult, R + ["tmpa", "mag", "s2"], ["tmpa"])
            tt(s4[:], ai_, bi_, ALU.mult, R + ["tmpb", "s4", "ki"], ["tmpb"])
            tt(dr, s2_[:], s4[:], ALU.subtract, ["tmpa", "tmpb"], W)
            tt(s2_[:], ar_, bi_, ALU.mult, R + ["tmpa"] + W, ["tmpa"])
            tt(s4[:], ai_, br_, ALU.mult, R + ["tmpb"] + W, ["tmpb"])
            tt(di, s2_[:], s4[:], ALU.add, ["tmpa", "tmpb"], W)

        cp("dve", pwr[:, :, 0], abr[:], ["abr", "mag"], ["pw"])
        cp("dve", pwi[:, :, 0], abi[:], ["abi"], ["pw"])
        cp("dve", nwr[:, :, 0], ivr[:], ["ivr"], ["nw"])
        cp("dve", nwi[:, :, 0], ivi[:], ["ivi"], ["nw"])
        for k in range(1, 8):
            cmul_into(pwr[:, :, k], pwi[:, :, k], pwr[:, :, k - 1], pwi[:, :, k - 1], abr[:], abi[:], ["pw", "abr", "abi"], ["pw"])
            cmul_into(nwr[:, :, k], nwi[:, :, k], nwr[:, :, k - 1], nwi[:, :, k - 1], ivr[:], ivi[:], ["nw", "ivr", "ivi"], ["nw"])
        cp("dve", A8r[:], pwr[:, :, 7], ["pw"], ["A8"])
        cp("dve", A8i[:], pwi[:, :, 7], ["pw"], ["A8"])
        Tc = big[:, :, 0:64]; Ts = big[:, :, 64:128]; D0 = Mi[:, :, 0:64]; Tt = Mi[:, :, 64:96]
        rho = sbt("rho", (128, 32)); ur = sbt("ur", (128, 32)); ui = sbt("ui", (128, 32)); u2r = sbt("u2r", (128, 32)); u2i = sbt("u2i", (128, 32))
        tt(rho[:], A8r[:], A8r[:], ALU.mult, ["A8", "rho"], ["rho"])
        tt(u2r[:], A8i[:], A8i[:], ALU.mult, ["A8"], ["u2r"])
        tt(rho[:], rho[:], u2r[:], ALU.add, ["rho", "u2r"], ["rho"])
        kb.op("dve", lambda h: h.reciprocal(out=u2i[:], in_=rho[:]), R=["rho"], W=["u2i"])
        kb.op("pool", lambda h: h.memset(ur[:], 1.0), W=["ur"])
        for _ in range(7):
            tt(ui[:], ur[:], ur[:], ALU.mult, ["ur", "ui"], ["ui"])
            tt(ui[:], ui[:], rho[:], ALU.mult, ["ui", "rho"], ["ui"])
            ts(ui[:], ui[:], -0.5, 1.5, ALU.mult, ALU.add, ["ui"], ["ui"])
            tt(ur[:], ur[:], ui[:], ALU.mult, ["ur", "ui"], ["ur"])
        tt(rho[:], rho[:], ur[:], ALU.mult, ["rho", "ur"], ["rho"])
        tt(ui[:], A8i[:], ur[:], ALU.mult, ["A8", "ur", "ui"], ["ui"])
        tt(ur[:], A8r[:], ur[:], ALU.mult, ["A8", "ur"], ["ur"])
        kb.op("pool", lambda h: h.memset(Tc[:, :, 0:1], 1.0), W=["big"])
        kb.op("pool", lambda h: h.memset(Ts[:, :, 0:1], 0.0), W=["big"])
        cp("dve", u2r[:], ur[:], ["ur", "u2r", "rho"], ["u2"]); cp("dve", u2i[:], ui[:], ["ui", "u2i"], ["u2"])
        n_ = 1
        while n_ < 64:
            def bj(a):
                return a.unsqueeze(2).to_broadcast([128, 32, n_])
            tt(Tt[:, :, 0:n_], Tc[:, :, 0:n_], bj(u2r[:]), ALU.mult, ["big", "u2", "Mi"], ["Mi"])
            tt(Tc[:, :, n_:2 * n_], Ts[:, :, 0:n_], bj(u2i[:]), ALU.mult, ["big", "u2", "big"], ["big"])
            tt(Tc[:, :, n_:2 * n_], Tt[:, :, 0:n_], Tc[:, :, n_:2 * n_], ALU.subtract, ["Mi", "big"], ["big"])
            tt(Tt[:, :, 0:n_], Tc[:, :, 0:n_], bj(u2i[:]), ALU.mult, ["big", "u2", "Mi"], ["Mi"])
            tt(Ts[:, :, n_:2 * n_], Ts[:, :, 0:n_], bj(u2r[:]), ALU.mult, ["big", "u2"], ["big"])
            tt(Ts[:, :, n_:2 * n_], Tt[:, :, 0:n_], Ts[:, :, n_:2 * n_], ALU.add, ["Mi", "big"], ["big"])
            tt(Tt[:, :, 0], u2r[:], u2r[:], ALU.mult, ["u2", "Mi"], ["Mi"])
            tt(Tt[:, :, 1], u2i[:], u2i[:], ALU.mult, ["u2", "Mi"], ["Mi"])
            tt(Tt[:, :, 2], u2r[:], u2i[:], ALU.mult, ["u2", "Mi"], ["Mi"])
            tt(u2r[:], Tt[:, :, 0], Tt[:, :, 1], ALU.subtract, ["Mi", "u2"], ["u2"])
            ts(u2i[:], Tt[:, :, 2], 2.0, None, ALU.mult, None, ["Mi", "u2"], ["u2"])
            n_ *= 2
        cp("dve", D0, rho[:].unsqueeze(2).to_broadcast([128, 32, 64]), ["rho"], ["Mi"])
        kb.op("pool", lambda h: h.memset(D0[:, :, 0:1], 0.0), R=["Mi"], W=["Mi"])
        stage("s4b", dumps=[("big", big, ["big"]), ("Mi", Mi, ["Mi"]), ("rho", rho, ["rho"]), ("ur", ur, ["ur"]), ("ui", ui, ["ui"])])
        for (t_, d_, nm) in ((Tc, d_Cj, "big"), (Ts, d_Sj, "big"), (D0, d_D0, "Mi")):
            kb.dma("sp", d_.rearrange("p (g j) -> p g j", j=64), t_, R=[nm], W=["dstash"], key="stash")

        stage("s4", dumps=[("cwT", cwT, ["cwT"]), ("pwr", pwr, ["pw"]), ("pwi", pwi, ["pw"]), ("nwr", nwr, ["nw"]), ("nwi", nwi, ["nw"])])

        def bc16(a):
            return a.unsqueeze(2).to_broadcast([128, 32, 16])
        tt(t16a[:], Br[:], bc16(kr[:]), ALU.mult, ["Br", "kr"], ["t16a"])
        tt(t16b[:], Bi[:], bc16(ki[:]), ALU.mult, ["Bi", "ki"], ["t16b"])
        tt(Bbr[:], t16a[:], t16b[:], ALU.subtract, ["t16a", "t16b"], ["Bbr"])
        tt(t16a[:], Bi[:], bc16(kr[:]), ALU.mult, ["Bi", "kr", "Bbr"], ["t16a"])
        tt(t16b[:], Br[:], bc16(ki[:]), ALU.mult, ["Br", "ki", "Bbr"], ["t16b"])
        tt(Bbi[:], t16a[:], t16b[:], ALU.add, ["t16a", "t16b"], ["Bbi"])

        def v4(t):
            return t[:].rearrange("p g (s c) -> p g s c", c=16)

        def bs(t):
            return t[:].unsqueeze(3).to_broadcast([128, 32, 8, 16])

        def bcc(t):
            return t[:].unsqueeze(2).to_broadcast([128, 32, 8, 16])
        tt(v4(Qr), bs(nwr), bcc(Bbr), ALU.mult, ["nw", "Bbr"], ["Qr"])
        tt(v4(big), bs(nwi), bcc(Bbi), ALU.mult, ["nw", "Bbi"], ["big"])
        tt(Qr[:], Qr[:], big[:], ALU.subtract, ["Qr", "big"], ["Qr"])
        tt(v4(Qi), bs(nwr), bcc(Bbi), ALU.mult, ["nw", "Bbi"], ["Qi"])
        tt(v4(big), bs(nwi), bcc(Bbr), ALU.mult, ["nw", "Bbr", "Qr"], ["big"])
        tt(Qi[:], Qi[:], big[:], ALU.add, ["Qi", "big"], ["Qi"])

        def b128(a):
            return a.unsqueeze(2).to_broadcast([128, 32, 128])
        tt(Mr[:], Qr[:], b128(A8r[:]), ALU.mult, ["Qr", "A8"], ["Mr"])
        tt(big[:], Qi[:], b128(A8i[:]), ALU.mult, ["Qi", "A8"], ["big"])
        tt(Mr[:], Mr[:], big[:], ALU.subtract, ["Mr", "big"], ["Mr"])
        tt(Mi[:], Qi[:], b128(A8r[:]), ALU.mult, ["Qi", "A8"], ["Mi"])
        tt(big[:], Qr[:], b128(A8i[:]), ALU.mult, ["Qr", "A8", "Mr"], ["big"])
        tt(Mi[:], Mi[:], big[:], ALU.add, ["Mi", "big"], ["Mi"])
        tt(v4(Y1r), bs(pwr), bcc(CrT), ALU.mult, ["pw", "CrT"], ["Y1r"])
        tt(v4(big), bs(pwi), bcc(CiT), ALU.mult, ["pw", "CiT", "Mi"], ["big"])
        tt(Y1r[:], Y1r[:], big[:], ALU.subtract, ["Y1r", "big"], ["Y1r"])
        tt(v4(Y1i), bs(pwi), bcc(CrT), ALU.mult, ["pw", "CrT"], ["Y1i"])
        tt(v4(big), bs(pwr), bcc(CiT), ALU.mult, ["pw", "CiT", "Y1r"], ["big"])
        tt(Y1i[:], Y1i[:], big[:], ALU.add, ["Y1i", "big"], ["Y1i"])
        ts(Y1i[:], Y1i[:], -1.0, None, ALU.mult, None, ["Y1i"], ["Y1i"])
        cp("act", MY1rb[:].rearrange("p (g n) -> p g n", n=128), Y1r[:], ["Y1r"], ["MY1rb"])
        cp("act", MY1ib[:].rearrange("p (g n) -> p g n", n=128), Y1i[:], ["Y1i"], ["MY1ib"])
        stage("s5", dumps=[("cwT", cwT, ["cwT"]), ("Qr", Qr, ["Qr"]), ("Qi", Qi, ["Qi"]), ("Mr", Mr, ["Mr"]), ("Y1r", Y1r, ["Y1r"]), ("Y1i", Y1i, ["Y1i"])])
        for gp in range(32):
            b = kb.ps()
            for ri, M_ in enumerate((Mr, Mi)):
                tr(psf(b)[:, ri * 128:(ri + 1) * 128], M_[:, gp, :], ident_f[:], ["Mr", "Mi", "ident_f"], [PK[b]])
            cp("act", MS2r[:, gp * 128:(gp + 1) * 128], psf(b)[:, 0:128], [PK[b]], ["MS2r"])
            cp("act", MS2i[:, gp * 128:(gp + 1) * 128], psf(b)[:, 128:256], [PK[b]], ["MS2i"])
        stage("s6", dumps=[("cwT", cwT, ["cwT"]), ("MS2r", MS2r, ["MS2r"]), ("MS2i", MS2i, ["MS2i"])])
        MrB = Mr[:].bitcast(BF16); MiB = Mi[:].bitcast(BF16); bigB = big[:].bitcast(BF16)
        cp("act", MrB[:, :, 0:128], Qr[:], ["Qr", "Mr", "MS2r", "MS2i"], ["Mr"])
        cp("act", MrB[:, :, 128:256], Qi[:], ["Qi", "Mr"], ["Mr"])
        for ge in range(2):
            ts(MiB[:, :, ge * 128:(ge + 1) * 128], Y1r[:], pm[:, ge:ge + 1], None, ALU.mult, None, ["Y1r", "pm", "Mi", "MS2r", "MS2i"], ["Mi"])
            ts(bigB[:, :, ge * 128:(ge + 1) * 128], Y1i[:], pm[:, ge:ge + 1], None, ALU.mult, None, ["Y1i", "pm", "big", "MY1ib", "MY1rb"], ["big"])
        for gp in range(32):
            bge = (kb.ps(), kb.ps())
            for ge in range(2):
                mm(psf(bge[ge])[:, 0:128], MrB[:, gp, 0:128], MiB[:, gp, ge * 128:(ge + 1) * 128], True, False, ["Mr", "Mi"], [PK[bge[ge]]])
                mm(psf(bge[ge])[:, 0:128], MrB[:, gp, 128:256], bigB[:, gp, ge * 128:(ge + 1) * 128], False, True, ["Mr", "big"], [PK[bge[ge]]])
                tt(tmy[:, ge, :], psf(bge[ge])[:, 0:128], mask2[:], ALU.mult, [PK[bge[ge]], "mask2", "tmy"], ["tmy"])
            for ge in range(2):
                g = 2 * gp + ge
                stt(MY2[:, g * 128:(g + 1) * 128], ident2[:], Dcol[:, g:g + 1], tmy[:, ge, :], ALU.mult, ALU.add,
                    ["tmy", "ident2", "Dcol"], ["MY2"])
        stage("setup", dumps=[("A8r", A8r, ["A8"]), ("A8i", A8i, ["A8"]), ("MS2r", MS2r, ["MS2r"]), ("MS2i", MS2i, ["MS2i"]), ("MY2", MY2, ["MY2"]),
                              ("MY1rb", MY1rb, ["MY1rb"]), ("cwT", cwT, ["cwT"]), ("kr", kr, ["kr"]), ("abr", abr, ["abr"]), ("abi", abi, ["abi"])])
        for (t_, d_, nm) in ((MS2r, d_MS2r, "MS2r"), (MS2i, d_MS2i, "MS2i"), (MY1rb, d_MY1r, "MY1rb"), (MY1ib, d_MY1i, "MY1ib"), (MY2, d_MY2, "MY2")):
            kb.dma("sp", d_[:, :], t_[:], R=[nm], W=["dstash"], key="stash")
        if not kb.dead:
            kb.eng["act"]["h"].wait_ge(kb.sems["d_stash"], kb.dmas["d_stash"])
        barrier()

    gbc = sb("gbc", (128, D))
    xt = [sb(f"xt{i}", (128, D)) for i in range(2)]
    xbs = [sb(f"xb{i}", (128, D), BF16) for i in range(2)]
    junk = sb("junk", (128, D), BF16)
    xb = xbs[0]
    nt_cnt = {"n": 0}
    col = sb("col", (128, 8))
    HRc = sb("HRc", (128, 2, 32)); HIc = sb("HIc", (128, 2, 32))
    HRs = sb("HRs", (128, 16, 32)); HIs = sb("HIs", (128, 16, 32))
    concat = sb("concat", (128, 16, 640), BF16)
    utail = sb("utail", (128, 8, 32), BF16)
    rt = [sb(f"rt{i}", (128, 16, 32)) for i in range(4)]
    def P0_tiles(stack_x, tiles, xnT, prefix=False):
        load_g(norm_mix, "mix")
        for (rows, c0) in tiles:
            xtile, xk = load_x(rows)
            norm_T(xtile[:], [xk], lambda hf, c0=c0: xnT[:, hf * 8:(hf + 1) * 8, c0:c0 + 128], ["xnT"])

    for grp in range(2):
        has_s = (grp == 0)
        NTOK = 640 if has_s else 512
        NT = NTOK // 128
        NCOL = 32 + NTOK
        tiles_main = [(xm[grp * 512 + 128 * t: grp * 512 + 128 * (t + 1), :], 32 + 128 * t) for t in range(4)]
        if has_s:
            tiles_main.append((xs[:, :], 32 + 512))
        blocks = [(0, 512)] + ([(512, 128)] if has_s else [])

        with ExitStack() as x1:
            xnT = sbL(x1, "xnT", (128, 16, 672), BF16)
            u = sbL(x1, "u", (128, 8, 672), BF16)
            ups = sbL(x1, "ups", (128, 8, 16, 38), BF16)
            y32 = sbL(x1, "y32", (128, 8, 640))
            uf = sbL(x1, "uf", (128, 512)); sg = sbL(x1, "sg", (128, 512))
            dgs = [sbL(x1, f"dg{i}", (128, 31, 128), BF16) for i in range(2)]
            ybf = sbL(x1, "ybf", (128, 640), BF16); ysq = sbL(x1, "ysq", (128, 640), BF16)
            mu = sbL(x1, "mu", (128, 640)); rs = sbL(x1, "rs", (128, 640)); vt = sbL(x1, "vt", (128, 640))
            uout = sbL(x1, "uout", (128, 8, 160))
            kb.op("dve", lambda h: h.memset(uout[:], 0.0), W=["uout"])
            if grp == 1:
                kb.op("dve", lambda h: h.memset(xnT[:, :, 0:32], 0.0), W=["xnT"])
            if grp == 0:
                pass
            if grp == 0:
                P0_tiles(x1, [(xp[896:1024, :], 32)], xnT)
                cp("dve", xnT[:, :, 0:32], xnT[:, :, 128:160], ["xnT"], ["xnT"])
            P0_tiles(x1, tiles_main, xnT)
            if has_s:
                for i in range(4):
                    xtile, xk = load_x(stc[4 * i:4 * i + 4].rearrange("b i c -> (b i) c"), 120, DC)
                    for bl in range(4):
                        kb.dma("sp", o_cs[4 * i + bl, 0:22, :], xtile[bl * 30 + 8:bl * 30 + 30, 0:DC], R=[xk], W=["o_cs_a"], key="ocs")
                    for q in range(8):
                        b = kb.ps()
                        tr(psf(b)[:, 0:128], xtile[:, q * 128:(q + 1) * 128], ident_f[:], [xk, "ident_f"], [PK[b]])
                        cp("act" if q % 2 == 0 else "dve", ups[:, q, 4 * i:4 * i + 4, 0:30],
                           psf(b)[:, 0:120].rearrange("p (b i) -> p b i", i=30), [PK[b]], ["ups"])
            loads = [[(lambda s: s[:, 0, :, :], w_in_v[:, :, q * 128:(q + 1) * 128]),
                      (lambda s: s[:, 1, :, :], w_in_v[:, :, 1024 + q * 128:1024 + (q + 1) * 128])] for q in range(8)]
            with ExitStack() as wsx:
                wvg = Stream(wsx, "wvg", 2, (128, 2, 16, 128), loads)
                mblocks = [(0, 512), (512, NCOL - 512)]
                for q in range(8):
                    wslot, wk = wvg.get(q)
                    for (c0, n) in mblocks:
                        bv = kb.ps(); bg = kb.ps()
                        for vg, bb in ((0, bv), (1, bg)):
                            for kc in range(16):
                                mm(psf(bb)[:, 0:n], wslot[:, vg, kc, :], xnT[:, kc, c0:c0 + n], kc == 0, kc == 15, [wk, "xnT"], [PK[bb]])
                        actf(sg[:, 0:n], psf(bg)[:, 0:n], AF.Sigmoid, [PK[bg]], ["sg"])
                        tt(uf[:, 0:n], psf(bv)[:, 0:n], sg[:, 0:n], ALU.mult, [PK[bv], "sg"], ["uf"])
                        lo = 0 if (grp == 0 or c0 > 0) else 32
                        pe_ = min(c0 + n, 32 + 512)
                        if pe_ > c0 + lo:
                            cp("pool", u[:, q, c0 + lo:pe_], uf[:, lo:pe_ - c0], ["uf"], ["u"])
                        if has_s and c0 + n > 544:
                            s0 = 544 - c0
                            cp("act", ups[:, q, :, 30:38], uf[:, s0:s0 + 128].rearrange("p (b i) -> p b i", i=8), ["uf"], ["ups"])
                            cp("dve", uout[:, q, 32:160], uf[:, s0:s0 + 128], ["uf"], ["uout"])
                        if grp == 1 and c0 + n >= 544:
                            s0 = 514 - c0
                            cp("dve", uout[:, q, 0:30], uf[:, s0:s0 + 30], ["uf"], ["uout"])
                if grp == 1:
                    cp("dve", u[:, :, 2:32], utail[:, :, 2:32], ["utail"], ["u"])
                else:
                    pass
                if grp == 0:
                    cp("dve", utail[:, :, 2:32], u[:, :, 514:544], ["u"], ["utail"])
            if has_s:
                for tch in range(2):
                    xo = xt[xcount["n"] % 2]; xok = f"xt{xcount['n'] % 2}"; xcount["n"] += 1
                    for q4 in range(4):
                        q = tch * 4 + q4
                        b = kb.ps()
                        tr(psf(b)[:, 0:128], uout[:, q, 32:160], ident_f[:], ["uout", "ident_f"], [PK[b]])
                        cp("act" if q % 2 == 0 else "dve", xo[:, q * 128:(q + 1) * 128], psf(b)[:, 0:128], [PK[b]], [xok])
                for bq in range(16):
                    for tch in range(2):
                        pass
                s_a = (xcount["n"] - 2) % 2; s_b = (xcount["n"] - 1) % 2
                for bq in range(16):
                    kb.dma("sp", o_cs[bq, 22:30, 0:512], xt[s_a][bq * 8:(bq + 1) * 8, 0:512], R=[f"xt{s_a}"], W=["o_cs_b"], key="ocs")
                    kb.dma("sp", o_cs[bq, 22:30, 512:1024], xt[s_b][bq * 8:(bq + 1) * 8, 512:1024], R=[f"xt{s_b}"], W=["o_cs_b"], key="ocs")
            if grp == 1:
                xo = xt[xcount["n"] % 2]; xok = f"xt{xcount['n'] % 2}"; xcount["n"] += 1
                for q in range(8):
                    b = kb.ps()
                    tr(psf(b)[:, 0:128], uout[:, q, 0:128], ident_f[:], ["uout", "ident_f"], [PK[b]])
                    cp("act" if q % 2 == 0 else "dve", xo[0:30, q * 128:(q + 1) * 128], psf(b)[0:30, 0:128], [PK[b]], [xok])
                kb.dma("sp", o_cp[:, :], xo[0:30, 0:1024], R=[xok], W=["o_cp"], key="ocp")
            sblocks = [(0, 512, False)] + ([(512, 128, True)] if has_s else [])
            st_b = {}
            for (t0, n, is_s) in sblocks:
                st_b[t0] = (kb.ps(reserve=True), kb.ps(reserve=True))
            for q in range(8):
                dg = dgs[q % 2]; dgk = f"dg{q % 2}"
                for k in range(31):
                    ts(dg[:, k, :], ident_b[:], cwT[:, q, k:k + 1], None, ALU.mult, None, ["ident_b", "cwT", dgk], [dgk])
                for (t0, n, is_s) in sblocks:
                    b = kb.ps()
                    for k in range(31):
                        if is_s:
                            rhs = ups[:, q, :, k:k + 8]
                        else:
                            rhs = u[:, q, 2 + k:2 + k + 512]
                        mm(psf(b)[:, 0:n], dg[:, k, :], rhs, k == 0, k == 30, [dgk, "u", "ups"], [PK[b]])
                    actf(y32[:, q, t0:t0 + n], psf(b)[:, 0:n], AF.Identity, [PK[b], "cb"], ["y32"], bias=cb_t[:, q:q + 1])
                    actf(ybf[:, t0:t0 + n], psf(b)[:, 0:n], AF.Identity, [PK[b], "cb", "ybf"], ["ybf"], bias=cb_t[:, q:q + 1])
                    actf(ysq[:, t0:t0 + n], psf(b)[:, 0:n], AF.Square, [PK[b], "cb", "ysq"], ["ysq"], bias=cb_t[:, q:q + 1])
                    b1, b2 = st_b[t0]
                    mm(psf(b1)[:, 0:n], ones_b[:], ybf[:, t0:t0 + n], q == 0, q == 7, ["ones_b", "ybf"], [PK[b1]])
                    mm(psf(b2)[:, 0:n], ones_b[:], ysq[:, t0:t0 + n], q == 0, q == 7, ["ones_b", "ysq"], [PK[b2]])
            for (t0, n, is_s) in sblocks:
                b1, b2 = st_b[t0]
                ts(mu[:, t0:t0 + n], psf(b1)[:, 0:n], 1.0 / DC, None, ALU.mult, None, [PK[b1]], ["mu"])
                tt(vt[:, t0:t0 + n], mu[:, t0:t0 + n], mu[:, t0:t0 + n], ALU.mult, ["mu"], ["vt"])
                stt(vt[:, t0:t0 + n], psf(b2)[:, 0:n], 1.0 / DC, vt[:, t0:t0 + n], ALU.mult, ALU.subtract, [PK[b2], "vt"], ["vt"])
                ts(vt[:, t0:t0 + n], vt[:, t0:t0 + n], EPS, None, ALU.add, None, ["vt"], ["vt"])
                rsqrt(rs[:, t0:t0 + n], vt[:, t0:t0 + n], n, ["vt"], ["rs"])
                kb.release(b1); kb.release(b2)

            def rms_finish(src32, key32, gvec, gk, cbase):
                fb = {}
                for (t0, n, is_s) in sblocks:
                    fb[t0] = kb.ps()
                for q in range(8):
                    for (t0, n, is_s) in sblocks:
                        actf(ysq[:, t0:t0 + n], src32[:, q, t0:t0 + n], AF.Square, [key32, "ysq"], ["ysq"])
                        mm(psf(fb[t0])[:, 0:n], ones_b[:], ysq[:, t0:t0 + n], q == 0, q == 7, ["ones_b", "ysq"], [PK[fb[t0]]])
                for (t0, n, is_s) in sblocks:
                    ts(vt[:, t0:t0 + n], psf(fb[t0])[:, 0:n], 1.0 / DC, EPS, ALU.mult, ALU.add, [PK[fb[t0]], "vt"], ["vt"])
                    rsqrt(rs[:, t0:t0 + n], vt[:, t0:t0 + n], n, ["vt"], ["rs"])
                for q in range(8):
                    stt(concat[:, cbase + q, 0:NTOK], src32[:, q, 0:NTOK], gvec[:, q:q + 1], rs[:, 0:NTOK], ALU.mult, ALU.mult,
                        [key32, gk, "rs"], ["concat"])

            for q in range(8):
                tt(y32[:, q, 0:NTOK], y32[:, q, 0:NTOK], mu[:, 0:NTOK], ALU.subtract, ["y32", "mu"], ["y32"])
                tt(y32[:, q, 0:NTOK], y32[:, q, 0:NTOK], rs[:, 0:NTOK], ALU.mult, ["y32", "rs"], ["y32"])
                for (t0, n, is_s) in sblocks:
                    actf(uf[:, 0:n], y32[:, q, t0:t0 + n], AF.Identity, ["y32", "lng", "lnb", "uf"], ["uf"], scale=lng_t[:, q:q + 1], bias=lnb_t[:, q:q + 1])
                    actf(sg[:, 0:n], uf[:, 0:n], AF.Sigmoid, ["uf", "sg"], ["sg"])
                    tt(y32[:, q, t0:t0 + n], uf[:, 0:n], sg[:, 0:n], ALU.mult, ["uf", "sg", "y32"], ["y32"])
            rms_finish(y32, "y32", gnc_t, "gnc", 0)
            if grp == 0:
                stage("x1c", dumps=[("concat", concat, ["concat"])])
                stage("x1", dumps=[("concat", concat, ["concat"]), ("u", u, ["u"]), ("y32", y32, ["y32"]), ("xnT", xnT, ["xnT"]), ("ups", ups, ["ups"]), ("mu", mu, ["mu"]), ("rs", rs, ["rs"]), ("vt", vt, ["vt"]), ("ybf", ybf, ["ybf"]), ("cwT", cwT, ["cwT"])])
        if grp == 0:
            stage("x1d", dumps=[("concat", concat, ["concat"])])
        barrier()

        if grp == 0:
            stage("x1b", dumps=[("concat", concat, ["concat"])])
        with ExitStack() as e_:
            NBT = 80
            U = sbL(e_, "U", (128, 64, NBT), BF16)
            HRb = [sbL(e_, f"HRb{i}", (128, 32, NBT), BF16) for i in range(2)]
            HIb = [sbL(e_, f"HIb{i}", (128, 32, NBT), BF16) for i in range(2)]
            ygT = sbL(e_, "ygT", (128, 8, 640), BF16)

            def run_A1(xnT, ZT, wst_stack, NB, cbase, uo):
                R2 = 2 * NB
                loads = [[(lambda s: s[:, :, :], w_in_v[:, :, 2048 + i * 256:2048 + (i + 1) * 256])] for i in range(4)]
                with ExitStack() as ws_:
                    wS = Stream(ws_, "wS", 2, (128, 16, 256), loads)
                    for nbk in range(4):
                        wslot, wk = wS.get(nbk)
                        for m in range(4):
                            b = kb.ps()
                            for kc in range(16):
                                lhs = xnT[:, kc, cbase + m:cbase + m + 8 * NB - 3:4]
                                mm(psf(b)[0:R2, 0:256], lhs, wslot[:, kc, :], kc == 0, kc == 15, ["xnT", wk], [PK[b]])
                            srcv = psf(b)[0:R2, 0:256].rearrange("r (g c) -> r g c", c=16)
                            cp("act", ZT[0:R2, nbk * 16:(nbk + 1) * 16, 0, m * 16:(m + 1) * 16], srcv, [PK[b]], ["ZT"])
                            cp("act", ZT[0:R2, nbk * 16:(nbk + 1) * 16, 1, m * 16:(m + 1) * 16], srcv, [PK[b]], ["ZT"])
                    for g8 in range(8):
                        b = kb.ps()
                        for gl in range(8):
                            g = g8 * 8 + gl
                            tr(psb(b)[:, gl * R2:(gl + 1) * R2], ZT[0:R2, g, :, :], ident_b[0:R2, 0:R2], ["ZT", "ident_b"], [PK[b]])
                        v = psb(b)[:, 0:16 * NB].rearrange("k (g j s) -> k g j s", g=8, s=2)
                        cp("act", U[0:64, g8 * 8:(g8 + 1) * 8, uo:uo + NB], v[0:64, :, :, 0], [PK[b]], ["U"])
                        cp("act", U[64:128, g8 * 8:(g8 + 1) * 8, uo:uo + NB], v[64:128, :, :, 1], [PK[b]], ["U"])

            def run_A2(a2, SRs, SIs, NB, uo, chain):
                if chain:
                    Cj = sbL(a2, "Cj", (128, 32, 64)); Sj = sbL(a2, "Sj", (128, 32, 64)); D0 = sbL(a2, "D0_", (128, 32, 64))
                    ld(Cj[:].rearrange("p g j -> p (g j)"), d_Cj[:, :], "cst2", ["Cj"]); ld(Sj[:].rearrange("p g j -> p (g j)"), d_Sj[:, :], "cst2", ["Sj"])
                    ld(D0[:].rearrange("p g j -> p (g j)"), d_D0[:, :], "cst2", ["D0"])
                with ExitStack() as ms_:
                    MS2r = sbL(ms_, "MS2r_", (128, 4096), BF16); MS2i = sbL(ms_, "MS2i_", (128, 4096), BF16)
                    ld(MS2r[:], d_MS2r[:, :], "cst", ["MS2r"]); ld(MS2i[:], d_MS2i[:, :], "cst", ["MS2i"])
                    for gq in range(4):
                        bR = kb.ps(); bI = kb.ps()
                        for gl in range(8):
                            gp = gq * 8 + gl
                            for ge in range(2):
                                g = 2 * gp + ge
                                for (bb, MS_, mk) in ((bR, MS2r, "MS2r"), (bI, MS2i, "MS2i")):
                                    mm(psf(bb)[64 * ge:64 * ge + 64, gl * NB:(gl + 1) * NB], MS_[:, gp * 128 + ge * 64:gp * 128 + ge * 64 + 64],
                                       U[:, g, uo:uo + NB], True, True, [mk, "U"], [PK[bb]])
                        for (bb, dst, nm, eng) in ((bR, SRs, "SRs", "act"), (bI, SIs, "SIs", "dve")):
                            src = psf(bb)[:, 0:8 * NB].rearrange("p (g j) -> p g j", g=8)
                            cp(eng, dst[:, gq * 8:(gq + 1) * 8, 0:NB], src, [PK[bb]], [nm])
                if chain:
                    wr = sbL(a2, "wr", (128, 32, 64)); wi = sbL(a2, "wi", (128, 32, 64))
                    t1 = sbL(a2, "t1", (128, 32, 64)); t2 = sbL(a2, "t2", (128, 32, 64))
                    hr = HRc[:, 0, :]; hi = HIc[:, 0, :]
                    tt(rt[0][:, 0, :], A8r[:], hr, ALU.mult, ["H0c", "A8", "rt0"], ["rt0"])
                    tt(rt[1][:, 0, :], A8i[:], hi, ALU.mult, ["H0c", "A8", "rt1"], ["rt1"])
                    tt(rt[2][:, 0, :], A8r[:], hi, ALU.mult, ["H0c", "A8", "rt2"], ["rt2"])
                    tt(rt[3][:, 0, :], A8i[:], hr, ALU.mult, ["H0c", "A8", "rt3"], ["rt3"])
                    tt(rt[0][:, 0, :], rt[0][:, 0, :], rt[1][:, 0, :], ALU.subtract, ["rt0", "rt1"], ["rt0"])
                    tt(rt[2][:, 0, :], rt[2][:, 0, :], rt[3][:, 0, :], ALU.add, ["rt2", "rt3"], ["rt2"])
                    tt(SRs[:, :, 0], SRs[:, :, 0], rt[0][:, 0, :], ALU.add, ["rt0", "SRs"], ["SRs"])
                    tt(SIs[:, :, 0], SIs[:, :, 0], rt[2][:, 0, :], ALU.add, ["rt2", "SIs"], ["SIs"])
                    for ge in range(2):
                        ts(HRb[ge][:, :, uo], hr, pm[:, ge:ge + 1], None, ALU.mult, None, ["H0c", "pm"], ["HRb"])
                        ts(HIb[ge][:, :, uo], hi, pm[:, ge:ge + 1], None, ALU.mult, None, ["H0c", "pm"], ["HIb"])
                    tt(t1[:], Cj[:], SRs[:], ALU.mult, ["Cj", "SRs", "t1"], ["t1"])
                    tt(t2[:], Sj[:], SIs[:], ALU.mult, ["Sj", "SIs", "t2"], ["t2"])
                    tt(wr[:], t1[:], t2[:], ALU.add, ["t1", "t2", "wr"], ["wr"])
                    tt(t1[:], Cj[:], SIs[:], ALU.mult, ["Cj", "SIs", "t1"], ["t1"])
                    tt(t2[:], Sj[:], SRs[:], ALU.mult, ["Sj", "SRs", "t2"], ["t2"])
                    tt(wi[:], t1[:], t2[:], ALU.subtract, ["t1", "t2", "wi"], ["wi"])
                    fl = lambda a: a[:].rearrange("p g j -> p (g j)")
                    kb.op("dve", lambda h: h.tensor_tensor_scan(out=fl(t1), data0=fl(D0), data1=fl(wr), initial=0.0, op0=ALU.mult, op1=ALU.add),
                          R=["D0", "wr", "t1"], W=["t1"])
                    kb.op("dve", lambda h: h.tensor_tensor_scan(out=fl(t2), data0=fl(D0), data1=fl(wi), initial=0.0, op0=ALU.mult, op1=ALU.add),
                          R=["D0", "wi", "t2"], W=["t2"])
                    tt(SRs[:], Cj[:], t1[:], ALU.mult, ["Cj", "t1", "SRs"], ["SRs"])
                    tt(SIs[:], Sj[:], t2[:], ALU.mult, ["Sj", "t2", "SIs"], ["SIs"])
                    tt(wr[:], SRs[:], SIs[:], ALU.subtract, ["SRs", "SIs", "wr"], ["wr"])
                    tt(SRs[:], Cj[:], t2[:], ALU.mult, ["Cj", "t2", "SRs"], ["SRs"])
                    tt(SIs[:], Sj[:], t1[:], ALU.mult, ["Sj", "t1", "SIs"], ["SIs"])
                    tt(wi[:], SRs[:], SIs[:], ALU.add, ["SRs", "SIs", "wi"], ["wi"])
                    for ge in range(2):
                        ts(HRb[ge][:, :, uo + 1:uo + NB], wr[:, :, 0:NB - 1], pm[:, ge:ge + 1], None, ALU.mult, None, ["wr", "pm"], ["HRb"])
                        ts(HIb[ge][:, :, uo + 1:uo + NB], wi[:, :, 0:NB - 1], pm[:, ge:ge + 1], None, ALU.mult, None, ["wi", "pm"], ["HIb"])
                    stage("a2x", dumps=[("Cj", Cj, ["Cj"]), ("Sj", Sj, ["Sj"]), ("D0", D0, ["D0"]), ("wr", wr, ["wr"]), ("t1", t1, ["t1"]), ("t2", t2, ["t2"]), ("SRs", SRs, ["SRs"])])
                    cp("dve", HRc[:, 1, :], wr[:, :, NB - 1], ["wr"], ["H64"])
                    cp("dve", HIc[:, 1, :], wi[:, :, NB - 1], ["wi"], ["H64"])
                else:
                    a8r = A8r[:].unsqueeze(1).to_broadcast([128, NB, 32]); a8i = A8i[:].unsqueeze(1).to_broadcast([128, NB, 32])
                    for ge in range(2):
                        ts(HRb[ge][:, :, uo:uo + NB], HRs[:].rearrange("p j g -> p g j"), pm[:, ge:ge + 1], None, ALU.mult, None, ["HRs", "pm"], ["HRb"])
                        ts(HIb[ge][:, :, uo:uo + NB], HIs[:].rearrange("p j g -> p g j"), pm[:, ge:ge + 1], None, ALU.mult, None, ["HIs", "pm"], ["HIb"])
                    tt(rt[0][:], a8r, HRs[:], ALU.mult, ["HRs", "A8", "rt0"], ["rt0"])
                    tt(rt[1][:], a8i, HIs[:], ALU.mult, ["HIs", "A8", "rt1"], ["rt1"])
                    tt(rt[2][:], a8r, HIs[:], ALU.mult, ["HIs", "A8", "rt2", "HIb"], ["rt2"])
                    tt(rt[3][:], a8i, HRs[:], ALU.mult, ["HRs", "A8", "rt3", "HRb"], ["rt3"])
                    tt(rt[0][:], rt[0][:], rt[1][:], ALU.subtract, ["rt0", "rt1"], ["rt0"])
                    tt(rt[2][:], rt[2][:], rt[3][:], ALU.add, ["rt2", "rt3"], ["rt2"])
                    tt(rt[0][:], rt[0][:], SRs[:, :, 0:NB].rearrange("p g j -> p j g"), ALU.add, ["rt0", "SRs"], ["rt0"])
                    tt(rt[2][:], rt[2][:], SIs[:, :, 0:NB].rearrange("p g j -> p j g"), ALU.add, ["rt2", "SIs"], ["rt2"])
                    for (src_t, sk, dst_o, ok) in ((rt[0], "rt0", o_rs, "ors"), (rt[2], "rt2", o_is, "ois")):
                        xo = xt[xcount["n"] % 2]; xok = f"xt{xcount['n'] % 2}"; xcount["n"] += 1
                        for i in range(4):
                            b = kb.ps()
                            tr(psf(b)[:, 0:128], src_t[:, 4 * i:4 * i + 4, :], ident_f[:], [sk, "ident_f"], [PK[b]])
                            cp("act" if i % 2 == 0 else "dve", xo[:, i * 128:(i + 1) * 128], psf(b)[:, 0:128], [PK[b]], [xok])
                        for i in range(4):
                            kb.dma("sp", dst_o.rearrange("b (gp ge) p -> (b gp) (ge p)", ge=2)[128 * i:128 * (i + 1), :],
                                   xo[:, i * 128:(i + 1) * 128], R=[xok], W=[ok], key=ok)

            passes = []
            if grp == 0:
                passes += [("pre", 0), ("pre", 1)]
            passes += [("main", grp)]
            if has_s:
                passes += [("samp", 0)]
            if grp == 0:
                kb.op("pool", lambda h: h.memset(HRc[:, 0, :], 0.0), W=["H0c"])
                kb.op("pool", lambda h: h.memset(HIc[:, 0, :], 0.0), W=["H0c"])
                for (src_d, dst_t, nm) in ((sre, HRs, "HRs"), (sim, HIs, "HIs")):
                    for i in range(4):
                        xtile, xk = load_x(src_d.rearrange("b (gp ge) p -> (b gp) (ge p)", ge=2)[128 * i:128 * (i + 1), :], 128, 128)
                        b = kb.ps()
                        tr(psf(b)[:, 0:128], xtile[:, 0:128], ident_f[:], [xk, "ident_f"], [PK[b]])
                        cp("act", dst_t[:, 4 * i:4 * i + 4, :], psf(b)[:, 0:128].rearrange("p (b g) -> p b g", b=4), [PK[b]], [nm])
            for (kind, idx) in passes:
                NB = 16 if kind == "samp" else 64
                uo = 64 if kind == "samp" else 0
                with ExitStack() as a1:
                    xnT = sbL(a1, "xnT2", (128, 16, 672), BF16)
                    ZT = sbL(a1, "ZT", (128, 64, 2, 64), BF16)
                    if kind == "pre":
                        tl = [(xp[idx * 512 + 128 * t: idx * 512 + 128 * (t + 1), :], 32 + 128 * t) for t in range(4)]
                        cbase = 32
                    elif kind == "main":
                        tl = [(xm[idx * 512 + 128 * t: idx * 512 + 128 * (t + 1), :], 32 + 128 * t) for t in range(4)]
                        cbase = 32
                    else:
                        tl = [(xs[:, :], 32 + 512)]
                        cbase = 32 + 512
                    P0_tiles(a1, tl, xnT)
                    run_A1(xnT, ZT, a1, NB, cbase, uo)
                    if grp == 0 and kind == "pre" and idx == 0:
                        stage("a1", dumps=[("concat", concat, ["concat"]), ("U", U, ["U"]), ("ZT", ZT, ["ZT"]), ("xnT", xnT, ["xnT"])])
                    if grp == 0 and kind == "main":
                        stage("a1m", dumps=[("U", U, ["U"]), ("ZT", ZT, ["ZT"]), ("xnT", xnT, ["xnT"])])
                barrier()
                with ExitStack() as a2:
                    SRs = sbL(a2, "SRs", (128, 32, 64)); SIs = sbL(a2, "SIs", (128, 32, 64))
                    run_A2(a2, SRs, SIs, NB, uo, chain=(kind != "samp"))
                    if grp == 0 and kind == "pre" and idx == 0:
                        stage("a2", dumps=[("concat", concat, ["concat"]), ("HRc", HRc, ["H64"]), ("HIc", HIc, ["H64"]), ("SRs", SRs, ["SRs"])])
                    if grp == 0 and kind == "main":
                        stage("a2m", dumps=[("concat", concat, ["concat"]), ("HRc", HRc, ["H64"]), ("HIc", HIc, ["H64"]), ("SRs", SRs, ["SRs"]), ("HRb0", HRb[0], ["HRb"])])
                    if kind != "samp":
                        cp("dve", HRc[:, 0, :], HRc[:, 1, :], ["H64", "HRb", "HIb"], ["H0c"])
                        cp("dve", HIc[:, 0, :], HIc[:, 1, :], ["H64", "HRb", "HIb"], ["H0c"])
                    if kind == "main" and grp == 1:
                        for (src_t, dst_o, ok) in ((HRc, o_rp, "orp"), (HIc, o_ip, "oip")):
                            xo = xt[xcount["n"] % 2]; xok = f"xt{xcount['n'] % 2}"; xcount["n"] += 1
                            b = kb.ps()
                            cp("dve", rt[0][:, 0, :], src_t[:, 1, :], ["H64", "rt0"], ["rt0"])
                            tr(psf(b)[:, 0:128], rt[0][:, 0:4, :], ident_f[:], ["rt0", "ident_f"], [PK[b]])
                            cp("act", xo[0:32, 0:128], psf(b)[0:32, 0:128], [PK[b]], [xok])
                            kb.dma("sp", dst_o.rearrange("(gp ge) p -> gp (ge p)", ge=2), xo[0:32, 0:128], R=[xok], W=[ok], key=ok)
                barrier()
                if kind == "pre":
                    continue
            a3 = ExitStack()
            if True:
                MY1rb = sbL(a3, "MY1rb_", (128, 4096), BF16); MY1ib = sbL(a3, "MY1ib_", (128, 4096), BF16)
                MY2 = sbL(a3, "MY2_", (128, 8192), BF16)
                YT = sbL(a3, "YT", (64, 8, 1024), BF16)
                gq_ = sbL(a3, "gq_", (128, 512)); gw_ = sbL(a3, "gw_", (128, 512))
                ld(MY1rb[:], d_MY1r[:, :], "cst", ["MY1rb"]); ld(MY1ib[:], d_MY1i[:, :], "cst", ["MY1ib"]); ld(MY2[:], d_MY2[:, :], "cst", ["MY2"])
                ypasses = [(64, 0, 0)] + ([(16, 64, 512)] if has_s else [])
                for (NB, uo, tbase) in ypasses:
                    for g4 in range(16):
                        b = kb.ps()
                        for gl in range(4):
                            g = g4 * 4 + gl
                            gp, ge = g // 2, g % 2
                            o = gl * 128
                            P0_, P1_ = 64 * ge, 64 * ge + 64
                            mm(psf(b)[0:NB, o:o + 128], U[:, g, uo:uo + NB], MY2[:, g * 128:(g + 1) * 128], True, False, ["U", "MY2"], [PK[b]])
                            mm(psf(b)[0:NB, o:o + 128], HRb[ge][:, gp, uo:uo + NB], MY1rb[:, gp * 128:(gp + 1) * 128], False, False, ["HRb", "MY1rb"], [PK[b]])
                            mm(psf(b)[0:NB, o:o + 128], HIb[ge][:, gp, uo:uo + NB], MY1ib[:, gp * 128:(gp + 1) * 128], False, True, ["HIb", "MY1ib"], [PK[b]])
                        src = psf(b)[0:NB, :].rearrange("j (g r c) -> j r g c", g=4, r=8)
                        dst = YT[0:NB, :, g4 * 64:(g4 + 1) * 64].rearrange("j r (g c) -> j r g c", g=4)
                        cp("act" if g4 % 2 == 0 else "dve", dst, src, [PK[b]], ["YT"])
                    for r in range(8):
                        b = kb.ps()
                        for q in range(8):
                            tr(psb(b)[:, q * NB:(q + 1) * NB], YT[0:NB, r, q * 128:(q + 1) * 128], ident_b[0:NB, 0:NB], ["YT", "ident_b"], [PK[b]])
                        yv = psb(b)[:, 0:8 * NB]
                        n8 = 8 * NB
                        actf(gq_[:, 0:n8], yv, AF.Square, [PK[b], "gq"], ["gq"])
                        ts(gq_[:, 0:n8], gq_[:, 0:n8], GC2, GC1, ALU.mult, ALU.add, ["gq"], ["gq"])
                        tt(gq_[:, 0:n8], gq_[:, 0:n8], yv, ALU.mult, ["gq", PK[b]], ["gq"])
                        actf(gw_[:, 0:n8], gq_[:, 0:n8], AF.Sigmoid, ["gq", "gw"], ["gw"])
                        dstv = ygT[:, :, tbase:tbase + 8 * NB].rearrange("p q (j e) -> p q e j", e=8)[:, :, r, :]
                        tt(dstv, gw_[:, 0:n8].rearrange("p (q j) -> p q j", q=8), yv.rearrange("p (q j) -> p q j", q=8), ALU.mult, ["gw", PK[b]], ["ygT"])
                if grp == 0:
                    stage("a3", dumps=[("concat", concat, ["concat"]), ("ygT", ygT, ["ygT"]), ("YT", YT, ["YT"])])
            with ExitStack() as g_:
                ys32 = sbL(g_, "ys32", (128, 8, 640))
                sg = sbL(g_, "sg2", (128, 512))
                ysq = sbL(g_, "ysq2", (128, 640), BF16)
                vt = sbL(g_, "vt2", (128, 640)); rs = sbL(g_, "rs2", (128, 640))
                loads = [[(lambda s: s[:, :, :], w_glu_v[:, :, q * 128:(q + 1) * 128])] for q in range(8)]
                wgl = Stream(g_, "wgl", 2, (128, 8, 128), loads)
                sblocks = [(0, 512, False)] + ([(512, 128, True)] if has_s else [])
                for q in range(8):
                    wslot, wk = wgl.get(q)
                    for (t0, n, is_s) in sblocks:
                        b = kb.ps()
                        for kc in range(8):
                            mm(psf(b)[:, 0:n], wslot[:, kc, :], ygT[:, kc, t0:t0 + n], kc == 0, kc == 7, [wk, "ygT"], [PK[b]])
                        actf(sg[:, 0:n], psf(b)[:, 0:n], AF.Sigmoid, [PK[b], "sg"], ["sg"])
                        tt(ys32[:, q, t0:t0 + n], ygT[:, q, t0:t0 + n], sg[:, 0:n], ALU.mult, ["ygT", "sg"], ["ys32"])
                fb = {}
                for (t0, n, is_s) in sblocks:
                    fb[t0] = kb.ps()
                for q in range(8):
                    for (t0, n, is_s) in sblocks:
                        actf(ysq[:, t0:t0 + n], ys32[:, q, t0:t0 + n], AF.Square, ["ys32", "ysq"], ["ysq"])
                        mm(psf(fb[t0])[:, 0:n], ones_b[:], ysq[:, t0:t0 + n], q == 0, q == 7, ["ones_b", "ysq"], [PK[fb[t0]]])
                for (t0, n, is_s) in sblocks:
                    ts(vt[:, t0:t0 + n], psf(fb[t0])[:, 0:n], 1.0 / DC, EPS, ALU.mult, ALU.add, [PK[fb[t0]], "vt"], ["vt"])
                    rsqrt(rs[:, t0:t0 + n], vt[:, t0:t0 + n], n, ["vt"], ["rs"])
                for q in range(8):
                    stt(concat[:, 8 + q, 0:NTOK], ys32[:, q, 0:NTOK], gns_t[:, q:q + 1], rs[:, 0:NTOK], ALU.mult, ALU.mult,
                        ["ys32", "gns", "rs"], ["concat"])
                if grp == 0:
                    stage("g", dumps=[("concat", concat, ["concat"]), ("ys32", ys32, ["ys32"])])
            barrier()
            a3.close()
        with ExitStack() as f_:
            hm = [sbL(f_, f"hm{t}", (128, D)) for t in range(NT)]
            hnT = sbL(f_, "hnT", (128, 16, 640), BF16)
            act = sbL(f_, "act", (128, 11, 640), BF16)
            sg = sbL(f_, "sg3", (128, 512)); tg = sbL(f_, "tg3", (128, 512))
            rows = []
            for t in range(NT):
                if t < 4:
                    rows.append((xm[grp * 512 + 128 * t: grp * 512 + 128 * (t + 1), :], o_ym[grp * 512 + 128 * t: grp * 512 + 128 * (t + 1), :]))
                else:
                    rows.append((xs[:, :], o_ys[:, :]))
            for t in range(NT):
                ld(hm[t][:], rows[t][0], f"hm{t}", [f"hm{t}"])
            with ExitStack() as wo_:
                loads = [[(lambda s: s[:, :, :], w_out_v[:, :, nb * 256:(nb + 1) * 256])] for nb in range(8)]
                wo = Stream(wo_, "wo", 2, (128, 16, 256), loads)
                for nb in range(8):
                    wslot, wk = wo.get(nb)
                    for t in range(NT):
                        b = kb.ps()
                        for kc in range(16):
                            mm(psf(b)[:, 0:256], concat[:, kc, t * 128:(t + 1) * 128], wslot[:, kc, :], kc == 0, kc == 15, ["concat", wk], [PK[b]])
                        tt(hm[t][:, nb * 256:(nb + 1) * 256], psf(b)[:, 0:256], hm[t][:, nb * 256:(nb + 1) * 256], ALU.add, [PK[b], f"hm{t}"], [f"hm{t}"])
            if grp == 0:
                stage("f1", dumps=[("concat", concat, ["concat"]), ("hm0", hm[0], ["hm0"]), ("hm4", hm[4], ["hm4"])])
            load_g(norm_ffn, "ffn")
            for t in range(NT):
                norm_T(hm[t][:], [f"hm{t}"], lambda hf, t=t: hnT[:, hf * 8:(hf + 1) * 8, t * 128:(t + 1) * 128], ["hnT"])
            if NTOK == 640:
                mblocks = [(0, 320), (320, 320)]
            else:
                mblocks = [(0, 512)]
            with ExitStack() as wf_:
                gl_loads = []
                for c2 in range(NCH // 2):
                    gl_loads.append([(lambda s: s[:, 0, :, :], w_g_v[:, :, c2 * 256:(c2 + 1) * 256]),
                                     (lambda s: s[:, 1, :, :], w_u_v[:, :, c2 * 256:(c2 + 1) * 256])])
                wgu = Stream(wf_, "wgu", 2, (128, 2, 16, 256), gl_loads)
                d_loads = []
                for qd in range(4):
                    for nb in range(4):
                        for (k0, kn) in ((0, 6), (6, 5)):
                            kc = qd * 11 + k0
                            d_loads.append([(lambda s, kn=kn: s[:, 0:kn, :], w_d_v[:, kc:kc + kn, nb * 512:(nb + 1) * 512])])
                wdn = Stream(wf_, "wdn", 2, (128, 6, 512), d_loads)
                di = 0
                for qd in range(4):
                    wdn.ensure(di + 1)
                    for c11 in range(11):
                        c = qd * 11 + c11
                        wslot, wk = wgu.get(c // 2)
                        co = (c % 2) * 128
                        for (c0, n) in mblocks:
                            bg = kb.ps(); bu = kb.ps()
                            for vg, bb in ((0, bg), (1, bu)):
                                for kc in range(16):
                                    mm(psf(bb)[:, 0:n], wslot[:, vg, kc, co:co + 128], hnT[:, kc, c0:c0 + n], kc == 0, kc == 15, [wk, "hnT"], [PK[bb]])
                            actf(sg[:, 0:n], psf(bg)[:, 0:n], AF.Sigmoid, [PK[bg], "sg"], ["sg"])
                            tt(tg[:, 0:n], psf(bg)[:, 0:n], sg[:, 0:n], ALU.mult, [PK[bg], "sg", "tg"], ["tg"])
                            tt(act[:, c11, c0:c0 + n], tg[:, 0:n], psf(bu)[:, 0:n], ALU.mult, ["tg", PK[bu]], ["act"])
                    for nb in range(4):
                        bs_ = [kb.ps() for _ in range(NT)]
                        for (k0, kn) in ((0, 6), (6, 5)):
                            wslot, wk = wdn.get(di); di += 1
                            for kk in range(kn):
                                k2 = k0 + kk
                                for t in range(NT):
                                    mm(psf(bs_[t])[:, :], act[:, k2, t * 128:(t + 1) * 128], wslot[:, kk, :], k2 == 0, k2 == 10, ["act", wk], [PK[bs_[t]]])
                        for t in range(NT):
                            tt(hm[t][:, nb * 512:(nb + 1) * 512], psf(bs_[t])[:, :], hm[t][:, nb * 512:(nb + 1) * 512], ALU.add,
                               [PK[bs_[t]], f"hm{t}"], [f"hm{t}"])
            load_g(norm_final, "final")
            for t in range(NT):
                actf(junk[:], hm[t][:], AF.Square, [f"hm{t}", "junk"], ["junk", "col0"], accum=col[:, 0:1])
                ts(col[:, 1:2], col[:, 0:1], 1.0 / D, EPS, ALU.mult, ALU.add, ["col0"], ["col1"])
                rsqrt(col[:, 2:3], col[:, 1:2], 1, ["col1"], ["col2"])
                stt(hm[t][:], hm[t][:], col[:, 2:3], gbc[:], ALU.mult, ALU.mult, [f"hm{t}", "col2", "gbc"], [f"hm{t}"])
                kb.dma("sp", rows[t][1], hm[t][:], R=[f"hm{t}"], W=[f"oy{grp}{t}"], key="oy")
            if not kb.dead:
                kb.eng["sp"]["h"].wait_ge(kb.sems["d_oy"], kb.dmas["d_oy"])
            if grp == 0:
                stage("f", dumps=[("hnT", hnT, ["hnT"])])
        barrier()
    for sname, val in kb.dmas.items():
        if sname.startswith("d_o"):
            kb.eng["sp"]["h"].wait_ge(kb.sems[sname], val)
    kb.barrier()
    nc._dbg_names = dbg_names
    return nc, es


_CACHE = {}


def kernel(**inputs):
    inp = {k: np.ascontiguousarray(np.asarray(v), dtype=np.float32) for k, v in inputs.items()}
    if "nc" not in _CACHE:
        _CACHE["nc"] = build_program()
    nc, _es = _CACHE["nc"]
    ident = np.eye(128, dtype=np.float32)
    rho = np.arange(128); s_of = rho // 16
    colr = np.arange(128) // 16
    mask2 = (colr[None, :] >= s_of[:, None]).astype(np.float32)
    ident2 = np.eye(128, dtype=np.float32)
    shared = dict(
        norm_mix=inp["norm_mix"][0], norm_ffn=inp["norm_ffn"][0], norm_final=inp["norm_final"],
        w_in=inp["w_in"][0], conv_w=inp["conv_w"][0], conv_b=inp["conv_b"][0], ln_g=inp["conv_ln_g"][0], ln_b=inp["conv_ln_b"][0],
        A_re=inp["ssm_A_re"][0], A_im=inp["ssm_A_im"][0], log_dt=inp["ssm_log_dt"][0],
        B_re=inp["ssm_B_re"][0], B_im=inp["ssm_B_im"][0], C_re=inp["ssm_C_re"][0], C_im=inp["ssm_C_im"][0], ssm_D=inp["ssm_D"][0],
        w_glu=inp["w_glu"][0], gn_c=inp["gnorm_conv"][0], gn_s=inp["gnorm_ssm"][0], w_out=inp["w_out"][0],
        w_g=inp["w_ffn_gate"][0], w_u=inp["w_ffn_up"][0], w_d=inp["w_ffn_down"][0],
        ident=ident, mask2=mask2, ident2=ident2, pmask=np.stack([(np.arange(128) < 64), (np.arange(128) >= 64)], 1).astype(np.float32))
    shared = {k: np.ascontiguousarray(v) for k, v in shared.items()}
    in_maps = []
    for c in range(8):
        b, half = c // 2, c % 2
        m = dict(shared)
        m["xm"] = np.ascontiguousarray(inp["x_prompt"][b, half * 1024:(half + 1) * 1024])
        m["xp"] = np.ascontiguousarray(inp["x_prompt"][b, 0:1024]) if half == 1 else np.zeros((1024, D), np.float32)
        m["xs"] = np.ascontiguousarray(inp["x_sample"][16 * c:16 * c + 16].reshape(128, D))
        m["stc"] = np.ascontiguousarray(inp["state_conv"][0, 16 * c:16 * c + 16])
        m["sre"] = np.ascontiguousarray(inp["state_ssm_re"][0, 16 * c:16 * c + 16])
        m["sim"] = np.ascontiguousarray(inp["state_ssm_im"][0, 16 * c:16 * c + 16])
        in_maps.append(m)
    res = run_bass_kernel_spmd(nc, in_maps, core_ids=list(range(8)))
    R = res.results
    y_prompt = np.zeros((4, 2048, D), np.float32); y_sample = np.zeros((128, 8, D), np.float32)
    ncp = np.zeros((1, 4, 30, DC), np.float32); nrp = np.zeros((1, 4, 64, 64), np.float32); nip = np.zeros((1, 4, 64, 64), np.float32)
    ncs = np.zeros((1, 128, 30, DC), np.float32); nrs = np.zeros((1, 128, 64, 64), np.float32); nis = np.zeros((1, 128, 64, 64), np.float32)
    for c in range(8):
        b, half = c // 2, c % 2
        y_prompt[b, half * 1024:(half + 1) * 1024] = R[c]["o_ym"]
        y_sample[16 * c:16 * c + 16] = R[c]["o_ys"].reshape(16, 8, D)
        ncs[0, 16 * c:16 * c + 16] = R[c]["o_cs"]; nrs[0, 16 * c:16 * c + 16] = R[c]["o_rs"]; nis[0, 16 * c:16 * c + 16] = R[c]["o_is"]
        if half == 1:
            ncp[0, b] = R[c]["o_cp"]; nrp[0, b] = R[c]["o_rp"]; nip[0, b] = R[c]["o_ip"]
    return (y_prompt, y_sample, ncp, nrp, nip, ncs, nrs, nis)
```

```python
import numpy as np
from contextlib import ExitStack
import concourse.bass as bass
import concourse.mybir as mybir
from concourse.bass_utils import run_bass_kernel_spmd

F32 = mybir.dt.float32
BF16 = mybir.dt.bfloat16
AF = mybir.ActivationFunctionType
ALU = mybir.AluOpType

D = 2048
DC = 1024
DFF = 5632
EPS = 1e-6
NCH = 44
SEM_CH = 12000
TWO_PI = 6.283185307179586
GC1 = 1.5957691216057308
GC2 = GC1 * 0.044715


class KB:
    def __init__(self, nc, es):
        self.nc = nc
        self.es = es
        self.eng = {}
        for name, h in (("pe", nc.tensor), ("act", nc.scalar), ("dve", nc.vector), ("pool", nc.gpsimd), ("sp", nc.sync)):
            self.eng[name] = dict(h=h, n=0, seen={}, sems=[])
        self.sems = {}
        self.lastw = {}
        self.readers = {}
        self.dmas = {}
        self.psn = 0
        self.dead = False
        self.reserved = set()
        self.fence = None

    def sem(self, name):
        if name not in self.sems:
            self.sems[name] = self.es.enter_context(self.nc.semaphore(name))
        return self.sems[name]

    def _wait(self, en, tok):
        e = self.eng[en]
        sname, val, pen, pidx = tok
        if pen == en:
            if en in ("pe", "sp"):
                return
            if e["n"] - pidx > 3:
                return
        if pen == "dma":
            val = self.dmas[sname]
        if e["seen"].get(sname, 0) >= val:
            return
        e["h"].wait_ge(self.sems[sname], val)
        e["seen"][sname] = val

    def _deps(self, en, R, W):
        toks = {}
        for k in R:
            t = self.lastw.get(k)
            if t is not None:
                toks[(t[0], t[2])] = max(toks.get((t[0], t[2]), (0, 0)), (t[1], t[3] if t[3] is not None else 0))
        for k in W:
            t = self.lastw.get(k)
            if t is not None:
                toks[(t[0], t[2])] = max(toks.get((t[0], t[2]), (0, 0)), (t[1], t[3] if t[3] is not None else 0))
            for t in self.readers.get(k, {}).values():
                toks[(t[0], t[2])] = max(toks.get((t[0], t[2]), (0, 0)), (t[1], t[3] if t[3] is not None else 0))
        for (sname, pen), (val, pidx) in toks.items():
            self._wait(en, (sname, val, pen, pidx))

    def _record(self, tok, R, W):
        for k in W:
            self.lastw[k] = tok
            self.readers[k] = {}
        for k in R:
            d = self.readers.setdefault(k, {})
            old = d.get(tok[0])
            if old is None or old[1] < tok[1]:
                d[tok[0]] = tok

    def op(self, en, fn, R=(), W=()):
        if self.dead:
            return None
        e = self.eng[en]
        self._deps(en, R, W)
        ins = fn(e["h"])
        si = e["n"] // SEM_CH
        sname = f"s_{en}{si}"
        ins.then_inc(self.sem(sname), 1)
        val = e["n"] % SEM_CH + 1
        tok = (sname, val, en, e["n"])
        e["n"] += 1
        self._record(tok, R, W)
        return tok

    def dma(self, qn, out, in_, R, W, key):
        if self.dead:
            return None
        e = self.eng[qn]
        fk = []
        for k in R:
            t = self.lastw.get(k)
            if t is not None and t[2] in ("act", "dve", "pool") and self.fence is not None:
                en_ = t[2]
                f = self.fence
                if en_ == "act":
                    self.op(en_, lambda h: h.copy(out=f[:, 0:1], in_=f[:, 1:2]), R=[k], W=["fence_" + en_])
                else:
                    self.op(en_, lambda h: h.tensor_copy(out=f[:, 2:3] if en_ == "dve" else f[:, 4:5], in_=f[:, 3:4] if en_ == "dve" else f[:, 5:6]),
                            R=[k], W=["fence_" + en_])
                fk.append("fence_" + en_)
        R = list(R) + fk
        self._deps(qn, R, W)
        sname = "d_" + key
        s = self.sem(sname)
        ins = e["h"].dma_start(out=out, in_=in_)
        ins.then_inc(s, 16)
        self.dmas[sname] = self.dmas.get(sname, 0) + 16
        tok = (sname, self.dmas[sname], "dma", None)
        self._record(tok, R, W)
        return tok

    def barrier(self):
        if self.dead:
            return
        toks = []
        for en, e in self.eng.items():
            if e["n"] > 0:
                last = e["n"] - 1
                toks.append((f"s_{en}{last // SEM_CH}", last % SEM_CH + 1, en, last))
        for sname, val in self.dmas.items():
            toks.append((sname, val, "dma", None))
        for en, e in self.eng.items():
            for (sname, val, pen, pidx) in toks:
                if pen == en:
                    continue
                if e["seen"].get(sname, 0) >= val:
                    continue
                e["h"].wait_ge(self.sems[sname], val)
                e["seen"][sname] = val
        self.lastw.clear()
        self.readers.clear()

    def ps(self, reserve=False):
        while True:
            i = self.psn
            self.psn = (self.psn + 1) % 8
            if i not in self.reserved:
                break
        if reserve:
            self.reserved.add(i)
        return i

    def release(self, i):
        self.reserved.discard(i)


class _Stop(Exception):
    pass


def build_program(stop=None):
    nc = bass.Bass("TRN2", target_bir_lowering=False)
    es = ExitStack()
    kb = KB(nc, es)

    def din(name, shape):
        return nc.dram_tensor(name, list(shape), F32, kind="ExternalInput").ap()

    def dout(name, shape):
        return nc.dram_tensor(name, list(shape), F32, kind="ExternalOutput").ap()

    xm = din("xm", (1024, D)); xp = din("xp", (1024, D)); xs = din("xs", (128, D))
    stc = din("stc", (16, 30, DC)); sre = din("sre", (16, 64, 64)); sim = din("sim", (16, 64, 64))
    norm_mix = din("norm_mix", (D,)); norm_ffn = din("norm_ffn", (D,)); norm_final = din("norm_final", (D,))
    w_in = din("w_in", (D, 3072)); conv_w = din("conv_w", (31, DC)); conv_b = din("conv_b", (DC,))
    ln_g = din("ln_g", (DC,)); ln_b = din("ln_b", (DC,))
    A_re = din("A_re", (64, 64)); A_im = din("A_im", (64, 64)); log_dt = din("log_dt", (64,))
    B_re = din("B_re", (64, 64, 16)); B_im = din("B_im", (64, 64, 16))
    C_re = din("C_re", (64, 16, 64)); C_im = din("C_im", (64, 16, 64)); ssm_D = din("ssm_D", (DC,))
    w_glu = din("w_glu", (DC, DC)); gn_c = din("gn_c", (DC,)); gn_s = din("gn_s", (DC,))
    w_out = din("w_out", (D, D)); w_g = din("w_g", (D, DFF)); w_u = din("w_u", (D, DFF)); w_d = din("w_d", (DFF, D))
    pmaskd = din("pmask", (128, 2))
    identd = din("ident", (128, 128)); maskd = din("mask2", (128, 128)); ident2d = din("ident2", (128, 128))

    o_ym = dout("o_ym", (1024, D)); o_ys = dout("o_ys", (128, D))
    o_cp = dout("o_cp", (30, DC)); o_rp = dout("o_rp", (64, 64)); o_ip = dout("o_ip", (64, 64))
    o_cs = dout("o_cs", (16, 30, DC)); o_rs = dout("o_rs", (16, 64, 64)); o_is = dout("o_is", (16, 64, 64))

    d_MS2r = nc.dram_tensor("d_MS2r", [128, 4096], BF16, kind="Internal").ap()
    d_MS2i = nc.dram_tensor("d_MS2i", [128, 4096], BF16, kind="Internal").ap()
    d_MY1r = nc.dram_tensor("d_MY1r", [128, 4096], BF16, kind="Internal").ap()
    d_MY1i = nc.dram_tensor("d_MY1i", [128, 4096], BF16, kind="Internal").ap()
    d_MY2 = nc.dram_tensor("d_MY2", [128, 8192], BF16, kind="Internal").ap()
    d_Cj = nc.dram_tensor("d_Cj", [128, 2048], F32, kind="Internal").ap()
    d_Sj = nc.dram_tensor("d_Sj", [128, 2048], F32, kind="Internal").ap()
    d_D0 = nc.dram_tensor("d_D0", [128, 2048], F32, kind="Internal").ap()

    uniq = {"n": 0}

    def sbL(stack, name, shape, dt=F32):
        uniq["n"] += 1
        return stack.enter_context(nc.sbuf_tensor(f"{name}_{uniq['n']}", list(shape), dt))

    def sb(name, shape, dt=F32):
        return sbL(es, name, shape, dt)

    psum = [es.enter_context(nc.psum_tensor(f"ps{i}", [128, 512], F32)) for i in range(8)]

    def psf(i):
        return psum[i][:]

    def psb(i):
        return psum[i][:].bitcast(BF16)

    PK = [f"ps{i}" for i in range(8)]

    def barrier():
        kb.barrier()

    dbg_names = []

    def stage(name, dumps=()):
        if stop != name or kb.dead:
            return
        for (dn, t_, keys) in dumps:
            shp = list(t_.shape)
            d_ = nc.dram_tensor("dbg_" + dn, shp, t_.dtype, kind="ExternalOutput").ap()
            idx = tuple(slice(None) for _ in shp)
            kb.dma("sp", d_[idx], t_[idx] if not isinstance(t_, bass.AP) else t_, R=keys, W=["dbg_" + dn], key="o_dbg")
            dbg_names.append("dbg_" + dn)
        kb.dead = True

    def cp(eng, out, in_, R, W):
        if eng == "act":
            return kb.op("act", lambda h: h.copy(out=out, in_=in_), R=R, W=W)
        return kb.op(eng, lambda h: h.tensor_copy(out=out, in_=in_), R=R, W=W)

    def tt(out, a, b, op, R, W, eng="dve"):
        return kb.op(eng, lambda h: h.tensor_tensor(out=out, in0=a, in1=b, op=op), R=R, W=W)

    def ts(out, a, s1, s2, op0, op1, R, W):
        if op1 is None:
            return kb.op("dve", lambda h: h.tensor_scalar(out=out, in0=a, scalar1=s1, scalar2=None, op0=op0), R=R, W=W)
        return kb.op("dve", lambda h: h.tensor_scalar(out=out, in0=a, scalar1=s1, scalar2=s2, op0=op0, op1=op1), R=R, W=W)

    def stt(out, a, sc, b, op0, op1, R, W):
        return kb.op("dve", lambda h: h.scalar_tensor_tensor(out=out, in0=a, scalar=sc, in1=b, op0=op0, op1=op1), R=R, W=W)

    def actf(out, in_, func, R, W, scale=None, bias=None, accum=None):
        kw = {}
        if scale is not None:
            kw["scale"] = scale
        if bias is not None:
            kw["bias"] = bias
        if accum is not None:
            kw["accum_out"] = accum
        return kb.op("act", lambda h: h.activation(out=out, in_=in_, func=func, **kw), R=R, W=W)

    def mm(out, lhsT, rhs, start, stop, R, W):
        return kb.op("pe", lambda h: h.matmul(out, lhsT=lhsT, rhs=rhs, start=start, stop=stop), R=R, W=W)

    def tr(out, in_, ident, R, W):
        return kb.op("pe", lambda h: h.transpose(out=out, in_=in_, identity=ident), R=R, W=W)

    def ld(dst, src, key, W, q="sp", R=()):
        if key in ("c0", "c_aa", "c_bb", "c_mk"):
            key = "k_" + W[0]
        return kb.dma(q, dst, src, R=R, W=W, key=key)

    ident_f = sb("ident_f", (128, 128)); ident_b = sb("ident_b", (128, 128), BF16)
    ones_b = sb("ones_b", (128, 128), BF16)
    neghalf = sb("neghalf", (128, 640))
    pm = sb("pm", (128, 2))
    fence_t = sb("fence_t", (128, 8))
    kb.op("pool", lambda h: h.memset(fence_t[:], 0.0), W=["fence_act", "fence_dve", "fence_pool"])
    kb.fence = fence_t
    cb_t = sb("cb_t", (128, 8)); lng_t = sb("lng_t", (128, 8)); lnb_t = sb("lnb_t", (128, 8))
    gnc_t = sb("gnc_t", (128, 8)); gns_t = sb("gns_t", (128, 8))
    cwT = sb("cwT", (128, 8, 31))
    A8r = sb("A8r", (128, 32)); A8i = sb("A8i", (128, 32))

    ld(ident_f[:], identd[:, :], "c0", ["ident_f"])
    ld(pm[:], pmaskd[:, :], "c0", ["pm"])
    cp("dve", ident_b[:], ident_f[:], ["ident_f"], ["ident_b"])
    kb.op("pool", lambda h: h.memset(ones_b[:], 1.0), W=["ones_b"])
    kb.op("pool", lambda h: h.memset(neghalf[:], -0.5), W=["neghalf"])
    with nc.allow_non_contiguous_dma(reason="small param vectors"):
        for t, v, nm in ((cb_t, conv_b, "cb"), (lng_t, ln_g, "lng"), (lnb_t, ln_b, "lnb"), (gnc_t, gn_c, "gnc"), (gns_t, gn_s, "gns")):
            ld(t[:], v.rearrange("(q p) -> p q", p=128), "c0", [nm])

    stage("s0", dumps=[("ident_b", ident_b, ["ident_b"]), ("cb", cb_t, ["cb"]), ("ones", ones_b, ["ones_b"])])

    nwt = sb("nwt", (128, 640))

    def rsqrt(dst, src, n, Rk, Wk):
        if n <= 8:
            kb.op("pool", lambda h: h.tensor_tensor(out=dst, in0=src, in1=neghalf[:, 0:n], op=ALU.pow), R=Rk + ["neghalf"], W=Wk)
            return
        actf(dst, src, AF.Sqrt, Rk + Wk, Wk)
        kb.op("dve", lambda h: h.reciprocal(out=dst, in_=dst), R=Wk, W=Wk)
        t = nwt[:, 0:n]
        for _ in range(2):
            tt(t, dst, dst, ALU.mult, Wk + ["nwt"], ["nwt"])
            tt(t, t, src, ALU.mult, Rk + ["nwt"], ["nwt"])
            ts(t, t, -0.5, 1.5, ALU.mult, ALU.add, ["nwt"], ["nwt"])
            tt(dst, dst, t, ALU.mult, Wk + ["nwt"], Wk)

    gstate = {"g": None}

    def load_g(vec, name):
        if gstate["g"] != name:
            ld(gbc[:], vec.partition_broadcast(128), "gbc", ["gbc"])
            gstate["g"] = name

    def norm_T(src_ap, src_keys, dst_fn, dst_keys):
        i_ = nt_cnt["n"] % 2; nt_cnt["n"] += 1
        xbn = xbs[i_]; xk_ = f"xb{i_}"
        c0_ = 3 * i_
        ck = [f"col{c0_}", f"col{c0_ + 1}", f"col{c0_ + 2}"]
        actf(junk[:], src_ap, AF.Square, src_keys + ["junk"], ["junk", ck[0]], accum=col[:, c0_:c0_ + 1])
        ts(col[:, c0_ + 1:c0_ + 2], col[:, c0_:c0_ + 1], 1.0 / D, EPS, ALU.mult, ALU.add, [ck[0]], [ck[1]])
        rsqrt(col[:, c0_ + 2:c0_ + 3], col[:, c0_ + 1:c0_ + 2], 1, [ck[1]], [ck[2]])
        stt(xbn[:], src_ap, col[:, c0_ + 2:c0_ + 3], gbc[:], ALU.mult, ALU.mult, src_keys + [ck[2], "gbc", xk_], [xk_])
        for hf in range(2):
            b = kb.ps()
            for k8 in range(8):
                kc = hf * 8 + k8
                tr(psb(b)[:, k8 * 128:(k8 + 1) * 128], xbn[:, kc * 128:(kc + 1) * 128], ident_b[:], [xk_, "ident_b"], [PK[b]])
            src = psb(b)[:, 0:1024].rearrange("p (a r) -> p a r", a=8)
            cp("act" if hf == 0 else "dve", dst_fn(hf), src, [PK[b]], dst_keys)

    xcount = {"n": 0}

    def load_x(rows_ap, r=128, c=D):
        s = xcount["n"] % 2
        xcount["n"] += 1
        ld(xt[s][0:r, 0:c], rows_ap, f"xt{s}", [f"xt{s}"])
        return xt[s], f"xt{s}"

    class Stream:
        def __init__(self, stack, name, nslots, shape, loads):
            self.name = name; self.n = nslots
            self.slots = [sbL(stack, f"{name}{i}", shape, BF16) for i in range(nslots)]
            self.loads = loads; self.issued = 0

        def ensure(self, upto):
            while self.issued <= min(upto, len(self.loads) - 1):
                i = self.issued; s = i % self.n
                for (dst_fn, src) in self.loads[i]:
                    kb.dma("pool", dst_fn(self.slots[s]), src, R=(), W=[f"{self.name}{s}"], key=f"{self.name}{s}")
                self.issued += 1

        def get(self, i):
            self.ensure(i + self.n - 1)
            s = i % self.n
            return self.slots[s], f"{self.name}{s}"

    w_in_v = w_in.rearrange("(kc p) n -> p kc n", p=128)
    w_out_v = w_out.rearrange("(kc p) n -> p kc n", p=128)
    w_glu_v = w_glu.rearrange("(kc p) n -> p kc n", p=128)
    w_g_v = w_g.rearrange("(kc p) n -> p kc n", p=128)
    w_u_v = w_u.rearrange("(kc p) n -> p kc n", p=128)
    w_d_v = w_d.rearrange("(kc p) n -> p kc n", p=128)


    with ExitStack() as ses:
        def sbt(name, shape, dt=F32):
            return sbL(ses, name, shape, dt)
        tA = sbt("tA", (128, 128)); tB = sbt("tB", (128, 128))
        twopi = sbt("twopi", (128, 32))
        ArT = sbt("ArT", (128, 32)); AiT = sbt("AiT", (128, 32)); LdT = sbt("LdT", (128, 32))
        s1 = sbt("s1", (128, 32)); s2_ = sbt("s2_", (128, 32)); s3 = sbt("s3", (128, 32)); s4 = sbt("s4", (128, 32))
        abr = sbt("abr", (128, 32)); abi = sbt("abi", (128, 32)); kr = sbt("kr", (128, 32)); ki = sbt("ki", (128, 32))
        ivr = sbt("ivr", (128, 32)); ivi = sbt("ivi", (128, 32))
        pwr = sbt("pwr", (128, 32, 8)); pwi = sbt("pwi", (128, 32, 8)); nwr = sbt("nwr", (128, 32, 8)); nwi = sbt("nwi", (128, 32, 8))
        CrT = sbt("CrT", (128, 32, 16)); CiT = sbt("CiT", (128, 32, 16))
        Br = sbt("Br", (128, 32, 16)); Bi = sbt("Bi", (128, 32, 16)); Bbr = sbt("Bbr", (128, 32, 16)); Bbi = sbt("Bbi", (128, 32, 16))
        t16a = sbt("t16a", (128, 32, 16)); t16b = sbt("t16b", (128, 32, 16))
        Qr = sbt("Qr", (128, 32, 128)); Qi = sbt("Qi", (128, 32, 128))
        Mr = sbt("Mr", (128, 32, 128)); Mi = sbt("Mi", (128, 32, 128))
        Y1r = sbt("Y1r", (128, 32, 128)); Y1i = sbt("Y1i", (128, 32, 128))
        big = sbt("big", (128, 32, 128))
        mask2 = sbt("mask2", (128, 128)); ident2 = sbt("ident2", (128, 128)); Dcol = sbt("Dcol", (128, 64))
        cwl = sbt("cwl", (32, DC))
        tmy = sbt("tmy", (128, 2, 128))
        MS2r = sbt("MS2r", (128, 4096), BF16); MS2i = sbt("MS2i", (128, 4096), BF16)
        MY1rb = sbt("MY1rb", (128, 4096), BF16); MY1ib = sbt("MY1ib", (128, 4096), BF16)
        MY2 = sbt("MY2", (128, 8192), BF16)

        ld(mask2[:], maskd[:, :], "c_mk", ["mask2"])
        ld(ident2[:], ident2d[:, :], "c_mk", ["ident2"])
        with nc.allow_non_contiguous_dma(reason="small param vectors"):
            for m in range(8):
                ld(Dcol[16 * m:16 * m + 16, :], ssm_D.rearrange("(g c) -> c g", c=16), "c_dc", ["Dcol"])
            ld(tmy[:, 0, 0:64], log_dt.partition_broadcast(128), "c_ld", ["tmy"])
            for ge in range(2):
                cp("dve", LdT[64 * ge:64 * ge + 64, :], tmy[64 * ge:64 * ge + 64, 0, ge:64:2], ["tmy"], ["LdT"])
            for ge in range(2):
                ld(Br[64 * ge:64 * ge + 64, :, :], B_re.rearrange("(gp ge) p c -> ge p gp c", ge=2)[ge], "c_bb", ["Br"])
                ld(Bi[64 * ge:64 * ge + 64, :, :], B_im.rearrange("(gp ge) p c -> ge p gp c", ge=2)[ge], "c_bb", ["Bi"])
        with nc.allow_non_contiguous_dma(reason="one-time conv weight transpose load"):
            for q in range(8):
                ld(cwT[:, q, :], conv_w[:, q * 128:(q + 1) * 128].rearrange("k p -> p k"), "c_cw", ["cwT"])
        stage("s1", dumps=[("cwT", cwT, ["cwT"]), ("Br", Br, ["Br"]), ("LdT", LdT, ["LdT"]), ("Dcol", Dcol, ["Dcol"])])
        for (src, dst, nm, tAx, tk) in ((A_re, ArT, "ArT", tA, "tA"), (A_im, AiT, "AiT", tB, "tBA")):
            kb.op("pool", lambda h, tAx=tAx: h.memset(tAx[:], 0.0), W=[tk])
            ld(tAx[0:32, :], src.rearrange("(gp ge) p -> gp (ge p)", ge=2), "c_aa", [tk])
            b = kb.ps()
            tr(psf(b)[:, 0:128], tAx[:, :], ident_f[:], [tk, "ident_f"], [PK[b]])
            cp("act", dst[:], psf(b)[:, 0:32], [PK[b]], [nm])
        tBs = [sbt(f"tB{i}", (128, 128)) for i in range(8)]
        ti = 0
        for (src, dst, nm) in ((C_re, CrT, "CrT"), (C_im, CiT, "CiT")):
            for blk in range(4):
                tBx = tBs[ti]; tk = f"tB{ti}"; ti += 1
                for gl in range(8):
                    gp = blk * 8 + gl
                    ld(tBx[16 * gl:16 * gl + 16, :].rearrange("c (ge p) -> c ge p", ge=2),
                       src[2 * gp:2 * gp + 2].rearrange("ge c p -> c ge p"), "c_" + tk, [tk])
                b = kb.ps()
                tr(psf(b)[:, 0:128], tBx[:, :], ident_f[:], [tk, "ident_f"], [PK[b]])
                cp("act", dst[:, blk * 8:(blk + 1) * 8, :], psf(b)[:, 0:128].rearrange("p (g c) -> p g c", c=16), [PK[b]], [nm])
        PI = float(np.pi)

        def range_reduce(shift):
            ts(s4[:], s3[:], shift, None, ALU.add, None, ["ang", "abi", "abr", "s4"], ["s4"])
            ts(twopi[:], s4[:], -PI, None, ALU.add, None, ["s4", "twopi"], ["twopi"])
            for m in range(1, 6):
                ts(kr[:], s4[:], TWO_PI * m, -TWO_PI, ALU.is_ge, ALU.mult, ["s4", "kr"], ["kr"])
                tt(twopi[:], twopi[:], kr[:], ALU.add, ["twopi", "kr"], ["twopi"])
            cp("dve", s4[:], twopi[:], ["twopi"], ["s4"])
        stage("s2", dumps=[("cwT", cwT, ["cwT"]), ("ArT", ArT, ["ArT"]), ("AiT", AiT, ["AiT"]), ("CrT", CrT, ["CrT"]), ("CiT", CiT, ["CiT"])])
        import math as _m

        def horner(dst, zz, cs, R, W):
            ts(dst, zz, float(cs[-1]), None, ALU.mult, None, R + W, W)
            for c in reversed(cs[1:-1]):
                stt(dst, dst, float(c), zz, ALU.add, ALU.mult, R + W, W)
            ts(dst, dst, float(cs[0]), None, ALU.add, None, W, W)

        ecs = [1.0 / _m.factorial(k) for k in range(11)]
        ts(kr[:], LdT[:], 0.125, None, ALU.mult, None, ["LdT", "kr"], ["kr"])
        horner(s1[:], kr[:], ecs, ["kr"], ["dt"])
        for _ in range(3):
            tt(s1[:], s1[:], s1[:], ALU.mult, ["dt"], ["dt"])
        tt(kr[:], s1[:], ArT[:], ALU.mult, ["dt", "ArT", "kr"], ["kr"])
        horner(s2_[:], kr[:], ecs[:8], ["kr"], ["mag"])
        tt(s3[:], s1[:], AiT[:], ALU.mult, ["dt", "AiT"], ["ang"])
        range_reduce(PI)
        ts(s4[:], s4[:], 0.25, None, ALU.mult, None, ["s4"], ["s4"])
        tt(ki[:], s4[:], s4[:], ALU.mult, ["s4", "ki"], ["ki"])
        horner(abi[:], ki[:], [1.0, -1.0 / 6, 1.0 / 120, -1.0 / 5040, 1.0 / 362880], ["ki"], ["abi"])
        tt(abi[:], abi[:], s4[:], ALU.mult, ["abi", "s4"], ["abi"])
        horner(abr[:], ki[:], [1.0, -0.5, 1.0 / 24, -1.0 / 720, 1.0 / 40320, -1.0 / 3628800], ["ki"], ["abr"])
        for _ in range(2):
            tt(kr[:], abi[:], abr[:], ALU.mult, ["abi", "abr", "kr"], ["kr"])
            tt(ki[:], abi[:], abi[:], ALU.mult, ["abi", "ki"], ["ki"])
            ts(abi[:], kr[:], 2.0, None, ALU.mult, None, ["kr", "abi"], ["abi"])
            ts(abr[:], ki[:], -2.0, 1.0, ALU.mult, ALU.add, ["ki", "abr"], ["abr"])
        tt(abr[:], abr[:], s2_[:], ALU.mult, ["abr", "mag"], ["abr"])
        tt(abi[:], abi[:], s2_[:], ALU.mult, ["abi", "mag"], ["abi"])
        tt(s1[:], s2_[:], s2_[:], ALU.mult, ["mag", "dt"], ["m2"])
        kb.op("dve", lambda h: h.reciprocal(out=s1[:], in_=s1[:]), R=["m2"], W=["m2"])
        tt(ivr[:], abr[:], s1[:], ALU.mult, ["abr", "m2"], ["ivr"])
        tt(ivi[:], abi[:], s1[:], ALU.mult, ["abi", "m2"], ["ivi"])
        ts(ivi[:], ivi[:], -1.0, None, ALU.mult, None, ["ivi"], ["ivi"])
        tt(s1[:], ArT[:], ArT[:], ALU.mult, ["ArT", "ivr", "ivi", "m2"], ["den"])
        tt(s4[:], AiT[:], AiT[:], ALU.mult, ["AiT", "abr"], ["s4"])
        tt(s1[:], s1[:], s4[:], ALU.add, ["den", "s4"], ["den"])
        kb.op("dve", lambda h: h.reciprocal(out=s1[:], in_=s1[:]), R=["den"], W=["den"])
        ts(s3[:], abr[:], -1.0, None, ALU.add, None, ["abr", "ang", "s4"], ["nr"])
        tt(kr[:], s3[:], ArT[:], ALU.mult, ["nr", "ArT"], ["kr"])
        tt(s4[:], abi[:], AiT[:], ALU.mult, ["abi", "AiT", "den"], ["s4"])
        tt(kr[:], kr[:], s4[:], ALU.add, ["kr", "s4"], ["kr"])
        tt(kr[:], kr[:], s1[:], ALU.mult, ["kr", "den"], ["kr"])
        tt(ki[:], abi[:], ArT[:], ALU.mult, ["abi", "ArT"], ["ki"])
        tt(s4[:], s3[:], AiT[:], ALU.mult, ["nr", "AiT", "kr"], ["s4"])
        tt(ki[:], ki[:], s4[:], ALU.subtract, ["ki", "s4"], ["ki"])
        tt(ki[:], ki[:], s1[:], ALU.mult, ["ki", "den"], ["ki"])

        stage("s3", dumps=[("cwT", cwT, ["cwT"]), ("abr", abr, ["abr"]), ("abi", abi, ["abi"]), ("kr", kr, ["kr"]), ("ki", ki, ["ki"]), ("ivr", ivr, ["ivr"]), ("ivi", ivi, ["ivi"])])

        def cmul_into(dr, di, ar_, ai_, br_, bi_, R, W):
            tt(s2_[:], ar_, br_, ALU.mult, R + ["tmpa", "mag", "s2"], ["tmpa"])
            tt(s4[:], ai_, bi_, ALU.mult, R + ["tmpb", "s4", "ki"], ["tmpb"])
            tt(dr, s2_[:], s4[:], ALU.subtract, ["tmpa", "tmpb"], W)
            tt(s2_[:], ar_, bi_, ALU.mult, R + ["tmpa"] + W, ["tmpa"])
            tt(s4[:], ai_, br_, ALU.mult, R + ["tmpb"] + W, ["tmpb"])
            tt(di, s2_[:], s4[:], ALU.add, ["tmpa", "tmpb"], W)

        cp("dve", pwr[:, :, 0], abr[:], ["abr", "mag"], ["pw"])
        cp("dve", pwi[:, :, 0], abi[:], ["abi"], ["pw"])
        cp("dve", nwr[:, :, 0], ivr[:], ["ivr"], ["nw"])
        cp("dve", nwi[:, :, 0], ivi[:], ["ivi"], ["nw"])
        for k in range(1, 8):
            cmul_into(pwr[:, :, k], pwi[:, :, k], pwr[:, :, k - 1], pwi[:, :, k - 1], abr[:], abi[:], ["pw", "abr", "abi"], ["pw"])
            cmul_into(nwr[:, :, k], nwi[:, :, k], nwr[:, :, k - 1], nwi[:, :, k - 1], ivr[:], ivi[:], ["nw", "ivr", "ivi"], ["nw"])
        cp("dve", A8r[:], pwr[:, :, 7], ["pw"], ["A8"])
        cp("dve", A8i[:], pwi[:, :, 7], ["pw"], ["A8"])
        Tc = big[:, :, 0:64]; Ts = big[:, :, 64:128]; D0 = Mi[:, :, 0:64]; Tt = Mi[:, :, 64:96]
        rho = sbt("rho", (128, 32)); ur = sbt("ur", (128, 32)); ui = sbt("ui", (128, 32)); u2r = sbt("u2r", (128, 32)); u2i = sbt("u2i", (128, 32))
        tt(rho[:], A8r[:], A8r[:], ALU.mult, ["A8", "rho"], ["rho"])
        tt(u2r[:], A8i[:], A8i[:], ALU.mult, ["A8"], ["u2r"])
        tt(rho[:], rho[:], u2r[:], ALU.add, ["rho", "u2r"], ["rho"])
        kb.op("dve", lambda h: h.reciprocal(out=u2i[:], in_=rho[:]), R=["rho"], W=["u2i"])
        kb.op("pool", lambda h: h.memset(ur[:], 1.0), W=["ur"])
        for _ in range(7):
            tt(ui[:], ur[:], ur[:], ALU.mult, ["ur", "ui"], ["ui"])
            tt(ui[:], ui[:], rho[:], ALU.mult, ["ui", "rho"], ["ui"])
            ts(ui[:], ui[:], -0.5, 1.5, ALU.mult, ALU.add, ["ui"], ["ui"])
            tt(ur[:], ur[:], ui[:], ALU.mult, ["ur", "ui"], ["ur"])
        tt(rho[:], rho[:], ur[:], ALU.mult, ["rho", "ur"], ["rho"])
        tt(ui[:], A8i[:], ur[:], ALU.mult, ["A8", "ur", "ui"], ["ui"])
        tt(ur[:], A8r[:], ur[:], ALU.mult, ["A8", "ur"], ["ur"])
        kb.op("pool", lambda h: h.memset(Tc[:, :, 0:1], 1.0), W=["big"])
        kb.op("pool", lambda h: h.memset(Ts[:, :, 0:1], 0.0), W=["big"])
        cp("dve", u2r[:], ur[:], ["ur", "u2r", "rho"], ["u2"]); cp("dve", u2i[:], ui[:], ["ui", "u2i"], ["u2"])
        n_ = 1
        while n_ < 64:
            def bj(a):
                return a.unsqueeze(2).to_broadcast([128, 32, n_])
            tt(Tt[:, :, 0:n_], Tc[:, :, 0:n_], bj(u2r[:]), ALU.mult, ["big", "u2", "Mi"], ["Mi"])
            tt(Tc[:, :, n_:2 * n_], Ts[:, :, 0:n_], bj(u2i[:]), ALU.mult, ["big", "u2", "big"], ["big"])
            tt(Tc[:, :, n_:2 * n_], Tt[:, :, 0:n_], Tc[:, :, n_:2 * n_], ALU.subtract, ["Mi", "big"], ["big"])
            tt(Tt[:, :, 0:n_], Tc[:, :, 0:n_], bj(u2i[:]), ALU.mult, ["big", "u2", "Mi"], ["Mi"])
            tt(Ts[:, :, n_:2 * n_], Ts[:, :, 0:n_], bj(u2r[:]), ALU.mult, ["big", "u2"], ["big"])
            tt(Ts[:, :, n_:2 * n_], Tt[:, :, 0:n_], Ts[:, :, n_:2 * n_], ALU.add, ["Mi", "big"], ["big"])
            tt(Tt[:, :, 0], u2r[:], u2r[:], ALU.mult, ["u2", "Mi"], ["Mi"])
            tt(Tt[:, :, 1], u2i[:], u2i[:], ALU.mult, ["u2", "Mi"], ["Mi"])
            tt(Tt[:, :, 2], u2r[:], u2i[:], ALU.mult, ["u2", "Mi"], ["Mi"])
            tt(u2r[:], Tt[:, :, 0], Tt[:, :, 1], ALU.subtract, ["Mi", "u2"], ["u2"])
            ts(u2i[:], Tt[:, :, 2], 2.0, None, ALU.mult, None, ["Mi", "u2"], ["u2"])
            n_ *= 2
        cp("dve", D0, rho[:].unsqueeze(2).to_broadcast([128, 32, 64]), ["rho"], ["Mi"])
        kb.op("pool", lambda h: h.memset(D0[:, :, 0:1], 0.0), R=["Mi"], W=["Mi"])
        stage("s4b", dumps=[("big", big, ["big"]), ("Mi", Mi, ["Mi"]), ("rho", rho, ["rho"]), ("ur", ur, ["ur"]), ("ui", ui, ["ui"])])
        for (t_, d_, nm) in ((Tc, d_Cj, "big"), (Ts, d_Sj, "big"), (D0, d_D0, "Mi")):
            kb.dma("sp", d_.rearrange("p (g j) -> p g j", j=64), t_, R=[nm], W=["dstash"], key="stash")

        stage("s4", dumps=[("cwT", cwT, ["cwT"]), ("pwr", pwr, ["pw"]), ("pwi", pwi, ["pw"]), ("nwr", nwr, ["nw"]), ("nwi", nwi, ["nw"])])

        def bc16(a):
            return a.unsqueeze(2).to_broadcast([128, 32, 16])
        tt(t16a[:], Br[:], bc16(kr[:]), ALU.mult, ["Br", "kr"], ["t16a"])
        tt(t16b[:], Bi[:], bc16(ki[:]), ALU.mult, ["Bi", "ki"], ["t16b"])
        tt(Bbr[:], t16a[:], t16b[:], ALU.subtract, ["t16a", "t16b"], ["Bbr"])
        tt(t16a[:], Bi[:], bc16(kr[:]), ALU.mult, ["Bi", "kr", "Bbr"], ["t16a"])
        tt(t16b[:], Br[:], bc16(ki[:]), ALU.mult, ["Br", "ki", "Bbr"], ["t16b"])
        tt(Bbi[:], t16a[:], t16b[:], ALU.add, ["t16a", "t16b"], ["Bbi"])

        def v4(t):
            return t[:].rearrange("p g (s c) -> p g s c", c=16)

        def bs(t):
            return t[:].unsqueeze(3).to_broadcast([128, 32, 8, 16])

        def bcc(t):
            return t[:].unsqueeze(2).to_broadcast([128, 32, 8, 16])
        tt(v4(Qr), bs(nwr), bcc(Bbr), ALU.mult, ["nw", "Bbr"], ["Qr"])
        tt(v4(big), bs(nwi), bcc(Bbi), ALU.mult, ["nw", "Bbi"], ["big"])
        tt(Qr[:], Qr[:], big[:], ALU.subtract, ["Qr", "big"], ["Qr"])
        tt(v4(Qi), bs(nwr), bcc(Bbi), ALU.mult, ["nw", "Bbi"], ["Qi"])
        tt(v4(big), bs(nwi), bcc(Bbr), ALU.mult, ["nw", "Bbr", "Qr"], ["big"])
        tt(Qi[:], Qi[:], big[:], ALU.add, ["Qi", "big"], ["Qi"])

        def b128(a):
            return a.unsqueeze(2).to_broadcast([128, 32, 128])
        tt(Mr[:], Qr[:], b128(A8r[:]), ALU.mult, ["Qr", "A8"], ["Mr"])
        tt(big[:], Qi[:], b128(A8i[:]), ALU.mult, ["Qi", "A8"], ["big"])
        tt(Mr[:], Mr[:], big[:], ALU.subtract, ["Mr", "big"], ["Mr"])
        tt(Mi[:], Qi[:], b128(A8r[:]), ALU.mult, ["Qi", "A8"], ["Mi"])
        tt(big[:], Qr[:], b128(A8i[:]), ALU.mult, ["Qr", "A8", "Mr"], ["big"])
        tt(Mi[:], Mi[:], big[:], ALU.add, ["Mi", "big"], ["Mi"])
        tt(v4(Y1r), bs(pwr), bcc(CrT), ALU.mult, ["pw", "CrT"], ["Y1r"])
        tt(v4(big), bs(pwi), bcc(CiT), ALU.mult, ["pw", "CiT", "Mi"], ["big"])
        tt(Y1r[:], Y1r[:], big[:], ALU.subtract, ["Y1r", "big"], ["Y1r"])
        tt(v4(Y1i), bs(pwi), bcc(CrT), ALU.mult, ["pw", "CrT"], ["Y1i"])
        tt(v4(big), bs(pwr), bcc(CiT), ALU.mult, ["pw", "CiT", "Y1r"], ["big"])
        tt(Y1i[:], Y1i[:], big[:], ALU.add, ["Y1i", "big"], ["Y1i"])
        ts(Y1i[:], Y1i[:], -1.0, None, ALU.mult, None, ["Y1i"], ["Y1i"])
        cp("act", MY1rb[:].rearrange("p (g n) -> p g n", n=128), Y1r[:], ["Y1r"], ["MY1rb"])
        cp("act", MY1ib[:].rearrange("p (g n) -> p g n", n=128), Y1i[:], ["Y1i"], ["MY1ib"])
        stage("s5", dumps=[("cwT", cwT, ["cwT"]), ("Qr", Qr, ["Qr"]), ("Qi", Qi, ["Qi"]), ("Mr", Mr, ["Mr"]), ("Y1r", Y1r, ["Y1r"]), ("Y1i", Y1i, ["Y1i"])])
        for gp in range(32):
            b = kb.ps()
            for ri, M_ in enumerate((Mr, Mi)):
                tr(psf(b)[:, ri * 128:(ri + 1) * 128], M_[:, gp, :], ident_f[:], ["Mr", "Mi", "ident_f"], [PK[b]])
            cp("act", MS2r[:, gp * 128:(gp + 1) * 128], psf(b)[:, 0:128], [PK[b]], ["MS2r"])
            cp("act", MS2i[:, gp * 128:(gp + 1) * 128], psf(b)[:, 128:256], [PK[b]], ["MS2i"])
        stage("s6", dumps=[("cwT", cwT, ["cwT"]), ("MS2r", MS2r, ["MS2r"]), ("MS2i", MS2i, ["MS2i"])])
        MrB = Mr[:].bitcast(BF16); MiB = Mi[:].bitcast(BF16); bigB = big[:].bitcast(BF16)
        cp("act", MrB[:, :, 0:128], Qr[:], ["Qr", "Mr", "MS2r", "MS2i"], ["Mr"])
        cp("act", MrB[:, :, 128:256], Qi[:], ["Qi", "Mr"], ["Mr"])
        for ge in range(2):
            ts(MiB[:, :, ge * 128:(ge + 1) * 128], Y1r[:], pm[:, ge:ge + 1], None, ALU.mult, None, ["Y1r", "pm", "Mi", "MS2r", "MS2i"], ["Mi"])
            ts(bigB[:, :, ge * 128:(ge + 1) * 128], Y1i[:], pm[:, ge:ge + 1], None, ALU.mult, None, ["Y1i", "pm", "big", "MY1ib", "MY1rb"], ["big"])
        for gp in range(32):
            bge = (kb.ps(), kb.ps())
            for ge in range(2):
                mm(psf(bge[ge])[:, 0:128], MrB[:, gp, 0:128], MiB[:, gp, ge * 128:(ge + 1) * 128], True, False, ["Mr", "Mi"], [PK[bge[ge]]])
                mm(psf(bge[ge])[:, 0:128], MrB[:, gp, 128:256], bigB[:, gp, ge * 128:(ge + 1) * 128], False, True, ["Mr", "big"], [PK[bge[ge]]])
                tt(tmy[:, ge, :], psf(bge[ge])[:, 0:128], mask2[:], ALU.mult, [PK[bge[ge]], "mask2", "tmy"], ["tmy"])
            for ge in range(2):
                g = 2 * gp + ge
                stt(MY2[:, g * 128:(g + 1) * 128], ident2[:], Dcol[:, g:g + 1], tmy[:, ge, :], ALU.mult, ALU.add,
                    ["tmy", "ident2", "Dcol"], ["MY2"])
        stage("setup", dumps=[("A8r", A8r, ["A8"]), ("A8i", A8i, ["A8"]), ("MS2r", MS2r, ["MS2r"]), ("MS2i", MS2i, ["MS2i"]), ("MY2", MY2, ["MY2"]),
                              ("MY1rb", MY1rb, ["MY1rb"]), ("cwT", cwT, ["cwT"]), ("kr", kr, ["kr"]), ("abr", abr, ["abr"]), ("abi", abi, ["abi"])])
        for (t_, d_, nm) in ((MS2r, d_MS2r, "MS2r"), (MS2i, d_MS2i, "MS2i"), (MY1rb, d_MY1r, "MY1rb"), (MY1ib, d_MY1i, "MY1ib"), (MY2, d_MY2, "MY2")):
            kb.dma("sp", d_[:, :], t_[:], R=[nm], W=["dstash"], key="stash")
        if not kb.dead:
            kb.eng["act"]["h"].wait_ge(kb.sems["d_stash"], kb.dmas["d_stash"])
        barrier()

    gbc = sb("gbc", (128, D))
    xt = [sb(f"xt{i}", (128, D)) for i in range(2)]
    xbs = [sb(f"xb{i}", (128, D), BF16) for i in range(2)]
    junk = sb("junk", (128, D), BF16)
    xb = xbs[0]
    nt_cnt = {"n": 0}
    col = sb("col", (128, 8))
    HRc = sb("HRc", (128, 2, 32)); HIc = sb("HIc", (128, 2, 32))
    HRs = sb("HRs", (128, 16, 32)); HIs = sb("HIs", (128, 16, 32))
    concat = sb("concat", (128, 16, 640), BF16)
    utail = sb("utail", (128, 8, 32), BF16)
    rt = [sb(f"rt{i}", (128, 16, 32)) for i in range(4)]
    def P0_tiles(stack_x, tiles, xnT, prefix=False):
        load_g(norm_mix, "mix")
        for (rows, c0) in tiles:
            xtile, xk = load_x(rows)
            norm_T(xtile[:], [xk], lambda hf, c0=c0: xnT[:, hf * 8:(hf + 1) * 8, c0:c0 + 128], ["xnT"])

    for grp in range(2):
        has_s = (grp == 0)
        NTOK = 640 if has_s else 512
        NT = NTOK // 128
        NCOL = 32 + NTOK
        tiles_main = [(xm[grp * 512 + 128 * t: grp * 512 + 128 * (t + 1), :], 32 + 128 * t) for t in range(4)]
        if has_s:
            tiles_main.append((xs[:, :], 32 + 512))
        blocks = [(0, 512)] + ([(512, 128)] if has_s else [])

        with ExitStack() as x1:
            xnT = sbL(x1, "xnT", (128, 16, 672), BF16)
            u = sbL(x1, "u", (128, 8, 672), BF16)
            ups = sbL(x1, "ups", (128, 8, 16, 38), BF16)
            y32 = sbL(x1, "y32", (128, 8, 640))
            uf = sbL(x1, "uf", (128, 512)); sg = sbL(x1, "sg", (128, 512))
            dgs = [sbL(x1, f"dg{i}", (128, 31, 128), BF16) for i in range(2)]
            ybf = sbL(x1, "ybf", (128, 640), BF16); ysq = sbL(x1, "ysq", (128, 640), BF16)
            mu = sbL(x1, "mu", (128, 640)); rs = sbL(x1, "rs", (128, 640)); vt = sbL(x1, "vt", (128, 640))
            uout = sbL(x1, "uout", (128, 8, 160))
            kb.op("dve", lambda h: h.memset(uout[:], 0.0), W=["uout"])
            if grp == 1:
                kb.op("dve", lambda h: h.memset(xnT[:, :, 0:32], 0.0), W=["xnT"])
            if grp == 0:
                pass
            if grp == 0:
                P0_tiles(x1, [(xp[896:1024, :], 32)], xnT)
                cp("dve", xnT[:, :, 0:32], xnT[:, :, 128:160], ["xnT"], ["xnT"])
            P0_tiles(x1, tiles_main, xnT)
            if has_s:
                for i in range(4):
                    xtile, xk = load_x(stc[4 * i:4 * i + 4].rearrange("b i c -> (b i) c"), 120, DC)
                    for bl in range(4):
                        kb.dma("sp", o_cs[4 * i + bl, 0:22, :], xtile[bl * 30 + 8:bl * 30 + 30, 0:DC], R=[xk], W=["o_cs_a"], key="ocs")
                    for q in range(8):
                        b = kb.ps()
                        tr(psf(b)[:, 0:128], xtile[:, q * 128:(q + 1) * 128], ident_f[:], [xk, "ident_f"], [PK[b]])
                        cp("act" if q % 2 == 0 else "dve", ups[:, q, 4 * i:4 * i + 4, 0:30],
                           psf(b)[:, 0:120].rearrange("p (b i) -> p b i", i=30), [PK[b]], ["ups"])
            loads = [[(lambda s: s[:, 0, :, :], w_in_v[:, :, q * 128:(q + 1) * 128]),
                      (lambda s: s[:, 1, :, :], w_in_v[:, :, 1024 + q * 128:1024 + (q + 1) * 128])] for q in range(8)]
            with ExitStack() as wsx:
                wvg = Stream(wsx, "wvg", 2, (128, 2, 16, 128), loads)
                mblocks = [(0, 512), (512, NCOL - 512)]
                for q in range(8):
                    wslot, wk = wvg.get(q)
                    for (c0, n) in mblocks:
                        bv = kb.ps(); bg = kb.ps()
                        for vg, bb in ((0, bv), (1, bg)):
                            for kc in range(16):
                                mm(psf(bb)[:, 0:n], wslot[:, vg, kc, :], xnT[:, kc, c0:c0 + n], kc == 0, kc == 15, [wk, "xnT"], [PK[bb]])
                        actf(sg[:, 0:n], psf(bg)[:, 0:n], AF.Sigmoid, [PK[bg]], ["sg"])
                        tt(uf[:, 0:n], psf(bv)[:, 0:n], sg[:, 0:n], ALU.mult, [PK[bv], "sg"], ["uf"])
                        lo = 0 if (grp == 0 or c0 > 0) else 32
                        pe_ = min(c0 + n, 32 + 512)
                        if pe_ > c0 + lo:
                            cp("pool", u[:, q, c0 + lo:pe_], uf[:, lo:pe_ - c0], ["uf"], ["u"])
                        if has_s and c0 + n > 544:
                            s0 = 544 - c0
                            cp("act", ups[:, q, :, 30:38], uf[:, s0:s0 + 128].rearrange("p (b i) -> p b i", i=8), ["uf"], ["ups"])
                            cp("dve", uout[:, q, 32:160], uf[:, s0:s0 + 128], ["uf"], ["uout"])
                        if grp == 1 and c0 + n >= 544:
                            s0 = 514 - c0
                            cp("dve", uout[:, q, 0:30], uf[:, s0:s0 + 30], ["uf"], ["uout"])
                if grp == 1:
                    cp("dve", u[:, :, 2:32], utail[:, :, 2:32], ["utail"], ["u"])
                else:
                    pass
                if grp == 0:
                    cp("dve", utail[:, :, 2:32], u[:, :, 514:544], ["u"], ["utail"])
            if has_s:
                for tch in range(2):
                    xo = xt[xcount["n"] % 2]; xok = f"xt{xcount['n'] % 2}"; xcount["n"] += 1
                    for q4 in range(4):
                        q = tch * 4 + q4
                        b = kb.ps()
                        tr(psf(b)[:, 0:128], uout[:, q, 32:160], ident_f[:], ["uout", "ident_f"], [PK[b]])
                        cp("act" if q % 2 == 0 else "dve", xo[:, q * 128:(q + 1) * 128], psf(b)[:, 0:128], [PK[b]], [xok])
                for bq in range(16):
                    for tch in range(2):
                        pass
                s_a = (xcount["n"] - 2) % 2; s_b = (xcount["n"] - 1) % 2
                for bq in range(16):
                    kb.dma("sp", o_cs[bq, 22:30, 0:512], xt[s_a][bq * 8:(bq + 1) * 8, 0:512], R=[f"xt{s_a}"], W=["o_cs_b"], key="ocs")
                    kb.dma("sp", o_cs[bq, 22:30, 512:1024], xt[s_b][bq * 8:(bq + 1) * 8, 512:1024], R=[f"xt{s_b}"], W=["o_cs_b"], key="ocs")
            if grp == 1:
                xo = xt[xcount["n"] % 2]; xok = f"xt{xcount['n'] % 2}"; xcount["n"] += 1
                for q in range(8):
                    b = kb.ps()
                    tr(psf(b)[:, 0:128], uout[:, q, 0:128], ident_f[:], ["uout", "ident_f"], [PK[b]])
                    cp("act" if q % 2 == 0 else "dve", xo[0:30, q * 128:(q + 1) * 128], psf(b)[0:30, 0:128], [PK[b]], [xok])
                kb.dma("sp", o_cp[:, :], xo[0:30, 0:1024], R=[xok], W=["o_cp"], key="ocp")
            sblocks = [(0, 512, False)] + ([(512, 128, True)] if has_s else [])
            st_b = {}
            for (t0, n, is_s) in sblocks:
                st_b[t0] = (kb.ps(reserve=True), kb.ps(reserve=True))
            for q in range(8):
                dg = dgs[q % 2]; dgk = f"dg{q % 2}"
                for k in range(31):
                    ts(dg[:, k, :], ident_b[:], cwT[:, q, k:k + 1], None, ALU.mult, None, ["ident_b", "cwT", dgk], [dgk])
                for (t0, n, is_s) in sblocks:
                    b = kb.ps()
                    for k in range(31):
                        if is_s:
                            rhs = ups[:, q, :, k:k + 8]
                        else:
                            rhs = u[:, q, 2 + k:2 + k + 512]
                        mm(psf(b)[:, 0:n], dg[:, k, :], rhs, k == 0, k == 30, [dgk, "u", "ups"], [PK[b]])
                    actf(y32[:, q, t0:t0 + n], psf(b)[:, 0:n], AF.Identity, [PK[b], "cb"], ["y32"], bias=cb_t[:, q:q + 1])
                    actf(ybf[:, t0:t0 + n], psf(b)[:, 0:n], AF.Identity, [PK[b], "cb", "ybf"], ["ybf"], bias=cb_t[:, q:q + 1])
                    actf(ysq[:, t0:t0 + n], psf(b)[:, 0:n], AF.Square, [PK[b], "cb", "ysq"], ["ysq"], bias=cb_t[:, q:q + 1])
                    b1, b2 = st_b[t0]
                    mm(psf(b1)[:, 0:n], ones_b[:], ybf[:, t0:t0 + n], q == 0, q == 7, ["ones_b", "ybf"], [PK[b1]])
                    mm(psf(b2)[:, 0:n], ones_b[:], ysq[:, t0:t0 + n], q == 0, q == 7, ["ones_b", "ysq"], [PK[b2]])
            for (t0, n, is_s) in sblocks:
                b1, b2 = st_b[t0]
                ts(mu[:, t0:t0 + n], psf(b1)[:, 0:n], 1.0 / DC, None, ALU.mult, None, [PK[b1]], ["mu"])
                tt(vt[:, t0:t0 + n], mu[:, t0:t0 + n], mu[:, t0:t0 + n], ALU.mult, ["mu"], ["vt"])
                stt(vt[:, t0:t0 + n], psf(b2)[:, 0:n], 1.0 / DC, vt[:, t0:t0 + n], ALU.mult, ALU.subtract, [PK[b2], "vt"], ["vt"])
                ts(vt[:, t0:t0 + n], vt[:, t0:t0 + n], EPS, None, ALU.add, None, ["vt"], ["vt"])
                rsqrt(rs[:, t0:t0 + n], vt[:, t0:t0 + n], n, ["vt"], ["rs"])
                kb.release(b1); kb.release(b2)

            def rms_finish(src32, key32, gvec, gk, cbase):
                fb = {}
                for (t0, n, is_s) in sblocks:
                    fb[t0] = kb.ps()
                for q in range(8):
                    for (t0, n, is_s) in sblocks:
                        actf(ysq[:, t0:t0 + n], src32[:, q, t0:t0 + n], AF.Square, [key32, "ysq"], ["ysq"])
                        mm(psf(fb[t0])[:, 0:n], ones_b[:], ysq[:, t0:t0 + n], q == 0, q == 7, ["ones_b", "ysq"], [PK[fb[t0]]])
                for (t0, n, is_s) in sblocks:
                    ts(vt[:, t0:t0 + n], psf(fb[t0])[:, 0:n], 1.0 / DC, EPS, ALU.mult, ALU.add, [PK[fb[t0]], "vt"], ["vt"])
                    rsqrt(rs[:, t0:t0 + n], vt[:, t0:t0 + n], n, ["vt"], ["rs"])
                for q in range(8):
                    stt(concat[:, cbase + q, 0:NTOK], src32[:, q, 0:NTOK], gvec[:, q:q + 1], rs[:, 0:NTOK], ALU.mult, ALU.mult,
                        [key32, gk, "rs"], ["concat"])

            for q in range(8):
                tt(y32[:, q, 0:NTOK], y32[:, q, 0:NTOK], mu[:, 0:NTOK], ALU.subtract, ["y32", "mu"], ["y32"])
                tt(y32[:, q, 0:NTOK], y32[:, q, 0:NTOK], rs[:, 0:NTOK], ALU.mult, ["y32", "rs"], ["y32"])
                for (t0, n, is_s) in sblocks:
                    actf(uf[:, 0:n], y32[:, q, t0:t0 + n], AF.Identity, ["y32", "lng", "lnb", "uf"], ["uf"], scale=lng_t[:, q:q + 1], bias=lnb_t[:, q:q + 1])
                    actf(sg[:, 0:n], uf[:, 0:n], AF.Sigmoid, ["uf", "sg"], ["sg"])
                    tt(y32[:, q, t0:t0 + n], uf[:, 0:n], sg[:, 0:n], ALU.mult, ["uf", "sg", "y32"], ["y32"])
            rms_finish(y32, "y32", gnc_t, "gnc", 0)
            if grp == 0:
                stage("x1c", dumps=[("concat", concat, ["concat"])])
                stage("x1", dumps=[("concat", concat, ["concat"]), ("u", u, ["u"]), ("y32", y32, ["y32"]), ("xnT", xnT, ["xnT"]), ("ups", ups, ["ups"]), ("mu", mu, ["mu"]), ("rs", rs, ["rs"]), ("vt", vt, ["vt"]), ("ybf", ybf, ["ybf"]), ("cwT", cwT, ["cwT"])])
        if grp == 0:
            stage("x1d", dumps=[("concat", concat, ["concat"])])
        barrier()

        if grp == 0:
            stage("x1b", dumps=[("concat", concat, ["concat"])])
        with ExitStack() as e_:
            NBT = 80
            U = sbL(e_, "U", (128, 64, NBT), BF16)
            HRb = [sbL(e_, f"HRb{i}", (128, 32, NBT), BF16) for i in range(2)]
            HIb = [sbL(e_, f"HIb{i}", (128, 32, NBT), BF16) for i in range(2)]

            def run_A1(xnT, ZT, wst_stack, NB, cbase, uo):
                R2 = 2 * NB
                loads = [[(lambda s: s[:, :, :], w_in_v[:, :, 2048 + i * 256:2048 + (i + 1) * 256])] for i in range(4)]
                with ExitStack() as ws_:
                    wS = Stream(ws_, "wS", 2, (128, 16, 256), loads)
                    for nbk in range(4):
                        wslot, wk = wS.get(nbk)
                        for m in range(4):
                            b = kb.ps()
                            for kc in range(16):
                                lhs = xnT[:, kc, cbase + m:cbase + m + 8 * NB - 3:4]
                                mm(psf(b)[0:R2, 0:256], lhs, wslot[:, kc, :], kc == 0, kc == 15, ["xnT", wk], [PK[b]])
                            srcv = psf(b)[0:R2, 0:256].rearrange("r (g c) -> r g c", c=16)
                            cp("act", ZT[0:R2, nbk * 16:(nbk + 1) * 16, 0, m * 16:(m + 1) * 16], srcv, [PK[b]], ["ZT"])
                            cp("act", ZT[0:R2, nbk * 16:(nbk + 1) * 16, 1, m * 16:(m + 1) * 16], srcv, [PK[b]], ["ZT"])
                    for g8 in range(8):
                        b = kb.ps()
                        for gl in range(8):
                            g = g8 * 8 + gl
                            tr(psb(b)[:, gl * R2:(gl + 1) * R2], ZT[0:R2, g, :, :], ident_b[0:R2, 0:R2], ["ZT", "ident_b"], [PK[b]])
                        v = psb(b)[:, 0:16 * NB].rearrange("k (g j s) -> k g j s", g=8, s=2)
                        cp("act", U[0:64, g8 * 8:(g8 + 1) * 8, uo:uo + NB], v[0:64, :, :, 0], [PK[b]], ["U"])
                        cp("act", U[64:128, g8 * 8:(g8 + 1) * 8, uo:uo + NB], v[64:128, :, :, 1], [PK[b]], ["U"])

            def run_A2(a2, SRs, SIs, NB, uo, chain):
                if True:
                    for gq in range(4):
                        bR = kb.ps(); bI = kb.ps()
                        for gl in range(8):
                            gp = gq * 8 + gl
                            for ge in range(2):
                                g = 2 * gp + ge
                                for (bb, MS_, mk) in ((bR, MS2r, "MS2r"), (bI, MS2i, "MS2i")):
                                    mm(psf(bb)[64 * ge:64 * ge + 64, gl * NB:(gl + 1) * NB], MS_[:, gp * 128 + ge * 64:gp * 128 + ge * 64 + 64],
                                       U[:, g, uo:uo + NB], True, True, [mk, "U"], [PK[bb]])
                        for (bb, dst, nm, eng) in ((bR, SRs, "SRs", "act"), (bI, SIs, "SIs", "dve")):
                            src = psf(bb)[:, 0:8 * NB].rearrange("p (g j) -> p g j", g=8)
                            cp(eng, dst[:, gq * 8:(gq + 1) * 8, 0:NB], src, [PK[bb]], [nm])
                if chain:
                    wr = sbL(a2, "wr", (128, 32, 64)); wi = sbL(a2, "wi", (128, 32, 64))
                    t1 = sbL(a2, "t1", (128, 32, 64)); t2 = sbL(a2, "t2", (128, 32, 64))
                    hr = HRc[:, 0, :]; hi = HIc[:, 0, :]
                    tt(rt[0][:, 0, :], A8r[:], hr, ALU.mult, ["H0c", "A8", "rt0"], ["rt0"])
                    tt(rt[1][:, 0, :], A8i[:], hi, ALU.mult, ["H0c", "A8", "rt1"], ["rt1"])
                    tt(rt[2][:, 0, :], A8r[:], hi, ALU.mult, ["H0c", "A8", "rt2"], ["rt2"])
                    tt(rt[3][:, 0, :], A8i[:], hr, ALU.mult, ["H0c", "A8", "rt3"], ["rt3"])
                    tt(rt[0][:, 0, :], rt[0][:, 0, :], rt[1][:, 0, :], ALU.subtract, ["rt0", "rt1"], ["rt0"])
                    tt(rt[2][:, 0, :], rt[2][:, 0, :], rt[3][:, 0, :], ALU.add, ["rt2", "rt3"], ["rt2"])
                    tt(SRs[:, :, 0], SRs[:, :, 0], rt[0][:, 0, :], ALU.add, ["rt0", "SRs"], ["SRs"])
                    tt(SIs[:, :, 0], SIs[:, :, 0], rt[2][:, 0, :], ALU.add, ["rt2", "SIs"], ["SIs"])
                    for ge in range(2):
                        ts(HRb[ge][:, :, uo], hr, pm[:, ge:ge + 1], None, ALU.mult, None, ["H0c", "pm"], ["HRb"])
                        ts(HIb[ge][:, :, uo], hi, pm[:, ge:ge + 1], None, ALU.mult, None, ["H0c", "pm"], ["HIb"])
                    tt(t1[:], Cj[:], SRs[:], ALU.mult, ["Cj", "SRs", "t1"], ["t1"])
                    tt(t2[:], Sj[:], SIs[:], ALU.mult, ["Sj", "SIs", "t2"], ["t2"])
                    tt(wr[:], t1[:], t2[:], ALU.add, ["t1", "t2", "wr"], ["wr"])
                    tt(t1[:], Cj[:], SIs[:], ALU.mult, ["Cj", "SIs", "t1"], ["t1"])
                    tt(t2[:], Sj[:], SRs[:], ALU.mult, ["Sj", "SRs", "t2"], ["t2"])
                    tt(wi[:], t1[:], t2[:], ALU.subtract, ["t1", "t2", "wi"], ["wi"])
                    fl = lambda a: a[:].rearrange("p g j -> p (g j)")
                    kb.op("dve", lambda h: h.tensor_tensor_scan(out=fl(t1), data0=fl(D0), data1=fl(wr), initial=0.0, op0=ALU.mult, op1=ALU.add),
                          R=["D0", "wr", "t1"], W=["t1"])
                    kb.op("dve", lambda h: h.tensor_tensor_scan(out=fl(t2), data0=fl(D0), data1=fl(wi), initial=0.0, op0=ALU.mult, op1=ALU.add),
                          R=["D0", "wi", "t2"], W=["t2"])
                    tt(SRs[:], Cj[:], t1[:], ALU.mult, ["Cj", "t1", "SRs"], ["SRs"])
                    tt(SIs[:], Sj[:], t2[:], ALU.mult, ["Sj", "t2", "SIs"], ["SIs"])
                    tt(wr[:], SRs[:], SIs[:], ALU.subtract, ["SRs", "SIs", "wr"], ["wr"])
                    tt(SRs[:], Cj[:], t2[:], ALU.mult, ["Cj", "t2", "SRs"], ["SRs"])
                    tt(SIs[:], Sj[:], t1[:], ALU.mult, ["Sj", "t1", "SIs"], ["SIs"])
                    tt(wi[:], SRs[:], SIs[:], ALU.add, ["SRs", "SIs", "wi"], ["wi"])
                    for ge in range(2):
                        ts(HRb[ge][:, :, uo + 1:uo + NB], wr[:, :, 0:NB - 1], pm[:, ge:ge + 1], None, ALU.mult, None, ["wr", "pm"], ["HRb"])
                        ts(HIb[ge][:, :, uo + 1:uo + NB], wi[:, :, 0:NB - 1], pm[:, ge:ge + 1], None, ALU.mult, None, ["wi", "pm"], ["HIb"])
                    stage("a2x", dumps=[("Cj", Cj, ["Cj"]), ("Sj", Sj, ["Sj"]), ("D0", D0, ["D0"]), ("wr", wr, ["wr"]), ("t1", t1, ["t1"]), ("t2", t2, ["t2"]), ("SRs", SRs, ["SRs"])])
                    cp("dve", HRc[:, 1, :], wr[:, :, NB - 1], ["wr"], ["H64"])
                    cp("dve", HIc[:, 1, :], wi[:, :, NB - 1], ["wi"], ["H64"])
                else:
                    a8r = A8r[:].unsqueeze(1).to_broadcast([128, NB, 32]); a8i = A8i[:].unsqueeze(1).to_broadcast([128, NB, 32])
                    for ge in range(2):
                        ts(HRb[ge][:, :, uo:uo + NB], HRs[:].rearrange("p j g -> p g j"), pm[:, ge:ge + 1], None, ALU.mult, None, ["HRs", "pm"], ["HRb"])
                        ts(HIb[ge][:, :, uo:uo + NB], HIs[:].rearrange("p j g -> p g j"), pm[:, ge:ge + 1], None, ALU.mult, None, ["HIs", "pm"], ["HIb"])
                    tt(rt[0][:], a8r, HRs[:], ALU.mult, ["HRs", "A8", "rt0"], ["rt0"])
                    tt(rt[1][:], a8i, HIs[:], ALU.mult, ["HIs", "A8", "rt1"], ["rt1"])
                    tt(rt[2][:], a8r, HIs[:], ALU.mult, ["HIs", "A8", "rt2", "HIb"], ["rt2"])
                    tt(rt[3][:], a8i, HRs[:], ALU.mult, ["HRs", "A8", "rt3", "HRb"], ["rt3"])
                    tt(rt[0][:], rt[0][:], rt[1][:], ALU.subtract, ["rt0", "rt1"], ["rt0"])
                    tt(rt[2][:], rt[2][:], rt[3][:], ALU.add, ["rt2", "rt3"], ["rt2"])
                    tt(rt[0][:], rt[0][:], SRs[:, :, 0:NB].rearrange("p g j -> p j g"), ALU.add, ["rt0", "SRs"], ["rt0"])
                    tt(rt[2][:], rt[2][:], SIs[:, :, 0:NB].rearrange("p g j -> p j g"), ALU.add, ["rt2", "SIs"], ["rt2"])
                    for (src_t, sk, dst_o, ok) in ((rt[0], "rt0", o_rs, "ors"), (rt[2], "rt2", o_is, "ois")):
                        xo = xt[xcount["n"] % 2]; xok = f"xt{xcount['n'] % 2}"; xcount["n"] += 1
                        for i in range(4):
                            b = kb.ps()
                            tr(psf(b)[:, 0:128], src_t[:, 4 * i:4 * i + 4, :], ident_f[:], [sk, "ident_f"], [PK[b]])
                            cp("act" if i % 2 == 0 else "dve", xo[:, i * 128:(i + 1) * 128], psf(b)[:, 0:128], [PK[b]], [xok])
                        for i in range(4):
                            kb.dma("sp", dst_o.rearrange("b (gp ge) p -> (b gp) (ge p)", ge=2)[128 * i:128 * (i + 1), :],
                                   xo[:, i * 128:(i + 1) * 128], R=[xok], W=[ok], key=ok)

            passes = []
            if grp == 0:
                passes += [("pre", 0), ("pre", 1)]
            passes += [("main", grp)]
            if has_s:
                passes += [("samp", 0)]
            if grp == 0:
                kb.op("pool", lambda h: h.memset(HRc[:, 0, :], 0.0), W=["H0c"])
                kb.op("pool", lambda h: h.memset(HIc[:, 0, :], 0.0), W=["H0c"])
                for (src_d, dst_t, nm) in ((sre, HRs, "HRs"), (sim, HIs, "HIs")):
                    for i in range(4):
                        xtile, xk = load_x(src_d.rearrange("b (gp ge) p -> (b gp) (ge p)", ge=2)[128 * i:128 * (i + 1), :], 128, 128)
                        b = kb.ps()
                        tr(psf(b)[:, 0:128], xtile[:, 0:128], ident_f[:], [xk, "ident_f"], [PK[b]])
                        cp("act", dst_t[:, 4 * i:4 * i + 4, :], psf(b)[:, 0:128].rearrange("p (b g) -> p b g", b=4), [PK[b]], [nm])
            pcs = ExitStack()
            MS2r = sbL(pcs, "MS2r_", (128, 4096), BF16); MS2i = sbL(pcs, "MS2i_", (128, 4096), BF16)
            Cj = sbL(pcs, "Cj", (128, 32, 64)); Sj = sbL(pcs, "Sj", (128, 32, 64)); D0 = sbL(pcs, "D0_", (128, 32, 64))
            ld(MS2r[:], d_MS2r[:, :], "cstE", ["MS2r"]); ld(MS2i[:], d_MS2i[:, :], "cstE", ["MS2i"])
            ld(Cj[:].rearrange("p g j -> p (g j)"), d_Cj[:, :], "cstE", ["Cj"]); ld(Sj[:].rearrange("p g j -> p (g j)"), d_Sj[:, :], "cstE", ["Sj"])
            ld(D0[:].rearrange("p g j -> p (g j)"), d_D0[:, :], "cstE", ["D0"])
            for (kind, idx) in passes:
                NB = 16 if kind == "samp" else 64
                uo = 64 if kind == "samp" else 0
                with ExitStack() as a1:
                    xnT = sbL(a1, "xnT2", (128, 16, 672), BF16)
                    ZT = sbL(a1, "ZT", (128, 64, 2, 64), BF16)
                    if kind == "pre":
                        tl = [(xp[idx * 512 + 128 * t: idx * 512 + 128 * (t + 1), :], 32 + 128 * t) for t in range(4)]
                        cbase = 32
                    elif kind == "main":
                        tl = [(xm[idx * 512 + 128 * t: idx * 512 + 128 * (t + 1), :], 32 + 128 * t) for t in range(4)]
                        cbase = 32
                    else:
                        tl = [(xs[:, :], 32 + 512)]
                        cbase = 32 + 512
                    P0_tiles(a1, tl, xnT)
                    run_A1(xnT, ZT, a1, NB, cbase, uo)
                    if grp == 0 and kind == "pre" and idx == 0:
                        stage("a1", dumps=[("concat", concat, ["concat"]), ("U", U, ["U"]), ("ZT", ZT, ["ZT"]), ("xnT", xnT, ["xnT"])])
                    if grp == 0 and kind == "main":
                        stage("a1m", dumps=[("U", U, ["U"]), ("ZT", ZT, ["ZT"]), ("xnT", xnT, ["xnT"])])
                barrier()
                with ExitStack() as a2:
                    SRs = sbL(a2, "SRs", (128, 32, 64)); SIs = sbL(a2, "SIs", (128, 32, 64))
                    run_A2(a2, SRs, SIs, NB, uo, chain=(kind != "samp"))
                    if grp == 0 and kind == "pre" and idx == 0:
                        stage("a2", dumps=[("concat", concat, ["concat"]), ("HRc", HRc, ["H64"]), ("HIc", HIc, ["H64"]), ("SRs", SRs, ["SRs"])])
                    if grp == 0 and kind == "main":
                        stage("a2m", dumps=[("concat", concat, ["concat"]), ("HRc", HRc, ["H64"]), ("HIc", HIc, ["H64"]), ("SRs", SRs, ["SRs"]), ("HRb0", HRb[0], ["HRb"])])
                    if kind != "samp":
                        cp("dve", HRc[:, 0, :], HRc[:, 1, :], ["H64", "HRb", "HIb"], ["H0c"])
                        cp("dve", HIc[:, 0, :], HIc[:, 1, :], ["H64", "HRb", "HIb"], ["H0c"])
                    if kind == "main" and grp == 1:
                        for (src_t, dst_o, ok) in ((HRc, o_rp, "orp"), (HIc, o_ip, "oip")):
                            xo = xt[xcount["n"] % 2]; xok = f"xt{xcount['n'] % 2}"; xcount["n"] += 1
                            b = kb.ps()
                            cp("dve", rt[0][:, 0, :], src_t[:, 1, :], ["H64", "rt0"], ["rt0"])
                            tr(psf(b)[:, 0:128], rt[0][:, 0:4, :], ident_f[:], ["rt0", "ident_f"], [PK[b]])
                            cp("act", xo[0:32, 0:128], psf(b)[0:32, 0:128], [PK[b]], [xok])
                            kb.dma("sp", dst_o.rearrange("(gp ge) p -> gp (ge p)", ge=2), xo[0:32, 0:128], R=[xok], W=[ok], key=ok)
                barrier()
                if kind == "pre":
                    continue
            pcs.close()
            a3 = ExitStack()
            if True:
                ygT = sbL(a3, "ygT", (128, 8, 640), BF16)
                MY1rb = sbL(a3, "MY1rb_", (128, 4096), BF16); MY1ib = sbL(a3, "MY1ib_", (128, 4096), BF16)
                MY2 = sbL(a3, "MY2_", (128, 8192), BF16)
                YT = sbL(a3, "YT", (64, 8, 1024), BF16)
                gq_ = sbL(a3, "gq_", (128, 512)); gw_ = sbL(a3, "gw_", (128, 512))
                ld(MY1rb[:], d_MY1r[:, :], "cst", ["MY1rb"]); ld(MY1ib[:], d_MY1i[:, :], "cst", ["MY1ib"]); ld(MY2[:], d_MY2[:, :], "cst", ["MY2"])
                ypasses = [(64, 0, 0)] + ([(16, 64, 512)] if has_s else [])
                for (NB, uo, tbase) in ypasses:
                    for g4 in range(16):
                        b = kb.ps()
                        for gl in range(4):
                            g = g4 * 4 + gl
                            gp, ge = g // 2, g % 2
                            o = gl * 128
                            P0_, P1_ = 64 * ge, 64 * ge + 64
                            mm(psf(b)[0:NB, o:o + 128], U[:, g, uo:uo + NB], MY2[:, g * 128:(g + 1) * 128], True, False, ["U", "MY2"], [PK[b]])
                            mm(psf(b)[0:NB, o:o + 128], HRb[ge][:, gp, uo:uo + NB], MY1rb[:, gp * 128:(gp + 1) * 128], False, False, ["HRb", "MY1rb"], [PK[b]])
                            mm(psf(b)[0:NB, o:o + 128], HIb[ge][:, gp, uo:uo + NB], MY1ib[:, gp * 128:(gp + 1) * 128], False, True, ["HIb", "MY1ib"], [PK[b]])
                        src = psf(b)[0:NB, :].rearrange("j (g r c) -> j r g c", g=4, r=8)
                        dst = YT[0:NB, :, g4 * 64:(g4 + 1) * 64].rearrange("j r (g c) -> j r g c", g=4)
                        cp("act" if g4 % 2 == 0 else "dve", dst, src, [PK[b]], ["YT"])
                    for r in range(8):
                        b = kb.ps()
                        for q in range(8):
                            tr(psb(b)[:, q * NB:(q + 1) * NB], YT[0:NB, r, q * 128:(q + 1) * 128], ident_b[0:NB, 0:NB], ["YT", "ident_b"], [PK[b]])
                        yv = psb(b)[:, 0:8 * NB]
                        n8 = 8 * NB
                        actf(gq_[:, 0:n8], yv, AF.Square, [PK[b], "gq"], ["gq"])
                        ts(gq_[:, 0:n8], gq_[:, 0:n8], GC2, GC1, ALU.mult, ALU.add, ["gq"], ["gq"])
                        tt(gq_[:, 0:n8], gq_[:, 0:n8], yv, ALU.mult, ["gq", PK[b]], ["gq"])
                        actf(gw_[:, 0:n8], gq_[:, 0:n8], AF.Sigmoid, ["gq", "gw"], ["gw"])
                        dstv = ygT[:, :, tbase:tbase + 8 * NB].rearrange("p q (j e) -> p q e j", e=8)[:, :, r, :]
                        tt(dstv, gw_[:, 0:n8].rearrange("p (q j) -> p q j", q=8), yv.rearrange("p (q j) -> p q j", q=8), ALU.mult, ["gw", PK[b]], ["ygT"])
                if grp == 0:
                    stage("a3", dumps=[("concat", concat, ["concat"]), ("ygT", ygT, ["ygT"]), ("YT", YT, ["YT"])])
            with ExitStack() as g_:
                ys32 = sbL(g_, "ys32", (128, 8, 640))
                sg = sbL(g_, "sg2", (128, 512))
                ysq = sbL(g_, "ysq2", (128, 640), BF16)
                vt = sbL(g_, "vt2", (128, 640)); rs = sbL(g_, "rs2", (128, 640))
                loads = [[(lambda s: s[:, :, :], w_glu_v[:, :, q * 128:(q + 1) * 128])] for q in range(8)]
                wgl = Stream(g_, "wgl", 2, (128, 8, 128), loads)
                sblocks = [(0, 512, False)] + ([(512, 128, True)] if has_s else [])
                for q in range(8):
                    wslot, wk = wgl.get(q)
                    for (t0, n, is_s) in sblocks:
                        b = kb.ps()
                        for kc in range(8):
                            mm(psf(b)[:, 0:n], wslot[:, kc, :], ygT[:, kc, t0:t0 + n], kc == 0, kc == 7, [wk, "ygT"], [PK[b]])
                        actf(sg[:, 0:n], psf(b)[:, 0:n], AF.Sigmoid, [PK[b], "sg"], ["sg"])
                        tt(ys32[:, q, t0:t0 + n], ygT[:, q, t0:t0 + n], sg[:, 0:n], ALU.mult, ["ygT", "sg"], ["ys32"])
                fb = {}
                for (t0, n, is_s) in sblocks:
                    fb[t0] = kb.ps()
                for q in range(8):
                    for (t0, n, is_s) in sblocks:
                        actf(ysq[:, t0:t0 + n], ys32[:, q, t0:t0 + n], AF.Square, ["ys32", "ysq"], ["ysq"])
                        mm(psf(fb[t0])[:, 0:n], ones_b[:], ysq[:, t0:t0 + n], q == 0, q == 7, ["ones_b", "ysq"], [PK[fb[t0]]])
                for (t0, n, is_s) in sblocks:
                    ts(vt[:, t0:t0 + n], psf(fb[t0])[:, 0:n], 1.0 / DC, EPS, ALU.mult, ALU.add, [PK[fb[t0]], "vt"], ["vt"])
                    rsqrt(rs[:, t0:t0 + n], vt[:, t0:t0 + n], n, ["vt"], ["rs"])
                for q in range(8):
                    stt(concat[:, 8 + q, 0:NTOK], ys32[:, q, 0:NTOK], gns_t[:, q:q + 1], rs[:, 0:NTOK], ALU.mult, ALU.mult,
                        ["ys32", "gns", "rs"], ["concat"])
                if grp == 0:
                    stage("g", dumps=[("concat", concat, ["concat"]), ("ys32", ys32, ["ys32"])])
            barrier()
            a3.close()
        with ExitStack() as f_:
            hm = [sbL(f_, f"hm{t}", (128, D)) for t in range(NT)]
            hnT = sbL(f_, "hnT", (128, 16, 640), BF16)
            act = sbL(f_, "act", (128, 11, 640), BF16)
            sg = sbL(f_, "sg3", (128, 512)); tg = sbL(f_, "tg3", (128, 512))
            rows = []
            for t in range(NT):
                if t < 4:
                    rows.append((xm[grp * 512 + 128 * t: grp * 512 + 128 * (t + 1), :], o_ym[grp * 512 + 128 * t: grp * 512 + 128 * (t + 1), :]))
                else:
                    rows.append((xs[:, :], o_ys[:, :]))
            for t in range(NT):
                ld(hm[t][:], rows[t][0], f"hm{t}", [f"hm{t}"])
            with ExitStack() as wo_:
                loads = [[(lambda s: s[:, :, :], w_out_v[:, :, nb * 256:(nb + 1) * 256])] for nb in range(8)]
                wo = Stream(wo_, "wo", 2, (128, 16, 256), loads)
                for nb in range(8):
                    wslot, wk = wo.get(nb)
                    for t in range(NT):
                        b = kb.ps()
                        for kc in range(16):
                            mm(psf(b)[:, 0:256], concat[:, kc, t * 128:(t + 1) * 128], wslot[:, kc, :], kc == 0, kc == 15, ["concat", wk], [PK[b]])
                        tt(hm[t][:, nb * 256:(nb + 1) * 256], psf(b)[:, 0:256], hm[t][:, nb * 256:(nb + 1) * 256], ALU.add, [PK[b], f"hm{t}"], [f"hm{t}"])
            if grp == 0:
                stage("f1", dumps=[("concat", concat, ["concat"]), ("hm0", hm[0], ["hm0"]), ("hm4", hm[4], ["hm4"])])
            load_g(norm_ffn, "ffn")
            for t in range(NT):
                norm_T(hm[t][:], [f"hm{t}"], lambda hf, t=t: hnT[:, hf * 8:(hf + 1) * 8, t * 128:(t + 1) * 128], ["hnT"])
            if NTOK == 640:
                mblocks = [(0, 320), (320, 320)]
            else:
                mblocks = [(0, 512)]
            with ExitStack() as wf_:
                gl_loads = []
                for c2 in range(NCH // 2):
                    gl_loads.append([(lambda s: s[:, 0, :, :], w_g_v[:, :, c2 * 256:(c2 + 1) * 256]),
                                     (lambda s: s[:, 1, :, :], w_u_v[:, :, c2 * 256:(c2 + 1) * 256])])
                wgu = Stream(wf_, "wgu", 2, (128, 2, 16, 256), gl_loads)
                d_loads = []
                for qd in range(4):
                    for nb in range(4):
                        for (k0, kn) in ((0, 6), (6, 5)):
                            kc = qd * 11 + k0
                            d_loads.append([(lambda s, kn=kn: s[:, 0:kn, :], w_d_v[:, kc:kc + kn, nb * 512:(nb + 1) * 512])])
                wdn = Stream(wf_, "wdn", 2, (128, 6, 512), d_loads)
                di = 0
                for qd in range(4):
                    wdn.ensure(di + 1)
                    for c11 in range(11):
                        c = qd * 11 + c11
                        wslot, wk = wgu.get(c // 2)
                        co = (c % 2) * 128
                        for (c0, n) in mblocks:
                            bg = kb.ps(); bu = kb.ps()
                            for vg, bb in ((0, bg), (1, bu)):
                                for kc in range(16):
                                    mm(psf(bb)[:, 0:n], wslot[:, vg, kc, co:co + 128], hnT[:, kc, c0:c0 + n], kc == 0, kc == 15, [wk, "hnT"], [PK[bb]])
                            actf(sg[:, 0:n], psf(bg)[:, 0:n], AF.Sigmoid, [PK[bg], "sg"], ["sg"])
                            tt(tg[:, 0:n], psf(bg)[:, 0:n], sg[:, 0:n], ALU.mult, [PK[bg], "sg", "tg"], ["tg"])
                            tt(act[:, c11, c0:c0 + n], tg[:, 0:n], psf(bu)[:, 0:n], ALU.mult, ["tg", PK[bu]], ["act"])
                    for nb in range(4):
                        bs_ = [kb.ps() for _ in range(NT)]
                        for (k0, kn) in ((0, 6), (6, 5)):
                            wslot, wk = wdn.get(di); di += 1
                            for kk in range(kn):
                                k2 = k0 + kk
                                for t in range(NT):
                                    mm(psf(bs_[t])[:, :], act[:, k2, t * 128:(t + 1) * 128], wslot[:, kk, :], k2 == 0, k2 == 10, ["act", wk], [PK[bs_[t]]])
                        for t in range(NT):
                            tt(hm[t][:, nb * 512:(nb + 1) * 512], psf(bs_[t])[:, :], hm[t][:, nb * 512:(nb + 1) * 512], ALU.add,
                               [PK[bs_[t]], f"hm{t}"], [f"hm{t}"])
            load_g(norm_final, "final")
            for t in range(NT):
                actf(junk[:], hm[t][:], AF.Square, [f"hm{t}", "junk"], ["junk", "col0"], accum=col[:, 0:1])
                ts(col[:, 1:2], col[:, 0:1], 1.0 / D, EPS, ALU.mult, ALU.add, ["col0"], ["col1"])
                rsqrt(col[:, 2:3], col[:, 1:2], 1, ["col1"], ["col2"])
                stt(hm[t][:], hm[t][:], col[:, 2:3], gbc[:], ALU.mult, ALU.mult, [f"hm{t}", "col2", "gbc"], [f"hm{t}"])
                kb.dma("sp", rows[t][1], hm[t][:], R=[f"hm{t}"], W=[f"oy{grp}{t}"], key="oy")
            if not kb.dead:
                kb.eng["sp"]["h"].wait_ge(kb.sems["d_oy"], kb.dmas["d_oy"])
            if grp == 0:
                stage("f", dumps=[("hnT", hnT, ["hnT"])])
        barrier()
    for sname, val in kb.dmas.items():
        if sname.startswith("d_o"):
            kb.eng["sp"]["h"].wait_ge(kb.sems[sname], val)
    kb.barrier()
    nc._dbg_names = dbg_names
    return nc, es


_CACHE = {}


def kernel(**inputs):
    inp = {k: np.ascontiguousarray(np.asarray(v), dtype=np.float32) for k, v in inputs.items()}
    if "nc" not in _CACHE:
        _CACHE["nc"] = build_program()
    nc, _es = _CACHE["nc"]
    ident = np.eye(128, dtype=np.float32)
    rho = np.arange(128); s_of = rho // 16
    colr = np.arange(128) // 16
    mask2 = (colr[None, :] >= s_of[:, None]).astype(np.float32)
    ident2 = np.eye(128, dtype=np.float32)
    shared = dict(
        norm_mix=inp["norm_mix"][0], norm_ffn=inp["norm_ffn"][0], norm_final=inp["norm_final"],
        w_in=inp["w_in"][0], conv_w=inp["conv_w"][0], conv_b=inp["conv_b"][0], ln_g=inp["conv_ln_g"][0], ln_b=inp["conv_ln_b"][0],
        A_re=inp["ssm_A_re"][0], A_im=inp["ssm_A_im"][0], log_dt=inp["ssm_log_dt"][0],
        B_re=inp["ssm_B_re"][0], B_im=inp["ssm_B_im"][0], C_re=inp["ssm_C_re"][0], C_im=inp["ssm_C_im"][0], ssm_D=inp["ssm_D"][0],
        w_glu=inp["w_glu"][0], gn_c=inp["gnorm_conv"][0], gn_s=inp["gnorm_ssm"][0], w_out=inp["w_out"][0],
        w_g=inp["w_ffn_gate"][0], w_u=inp["w_ffn_up"][0], w_d=inp["w_ffn_down"][0],
        ident=ident, mask2=mask2, ident2=ident2, pmask=np.stack([(np.arange(128) < 64), (np.arange(128) >= 64)], 1).astype(np.float32))
    shared = {k: np.ascontiguousarray(v) for k, v in shared.items()}
    in_maps = []
    for c in range(8):
        b, half = c // 2, c % 2
        m = dict(shared)
        m["xm"] = np.ascontiguousarray(inp["x_prompt"][b, half * 1024:(half + 1) * 1024])
        m["xp"] = np.ascontiguousarray(inp["x_prompt"][b, 0:1024]) if half == 1 else np.zeros((1024, D), np.float32)
        m["xs"] = np.ascontiguousarray(inp["x_sample"][16 * c:16 * c + 16].reshape(128, D))
        m["stc"] = np.ascontiguousarray(inp["state_conv"][0, 16 * c:16 * c + 16])
        m["sre"] = np.ascontiguousarray(inp["state_ssm_re"][0, 16 * c:16 * c + 16])
        m["sim"] = np.ascontiguousarray(inp["state_ssm_im"][0, 16 * c:16 * c + 16])
        in_maps.append(m)
    res = run_bass_kernel_spmd(nc, in_maps, core_ids=list(range(8)))
    R = res.results
    y_prompt = np.zeros((4, 2048, D), np.float32); y_sample = np.zeros((128, 8, D), np.float32)
    ncp = np.zeros((1, 4, 30, DC), np.float32); nrp = np.zeros((1, 4, 64, 64), np.float32); nip = np.zeros((1, 4, 64, 64), np.float32)
    ncs = np.zeros((1, 128, 30, DC), np.float32); nrs = np.zeros((1, 128, 64, 64), np.float32); nis = np.zeros((1, 128, 64, 64), np.float32)
    for c in range(8):
        b, half = c // 2, c % 2
        y_prompt[b, half * 1024:(half + 1) * 1024] = R[c]["o_ym"]
        y_sample[16 * c:16 * c + 16] = R[c]["o_ys"].reshape(16, 8, D)
        ncs[0, 16 * c:16 * c + 16] = R[c]["o_cs"]; nrs[0, 16 * c:16 * c + 16] = R[c]["o_rs"]; nis[0, 16 * c:16 * c + 16] = R[c]["o_is"]
        if half == 1:
            ncp[0, b] = R[c]["o_cp"]; nrp[0, b] = R[c]["o_rp"]; nip[0, b] = R[c]["o_ip"]
    return (y_prompt, y_sample, ncp, nrp, nip, ncs, nrs, nis)
```
